# Optimizing a Trainium2 kernel written in Bass

```python
import math
import jax, jax.numpy as jnp
from jax import lax
import numpy as np

D_MODEL = 4096
BATCH = 2
SEQ = 4096
DEPTH = 2

CHUNK = 64
N_META = 16
Q_BLOCK = 128
EPS = 1e-6

MLA_HEADS = 16
MLA_Q_LORA = 1024
MLA_KV_LORA = 512
MLA_NOPE = 128
MLA_ROPE = 64
MLA_V = 128
MLA_QK = MLA_NOPE + MLA_ROPE
MLA_WIDTH = MLA_HEADS * MLA_V
ROPE_THETA = 10000.0

S5_WIDTH = 1024
S5_GROUP = 16
S5_GROUPS = S5_WIDTH // S5_GROUP
S5_STATE = 64
S5_DT_MIN = 1e-3
S5_DT_MAX = 1e-1

DN_HEADS = 8
DN_DK = 128
DN_DV = 128
DN_QK = DN_HEADS * DN_DK
DN_WIDTH = DN_HEADS * DN_DV
DN_CONV = 4
DN_CHUNK = CHUNK

FFN_HIDDEN = 11008

N_BRANCH = 3

OFF_Q = 0
OFF_KV = OFF_Q + MLA_Q_LORA
OFF_KR = OFF_KV + MLA_KV_LORA
OFF_S5 = OFF_KR + MLA_ROPE
OFF_DN_QKV = OFF_S5 + S5_WIDTH
OFF_DN_Z = OFF_DN_QKV + 2 * DN_QK + DN_WIDTH
OFF_DN_A = OFF_DN_Z + DN_WIDTH
OFF_DN_B = OFF_DN_A + DN_HEADS
OFF_GATE = OFF_DN_B + DN_HEADS
N_IN = OFF_GATE + N_BRANCH * D_MODEL

kernel_name = 'hybrid_mla_s5_gdn_macaron_encoder'


def rms_norm(x, g):
    xf = x.astype(jnp.float32)
    y = xf * lax.rsqrt(jnp.mean(xf * xf, axis=-1, keepdims=True) + EPS)
    return (y * g.astype(jnp.float32)).astype(x.dtype)


def l2_norm(x):
    return x * lax.rsqrt(jnp.sum(x * x, axis=-1, keepdims=True) + EPS)


def swiglu(x, w13, w2):
    a, b = jnp.split(x @ w13, 2, axis=-1)
    return (jax.nn.silu(a) * b) @ w2


def chunk_ids(n):
    p = jnp.arange(n)
    return jnp.where(p < N_META, 0, 1 + (p - N_META) // CHUNK)


def rope_tables(n):
    inv = ROPE_THETA ** (-jnp.arange(0, MLA_ROPE, 2, dtype=jnp.float32) / MLA_ROPE)
    ang = jnp.arange(n, dtype=jnp.float32)[:, None] * inv[None, :]
    return jnp.cos(ang), jnp.sin(ang)


def apply_rope(x, cos, sin):
    x1, x2 = jnp.split(x.astype(jnp.float32), 2, axis=-1)
    return jnp.concatenate([x1 * cos - x2 * sin, x2 * cos + x1 * sin], axis=-1).astype(x.dtype)


def mla_branch(c_q, c_kv, k_rope_raw, q_norm_g, kv_norm_g, w_uq, w_ukv, w_o, cos, sin, cid):
    B, L, _ = c_q.shape
    dt = c_q.dtype
    q = (rms_norm(c_q, q_norm_g) @ w_uq).reshape(B, L, MLA_HEADS, MLA_QK)
    q_nope = q[..., :MLA_NOPE]
    q_rope = apply_rope(q[..., MLA_NOPE:], cos[:, None, :], sin[:, None, :])
    kv = (rms_norm(c_kv, kv_norm_g) @ w_ukv).reshape(B, L, MLA_HEADS, MLA_NOPE + MLA_V)
    k_nope, v = kv[..., :MLA_NOPE], kv[..., MLA_NOPE:]
    k_rope = apply_rope(k_rope_raw, cos, sin)
    scale = MLA_QK ** -0.5
    n_blk = -(-L // Q_BLOCK)
    pad = n_blk * Q_BLOCK - L

    def blocks(t):
        t = jnp.pad(t, [(0, 0), (0, pad)] + [(0, 0)] * (t.ndim - 2))
        return jnp.moveaxis(t.reshape((B, n_blk, Q_BLOCK) + t.shape[2:]), 1, 0)

    q_cid = jnp.pad(cid, (0, pad), mode='edge').reshape(n_blk, Q_BLOCK)

    def attend(args):
        qn, qr, qc = args
        s = (jnp.einsum('bqhd,bkhd->bhqk', qn, k_nope, preferred_element_type=jnp.float32)
             + jnp.einsum('bqhr,bkr->bhqk', qr, k_rope, preferred_element_type=jnp.float32)) * scale
        visible = qc[:, None] >= cid[None, :]
        p = jax.nn.softmax(jnp.where(visible, s, -jnp.inf), axis=-1).astype(dt)
        return jnp.einsum('bhqk,bkhd->bqhd', p, v)

    o = lax.map(attend, (blocks(q_nope), blocks(q_rope), q_cid))
    o = jnp.moveaxis(o, 0, 1).reshape(B, n_blk * Q_BLOCK, MLA_WIDTH)[:, :L]
    return o @ w_o


def s5_branch(u, a_re, a_im, log_dt, b_re, b_im, c_re, c_im, d, w_glu):
    B, L, _ = u.shape
    dt = u.dtype
    f32 = jnp.float32
    uf = u.astype(f32).reshape(B, L, S5_GROUPS, S5_GROUP)
    ar, ai = a_re.astype(f32), a_im.astype(f32)
    delta = jnp.exp(log_dt.astype(f32))[:, None]
    mag = jnp.exp(ar * delta)
    abar_r, abar_i = mag * jnp.cos(ai * delta), mag * jnp.sin(ai * delta)
    den = ar * ar + ai * ai
    zr = ((abar_r - 1.0) * ar + abar_i * ai) / den
    zi = (abar_i * ar - (abar_r - 1.0) * ai) / den
    br, bi = b_re.astype(f32), b_im.astype(f32)
    bbar_r = zr[..., None] * br - zi[..., None] * bi
    bbar_i = zr[..., None] * bi + zi[..., None] * br
    bu_r = jnp.einsum('blgc,gpc->lbgp', uf, bbar_r)
    bu_i = jnp.einsum('blgc,gpc->lbgp', uf, bbar_i)
    a_r = jnp.broadcast_to(abar_r, bu_r.shape)
    a_i = jnp.broadcast_to(abar_i, bu_i.shape)

    def combine(e1, e2):
        a1r, a1i, b1r, b1i = e1
        a2r, a2i, b2r, b2i = e2
        return (a2r * a1r - a2i * a1i, a2r * a1i + a2i * a1r,
                a2r * b1r - a2i * b1i + b2r, a2r * b1i + a2i * b1r + b2i)

    _, _, xr, xi = lax.associative_scan(combine, (a_r, a_i, bu_r, bu_i), axis=0)
    y = (jnp.einsum('lbgp,gcp->blgc', xr, c_re.astype(f32))
         - jnp.einsum('lbgp,gcp->blgc', xi, c_im.astype(f32))
         + d.astype(f32) * uf)
    h = jax.nn.gelu(y.reshape(B, L, S5_WIDTH)).astype(dt)
    val, gate = jnp.split(h @ w_glu, 2, axis=-1)
    return val * jax.nn.sigmoid(gate)


def causal_dwconv(x, w):
    k = w.shape[0]
    return lax.conv_general_dilated(x, w[:, None, :], window_strides=(1,), padding=[(k - 1, 0)],
                                    dimension_numbers=('NWC', 'WIO', 'NWC'),
                                    feature_group_count=x.shape[-1])


def gated_deltanet_branch(qkv, z, a, b, conv_w, a_log, dt_bias, out_norm_g, w_o):
    B, L, _ = qkv.shape
    dt = qkv.dtype
    f32 = jnp.float32
    C = DN_CHUNK
    qkv = jax.nn.silu(causal_dwconv(qkv, conv_w))
    q, k, v = jnp.split(qkv.astype(f32), [DN_QK, 2 * DN_QK], axis=-1)
    q = l2_norm(q.reshape(B, L, DN_HEADS, DN_DK)) * (DN_DK ** -0.5)
    k = l2_norm(k.reshape(B, L, DN_HEADS, DN_DK))
    v = v.reshape(B, L, DN_HEADS, DN_DV)
    beta = jax.nn.sigmoid(b.astype(f32))
    g = -jnp.exp(a_log.astype(f32)) * jax.nn.softplus(a.astype(f32) + dt_bias.astype(f32))
    n_c = -(-L // C)
    pad = n_c * C - L

    def chunks(t):
        t = jnp.pad(t, [(0, 0), (0, pad)] + [(0, 0)] * (t.ndim - 2))
        t = t.reshape((B, n_c, C) + t.shape[2:])
        return jnp.moveaxis(t, (1, 3), (0, 2))

    qc, kc, vc, bc, gc = chunks(q), chunks(k), chunks(v), chunks(beta), chunks(g)
    gcum = jnp.cumsum(gc, axis=-1)
    tri = jnp.tril(jnp.ones((C, C), dtype=bool))
    strict = jnp.tril(jnp.ones((C, C), dtype=bool), -1)
    decay = jnp.exp(jnp.where(tri, gcum[..., :, None] - gcum[..., None, :], -jnp.inf))
    k_beta = kc * bc[..., None]
    v_beta = vc * bc[..., None]
    a_mat = jnp.where(strict, jnp.einsum('...id,...jd->...ij', k_beta, kc) * decay, 0.0)

    def solve(rhs):
        return lax.linalg.triangular_solve(a_mat, rhs, left_side=True, lower=True, unit_diagonal=True)

    u_c = solve(v_beta)
    w_c = solve(k_beta * jnp.exp(gcum)[..., None])
    qk = jnp.einsum('...id,...jd->...ij', qc, kc) * decay

    def step(S, inp):
        q_i, k_i, u_i, w_i, qk_i, g_i = inp
        v_new = u_i - jnp.einsum('bhck,bhkv->bhcv', w_i, S)
        o = (jnp.einsum('bhck,bhkv->bhcv', q_i * jnp.exp(g_i)[..., None], S)
             + jnp.einsum('bhij,bhjv->bhiv', qk_i, v_new))
        g_last = g_i[..., -1:]
        S = (S * jnp.exp(g_last)[..., None]
             + jnp.einsum('bhck,bhcv->bhkv', k_i * jnp.exp(g_last - g_i)[..., None], v_new))
        return S, o

    S0 = jnp.zeros((B, DN_HEADS, DN_DK, DN_DV), f32)
    _, o = lax.scan(step, S0, (qc, kc, u_c, w_c, qk, gcum))
    o = jnp.moveaxis(o, (0, 2), (1, 3)).reshape(B, n_c * C, DN_HEADS, DN_DV)[:, :L]
    zf = z.astype(f32).reshape(B, L, DN_HEADS, DN_DV)
    o = rms_norm(o, out_norm_g) * jax.nn.silu(zf)
    return o.reshape(B, L, DN_WIDTH).astype(dt) @ w_o


def hybrid_mixer(h, w_in, mla_q_norm_g, mla_kv_norm_g, mla_w_uq, mla_w_ukv, mla_w_o,
                 s5_a_re, s5_a_im, s5_log_dt, s5_b_re, s5_b_im, s5_c_re, s5_c_im, s5_d, s5_w_glu,
                 dn_conv_w, dn_a_log, dn_dt_bias, dn_out_norm_g, dn_w_o, w_out, cos, sin, cid):
    p = h @ w_in
    y_mla = mla_branch(p[..., OFF_Q:OFF_KV], p[..., OFF_KV:OFF_KR], p[..., OFF_KR:OFF_S5],
                       mla_q_norm_g, mla_kv_norm_g, mla_w_uq, mla_w_ukv, mla_w_o, cos, sin, cid)
    y_s5 = s5_branch(p[..., OFF_S5:OFF_DN_QKV], s5_a_re, s5_a_im, s5_log_dt,
                     s5_b_re, s5_b_im, s5_c_re, s5_c_im, s5_d, s5_w_glu)
    y_dn = gated_deltanet_branch(p[..., OFF_DN_QKV:OFF_DN_Z], p[..., OFF_DN_Z:OFF_DN_A],
                                 p[..., OFF_DN_A:OFF_DN_B], p[..., OFF_DN_B:OFF_GATE],
                                 dn_conv_w, dn_a_log, dn_dt_bias, dn_out_norm_g, dn_w_o)
    g_mla, g_s5, g_dn = jnp.split(jax.nn.sigmoid(p[..., OFF_GATE:]), N_BRANCH, axis=-1)
    merged = g_mla * y_mla + g_s5 * y_s5 + g_dn * y_dn
    return merged @ w_out


def setup_inputs(seed: int = 0) -> dict:
    key = jax.random.key(seed)
    ks = iter(jax.random.split(key, 48))
    f32 = jnp.float32

    def nrm(shape, scale):
        return jax.random.normal(next(ks), shape, f32) * scale

    def gain(shape):
        return 1.0 + 0.02 * jax.random.normal(next(ks), shape, f32)

    def unif(shape, lo, hi):
        return jax.random.uniform(next(ks), shape, f32, lo, hi)

    D, F = D_MODEL, FFN_HIDDEN
    dn_dt = jnp.exp(unif((DEPTH, DN_HEADS), math.log(1e-3), math.log(1e-1)))
    inp = {
        'x': nrm((BATCH, SEQ, D), 1.0),
        'meta_tokens': nrm((N_META, D), 1.0),
        'sandwich_g': gain((DEPTH, 6, D)),
        'ffn1_w13': nrm((DEPTH, D, 2 * F), D ** -0.5),
        'ffn1_w2': nrm((DEPTH, F, D), F ** -0.5),
        'w_in': nrm((DEPTH, D, N_IN), D ** -0.5),
        'mla_q_norm_g': gain((DEPTH, MLA_Q_LORA)),
        'mla_kv_norm_g': gain((DEPTH, MLA_KV_LORA)),
        'mla_w_uq': nrm((DEPTH, MLA_Q_LORA, MLA_HEADS * MLA_QK), MLA_Q_LORA ** -0.5),
        'mla_w_ukv': nrm((DEPTH, MLA_KV_LORA, MLA_HEADS * (MLA_NOPE + MLA_V)), MLA_KV_LORA ** -0.5),
        'mla_w_o': nrm((DEPTH, MLA_WIDTH, D), MLA_WIDTH ** -0.5),
        's5_a_re': -0.5 + nrm((DEPTH, S5_GROUPS, S5_STATE), 0.01),
        's5_a_im': math.pi * jnp.arange(S5_STATE, dtype=f32) + nrm((DEPTH, S5_GROUPS, S5_STATE), 0.01),
        's5_log_dt': unif((DEPTH, S5_GROUPS), math.log(S5_DT_MIN), math.log(S5_DT_MAX)),
        's5_b_re': nrm((DEPTH, S5_GROUPS, S5_STATE, S5_GROUP), (2 * S5_GROUP) ** -0.5),
        's5_b_im': nrm((DEPTH, S5_GROUPS, S5_STATE, S5_GROUP), (2 * S5_GROUP) ** -0.5),
        's5_c_re': nrm((DEPTH, S5_GROUPS, S5_GROUP, S5_STATE), S5_STATE ** -0.5),
        's5_c_im': nrm((DEPTH, S5_GROUPS, S5_GROUP, S5_STATE), S5_STATE ** -0.5),
        's5_d': nrm((DEPTH, S5_GROUPS, S5_GROUP), 0.5),
        's5_w_glu': nrm((DEPTH, S5_WIDTH, 2 * D), S5_WIDTH ** -0.5),
        'dn_conv_w': nrm((DEPTH, DN_CONV, 2 * DN_QK + DN_WIDTH), DN_CONV ** -0.5),
        'dn_a_log': jnp.log(unif((DEPTH, DN_HEADS), 1.0, 16.0)),
        'dn_dt_bias': dn_dt + jnp.log(-jnp.expm1(-dn_dt)),
        'dn_out_norm_g': gain((DEPTH, DN_DV)),
        'dn_w_o': nrm((DEPTH, DN_WIDTH, D), DN_WIDTH ** -0.5),
        'w_out': nrm((DEPTH, D, D), D ** -0.5),
        'ffn2_w13': nrm((DEPTH, D, 2 * F), D ** -0.5),
        'ffn2_w2': nrm((DEPTH, F, D), F ** -0.5),
    }
    return inp


def reference(x, meta_tokens, sandwich_g, ffn1_w13, ffn1_w2, w_in, mla_q_norm_g, mla_kv_norm_g,
              mla_w_uq, mla_w_ukv, mla_w_o, s5_a_re, s5_a_im, s5_log_dt, s5_b_re, s5_b_im,
              s5_c_re, s5_c_im, s5_d, s5_w_glu, dn_conv_w, dn_a_log, dn_dt_bias, dn_out_norm_g,
              dn_w_o, w_out, ffn2_w13, ffn2_w2):
    B = x.shape[0]
    meta = jnp.broadcast_to(meta_tokens[None].astype(x.dtype), (B, N_META, D_MODEL))
    h = jnp.concatenate([meta, x], axis=1)
    L = h.shape[1]
    cos, sin = rope_tables(L)
    cid = chunk_ids(L)
    for l in range(DEPTH):
        g = sandwich_g[l]
        h = h + 0.5 * rms_norm(swiglu(rms_norm(h, g[0]), ffn1_w13[l], ffn1_w2[l]), g[1])
        mix = hybrid_mixer(rms_norm(h, g[2]), w_in[l], mla_q_norm_g[l], mla_kv_norm_g[l],
                           mla_w_uq[l], mla_w_ukv[l], mla_w_o[l], s5_a_re[l], s5_a_im[l],
                           s5_log_dt[l], s5_b_re[l], s5_b_im[l], s5_c_re[l], s5_c_im[l], s5_d[l],
                           s5_w_glu[l], dn_conv_w[l], dn_a_log[l], dn_dt_bias[l],
                           dn_out_norm_g[l], dn_w_o[l], w_out[l], cos, sin, cid)
        h = h + rms_norm(mix, g[3])
        h = h + 0.5 * rms_norm(swiglu(rms_norm(h, g[4]), ffn2_w13[l], ffn2_w2[l]), g[5])
    return h[:, N_META:]
```

```python
import math
from contextlib import ExitStack
import numpy as np
import concourse.bass as bass
import concourse.mybir as mybir
from concourse.bass_utils import run_bass_kernel_spmd

F32 = mybir.dt.float32
BF16 = mybir.dt.bfloat16
AF = mybir.ActivationFunctionType
ALU = mybir.AluOpType
EPS = 1e-6


def full_cfg():
    return dict(D=4096, SEQ=4096, DEPTH=2, NMETA=16, CHUNK=64, F=11008,
                H=16, QL=1024, KVL=512, NOPE=128, ROPE=64, VD=128,
                S5W=1024, S5G=16, S5P=64, DNH=8, DK=128, DV=128, DNCONV=4,
                TG=384, THETA=10000.0)


def derive(c):
    c = dict(c)
    c['L'] = c['SEQ'] + c['NMETA']
    c['LP'] = -(-c['L'] // c['TG']) * c['TG']
    c['NG'] = c['LP'] // c['TG']
    c['NT'] = c['LP'] // 128
    c['QK'] = c['NOPE'] + c['ROPE']
    c['MLAW'] = c['H'] * c['VD']
    c['S5NG'] = c['S5W'] // c['S5G']
    c['DNQK'] = c['DNH'] * c['DK']
    c['DNW'] = c['DNH'] * c['DV']
    o = 0
    c['OFF_Q'] = o; o += c['QL']
    c['OFF_KV'] = o; o += c['KVL']
    c['OFF_KR'] = o; o += c['ROPE']
    c['OFF_S5'] = o; o += c['S5W']
    c['OFF_DN_QKV'] = o; o += 2 * c['DNQK'] + c['DNW']
    c['OFF_DN_Z'] = o; o += c['DNW']
    c['OFF_DN_A'] = o; o += c['DNH']
    c['OFF_DN_B'] = o; o += c['DNH']
    c['OFF_GATE'] = o; o += 3 * c['D']
    c['NIN'] = o
    return c


class Res:
    __slots__ = ('w', 'r', 'name')

    def __init__(self, name=''):
        self.w = {}
        self.r = {}
        self.name = name


class Sched:
    NDS = 24

    def __init__(self, nc, stack):
        self.nc = nc
        self.eng = {'pe': nc.tensor, 'act': nc.scalar, 'dve': nc.vector,
                    'pool': nc.gpsimd, 'sp': nc.sync}
        self.sem = {}
        for k in ('pe', 'act', 'dve', 'pool'):
            self.sem[k] = stack.enter_context(nc.semaphore('s_' + k))
        for i in range(self.NDS):
            self.sem[('d', i)] = stack.enter_context(nc.semaphore('s_d%d' % i))
        self.cnt = {k: 0 for k in self.sem}
        self.waited = {e: {} for e in self.eng}
        self.nd = 0
        self.nins = 0

    def _hazards(self, eng, reads, writes, waits, is_dma=False):
        def need(tok):
            if tok is None:
                return
            k, v = tok
            if eng == 'pe' and k == 'pe':
                return
            if self.waited[eng].get(k, 0) >= v:
                return
            if waits.get(k, 0) < v:
                waits[k] = v
        for r in reads:
            for k, v in r.w.items():
                need((k, v))
        for w in writes:
            if w.r:
                for k, v in w.r.items():
                    need((k, v))
                for k, v in w.w.items():
                    need((k, v))
            else:
                for k, v in w.w.items():
                    if k == eng or (is_dma and isinstance(k, tuple)):
                        continue
                    need((k, v))

    def _emit_waits(self, eng, waits):
        E = self.eng[eng]
        for k, v in waits.items():
            E.wait_ge(self.sem[k], v)
            self.waited[eng][k] = v
            self.nins += 1

    def _commit(self, tok, reads, writes):
        k, v = tok
        for r in reads:
            if r.r.get(k, 0) < v:
                r.r[k] = v
        for w in writes:
            if w.r:
                w.w = {}
                w.r = {}
            if w.w.get(k, 0) < v:
                w.w[k] = v

    def op(self, eng, fn, reads=(), writes=()):
        waits = {}
        self._hazards(eng, reads, writes, waits)
        self._emit_waits(eng, waits)
        ins = fn(self.eng[eng])
        self.cnt[eng] += 1
        ins.then_inc(self.sem[eng], 1)
        self.nins += 1
        self._commit((eng, self.cnt[eng]), reads, writes)

    def dma(self, q, out, in_, reads=(), writes=()):
        i = self.nd % self.NDS
        self.nd += 1
        key = ('d', i)
        waits = {}
        self._hazards(q, reads, writes, waits, is_dma=True)
        prev = self.cnt[key]
        if prev > 0 and self.waited[q].get(key, 0) < prev and waits.get(key, 0) < prev:
            waits[key] = prev
        self._emit_waits(q, waits)
        ins = self.eng[q].dma_start(out=out, in_=in_)
        self.cnt[key] += 16
        ins.then_inc(self.sem[key], 16)
        self.nins += 1
        self._commit((key, self.cnt[key]), reads, writes)

    def barrier(self):
        for e in self.eng:
            for k, v in self.cnt.items():
                if v > 0 and self.waited[e].get(k, 0) < v and not (e == k):
                    self.eng[e].wait_ge(self.sem[k], v)
                    self.waited[e][k] = v
                    self.nins += 1

    def finish(self):
        E = self.eng['sp']
        for i in range(self.NDS):
            key = ('d', i)
            if self.cnt[key] > 0:
                E.wait_ge(self.sem[key], self.cnt[key])
        for k in ('pe', 'act', 'dve', 'pool'):
            if self.cnt[k] > 0:
                E.wait_ge(self.sem[k], self.cnt[k])


class T:
    def __init__(self, ap_handle, name=''):
        self.t = ap_handle
        self.res = Res(name)

    def __getitem__(self, k):
        return self.t[k]


class K:
    def __init__(self, cfg):
        self.c = cfg
        self.nc = bass.Bass("TRN2", target_bir_lowering=False)
        self.root = ExitStack()
        self.S = Sched(self.nc, self.root)
        self.dres = {}
        self.relaid = set()

    def dram(self, name, shape, dt, kind="Internal"):
        t = self.nc.dram_tensor(name, list(shape), dt, kind=kind).ap()
        self.dres[name] = Res(name)
        return t

    def dump(self, name, t, shape, dt=F32):
        if name not in getattr(self, 'dbg', ()):
            return
        d = self.nc.dram_tensor('dbg_' + name, list(shape), dt, kind="ExternalOutput").ap()
        self.dres['dbg_' + name] = Res(name)
        self.S.dma('sp', d, t[:], reads=[t.res], writes=[self.dres['dbg_' + name]])

    def sb(self, stack, name, shape, dt):
        return T(stack.enter_context(self.nc.sbuf_tensor(name, list(shape), dt)), name)

    def ps(self, stack, name, shape, dt=F32):
        return T(stack.enter_context(self.nc.psum_tensor(name, list(shape), dt)), name)


def rstd_from_sumsq(kb, out_ap, out_res, ss_ap, ss_res, D, eps=EPS):
    S = kb.S
    S.op('dve', lambda e: e.tensor_scalar(out_ap, ss_ap, 1.0 / D, eps, ALU.mult, ALU.add),
         reads=[ss_res], writes=[out_res])
    S.op('act', lambda e: e.activation(out_ap, out_ap, AF.Sqrt), reads=[out_res], writes=[out_res])
    S.op('dve', lambda e: e.reciprocal(out_ap, out_ap), reads=[out_res], writes=[out_res])


def chunks(n, m):
    return [(i, min(m, n - i)) for i in range(0, n, m)]


def gemm_phase(kb, name, xsrcs, jobs, epilogue, SG, NB, nbanks_per_job, ep_alloc=None):
    c, S, nc = kb.c, kb.S, kb.nc
    TG, NG, LP = c['TG'], c['NG'], c['LP']
    xk = []
    for ap, rn in xsrcs:
        Ki = ap.shape[0]
        for r0, rows in chunks(Ki, 128):
            xk.append((ap, rn, r0, rows))
    nxk = len(xk)
    for j in jobs:
        segs = []
        for gi, grp in enumerate(j['groups']):
            for (W, c0, M, xkt0, nkt) in grp:
                segs.append((gi, W, c0, M, xkt0, nkt))
        j['segs'] = segs
        j['wcols'] = sum(s[3] * s[5] for s in segs)
    maxw = max(sum(j['wcols'] for j in jobs[b:b + NB]) for b in range(0, len(jobs), NB))
    with ExitStack() as st:
        XS = kb.sb(st, name + '_xs', [128, nxk, SG * TG], BF16)
        WB = [kb.sb(st, name + '_wb%d' % i, [128, maxw], BF16) for i in range(2)]
        nps = 8 // nbanks_per_job
        nps = min(nps, 4)
        PS = [[kb.ps(st, name + '_ps%d_%d' % (i, b), [128, 512]) for b in range(nbanks_per_job)]
              for i in range(nps)]
        ctx = ep_alloc(kb, st) if ep_alloc else None
        pscnt = 0
        wbcnt = 0
        for sg0 in range(0, NG, SG):
            ngs = min(SG, NG - sg0)
            for i, (ap, rn, r0, rows) in enumerate(xk):
                S.dma('sp', XS[0:rows, i, 0:ngs * TG], ap[r0:r0 + rows, sg0 * TG:(sg0 + ngs) * TG],
                      reads=[kb.dres[rn]], writes=[XS.res])
            for b0 in range(0, len(jobs), NB):
                blk = jobs[b0:b0 + NB]
                wb = WB[wbcnt % 2]
                wbcnt += 1
                off = 0
                for j in blk:
                    j['woff'] = []
                    for (gi, W, c0, M, xkt0, nkt) in j['segs']:
                        j['woff'].append(off)
                        if id(W) in kb.relaid:
                            S.dma('pool', wb[:, off:off + nkt * M], W[:, nkt * c0:nkt * c0 + nkt * M],
                                  reads=[], writes=[wb.res])
                        else:
                            dst = wb[:, off:off + nkt * M].rearrange("p (k m) -> p k m", m=M)
                            for k0, kn in chunks(nkt, 8):
                                src = W[k0 * 128:(k0 + kn) * 128, c0:c0 + M].rearrange("(k p) m -> p k m", p=128)
                                S.dma('pool', dst[:, k0:k0 + kn, :], src, reads=[], writes=[wb.res])
                        off += nkt * M
                for j in blk:
                    for g in range(ngs):
                        ps = PS[pscnt % nps]
                        pscnt += 1
                        ng = len(j['groups'])
                        for gi in range(ng):
                            segs = [(s, o) for s, o in zip(j['segs'], j['woff']) if s[0] == gi]
                            mm = []
                            for (s, o) in segs:
                                (_, W, c0, M, xkt0, nkt) = s
                                for kt in range(nkt):
                                    rows = xk[xkt0 + kt][3]
                                    mm.append((o + kt * M, M, xkt0 + kt, rows))

                            def fn(pe, mm=mm, ps=ps, gi=gi, g=g, wb=wb):
                                ins = None
                                for idx, (wo, M, xi, rows) in enumerate(mm):
                                    ins = pe.matmul(ps[gi][0:M, 0:TG], wb[0:rows, wo:wo + M],
                                                    XS[0:rows, xi, g * TG:(g + 1) * TG],
                                                    start=(idx == 0), stop=(idx == len(mm) - 1))
                                return ins
                            S.op('pe', fn, reads=[wb.res, XS.res], writes=[ps[gi].res])
                        epilogue(kb, j, ps, sg0 + g, (sg0 + g) * TG, ctx)
    S.barrier()


def norm_phase(kb, name, D, h, y=None, coef=1.0, g_post=None, g_pre=None,
               n_out=None, h_out=None):
    c, S = kb.c, kb.S
    TG, NG = c['TG'], c['NG']
    assert D % 128 == 0
    nkt = D // 128
    with ExitStack() as st:
        Hh = kb.sb(st, name + '_h', [128, nkt, TG], F32)
        Y = kb.sb(st, name + '_y', [128, nkt, TG], F32) if y is not None else None
        SQ = kb.sb(st, name + '_sq', [128, nkt, TG], F32)
        Nn = kb.sb(st, name + '_n', [128, nkt, TG], BF16) if n_out is not None else None
        ones = kb.sb(st, name + '_ones', [128, 128], F32)
        rs = kb.sb(st, name + '_rs', [128, TG], F32)
        ps = kb.ps(st, name + '_ps', [128, 512])
        S.op('dve', lambda e: e.memset(ones[:], 1.0), writes=[ones.res])

        def stats(src):
            for kt in range(nkt):
                S.op('act', lambda e, kt=kt: e.activation(SQ[:, kt, :], src[:, kt, :], AF.Square),
                     reads=[src.res], writes=[SQ.res])

            def fn(pe):
                ins = None
                for kt in range(nkt):
                    ins = pe.matmul(ps[:, 0:TG], ones[:], SQ[:, kt, :], start=(kt == 0), stop=(kt == nkt - 1))
                return ins
            S.op('pe', fn, reads=[ones.res, SQ.res], writes=[ps.res])
            rstd_from_sumsq(kb, rs[:], rs.res, ps[:, 0:TG], ps.res, D)

        def view(t, t0):
            ap, rn, r0 = t
            return ap[r0:r0 + D, t0:t0 + TG].rearrange("(k p) t -> p k t", p=128), kb.dres[rn]

        for g in range(NG):
            t0 = g * TG
            v, r = view(h, t0)
            S.dma('sp', Hh[:], v, reads=[r], writes=[Hh.res])
            if y is not None:
                v, r = view(y, t0)
                S.dma('sp', Y[:], v, reads=[r], writes=[Y.res])
                stats(Y)
                for kt in range(nkt):
                    S.op('dve', lambda e, kt=kt: e.scalar_tensor_tensor(
                        SQ[:, kt, :], Y[:, kt, :], g_post[:, kt:kt + 1], rs[:], ALU.mult, ALU.mult),
                        reads=[Y.res, rs.res], writes=[SQ.res])
                    S.op('dve', lambda e, kt=kt: e.scalar_tensor_tensor(
                        Hh[:, kt, :], SQ[:, kt, :], float(coef), Hh[:, kt, :], ALU.mult, ALU.add),
                        reads=[SQ.res, Hh.res], writes=[Hh.res])
                if h_out is not None:
                    v, r = view(h_out, t0)
                    S.dma('sp', v, Hh[:], reads=[Hh.res], writes=[r])
            if n_out is not None:
                stats(Hh)
                for kt in range(nkt):
                    S.op('dve', lambda e, kt=kt: e.scalar_tensor_tensor(
                        Nn[:, kt, :], Hh[:, kt, :], g_pre[:, kt:kt + 1], rs[:], ALU.mult, ALU.mult),
                        reads=[Hh.res, rs.res], writes=[Nn.res])
                v, r = view(n_out, t0)
                S.dma('sp', v, Nn[:], reads=[Nn.res], writes=[r])
    S.barrier()


def ffn_phase(kb, name, nT, w13, w2, hidT, fT):
    c, S = kb.c, kb.S
    D, F, TG = c['D'], c['F'], c['TG']
    nkt = D // 128
    jobs = []
    for j in range(F // 128):
        jobs.append(dict(groups=[[(w13, j * 128, 128, 0, nkt)], [(w13, F + j * 128, 128, 0, nkt)]], j=j))

    def alloc1(kb, st):
        return dict(sa=[kb.sb(st, name + '_sa%d' % i, [128, TG], F32) for i in range(2)],
                    ho=[kb.sb(st, name + '_ho%d' % i, [128, TG], BF16) for i in range(3)], n=[0])

    def ep1(kb, job, ps, g, t0, ctx):
        i = ctx['n'][0]
        ctx['n'][0] += 1
        sa = ctx['sa'][i % 2]
        ho = ctx['ho'][i % 3]
        S.op('act', lambda e: e.activation(sa[:], ps[0][:, 0:TG], AF.Silu), reads=[ps[0].res], writes=[sa.res])
        S.op('dve', lambda e: e.tensor_tensor(ho[:], sa[:], ps[1][:, 0:TG], ALU.mult),
             reads=[sa.res, ps[1].res], writes=[ho.res])
        j = job['j']
        S.dma('sp', hidT[j * 128:(j + 1) * 128, t0:t0 + TG], ho[:], reads=[ho.res], writes=[kb.dres['hidT']])
    gemm_phase(kb, name + 'a', [(nT, 'nT')], jobs, ep1, SG=c.get('SG1', 4), NB=c.get('NB1', 2),
               nbanks_per_job=2, ep_alloc=alloc1)

    fkt = F // 128
    jobs2 = [dict(groups=[[(w2, j * 128, 128, 0, fkt)]], j=j) for j in range(D // 128)]

    def alloc2(kb, st):
        return dict(o=[kb.sb(st, name + '_fo%d' % i, [128, TG], F32) for i in range(3)], n=[0])

    def ep2(kb, job, ps, g, t0, ctx):
        i = ctx['n'][0]
        ctx['n'][0] += 1
        o = ctx['o'][i % 3]
        eng = 'act' if i % 2 == 0 else 'dve'
        if eng == 'act':
            S.op('act', lambda e: e.copy(o[:], ps[0][:, 0:TG]), reads=[ps[0].res], writes=[o.res])
        else:
            S.op('dve', lambda e: e.tensor_copy(o[:], ps[0][:, 0:TG]), reads=[ps[0].res], writes=[o.res])
        j = job['j']
        S.dma('sp', fT[j * 128:(j + 1) * 128, t0:t0 + TG], o[:], reads=[o.res], writes=[kb.dres['fT']])
    gemm_phase(kb, name + 'b', [(hidT, 'hidT')], jobs2, ep2, SG=c.get('SG2', 2), NB=1,
               nbanks_per_job=1, ep_alloc=alloc2)


def load_gain(kb, st, name, src_row_ap, D):
    nkt = D // 128
    t = kb.sb(st, name, [128, nkt], F32)
    v = src_row_ap.rearrange("(k p) -> p k", p=128)
    for k0, kn in chunks(nkt, 8):
        with kb.nc.allow_non_contiguous_dma("gain vector transpose load"):
            kb.S.dma('sp', t[:, k0:k0 + kn], v[:, k0:k0 + kn], writes=[t.res])
    return t


def rope_tables(kb, csT):
    c, S = kb.c, kb.S
    LP = c['LP']
    R2 = c['ROPE'] // 2
    I32 = mybir.dt.int32
    with ExitStack() as st:
        pi_ = kb.sb(st, 'rt_pi', [R2, LP], I32)
        pf = kb.sb(st, 'rt_pf', [R2, LP], F32)
        ii = kb.sb(st, 'rt_ii', [R2, 1], I32)
        inv = kb.sb(st, 'rt_inv', [R2, 1], F32)
        ang = kb.sb(st, 'rt_ang', [R2, LP], F32)
        cs = kb.sb(st, 'rt_cs', [R2, 2, LP], F32)
        ki = kb.sb(st, 'rt_ki', [R2, LP], I32)
        kf = kb.sb(st, 'rt_kf', [R2, LP], F32)
        S.op('pool', lambda e: e.iota(pi_[:], [[1, LP]], 0, 0), writes=[pi_.res])
        S.op('pool', lambda e: e.iota(ii[:], [[1, 1]], 0, 1), writes=[ii.res])
        S.op('dve', lambda e: e.tensor_copy(pf[:], pi_[:]), reads=[pi_.res], writes=[pf.res])
        S.op('dve', lambda e: e.tensor_copy(inv[:], ii[:]), reads=[ii.res], writes=[inv.res])
        S.op('act', lambda e: e.activation(inv[:], inv[:], AF.Exp, scale=-math.log(c['THETA']) / R2),
             reads=[inv.res], writes=[inv.res])
        S.op('dve', lambda e: e.tensor_scalar(ang[:], pf[:], inv[:, 0:1], None, ALU.mult),
             reads=[pf.res, inv.res], writes=[ang.res])
        for j, sh in enumerate((0.5 * math.pi, 0.0)):
            S.op('dve', lambda e, j=j, sh=sh: e.tensor_scalar(cs[:, j, :], ang[:], sh, None, ALU.add),
                 reads=[ang.res], writes=[cs.res])
            S.op('dve', lambda e, j=j: e.tensor_scalar(kf[:], cs[:, j, :], 1.0 / (2 * math.pi), None, ALU.mult),
                 reads=[cs.res], writes=[kf.res])
            S.op('dve', lambda e: e.tensor_copy(ki[:], kf[:]), reads=[kf.res], writes=[ki.res])
            S.op('dve', lambda e: e.tensor_copy(kf[:], ki[:]), reads=[ki.res], writes=[kf.res])
            S.op('dve', lambda e, j=j: e.scalar_tensor_tensor(cs[:, j, :], kf[:], -2 * math.pi, cs[:, j, :],
                                                              ALU.mult, ALU.add),
                 reads=[kf.res, cs.res], writes=[cs.res])
            S.op('dve', lambda e, j=j: e.tensor_scalar(cs[:, j, :], cs[:, j, :], -3.14159, 3.14159, ALU.max, ALU.min),
                 reads=[cs.res], writes=[cs.res])
            S.op('act', lambda e, j=j: e.activation(cs[:, j, :], cs[:, j, :], AF.Sin), reads=[cs.res], writes=[cs.res])
        S.dma('sp', csT, cs[:], reads=[cs.res], writes=[kb.dres['csT']])
    S.barrier()


def inproj_tiles(c):
    bounds = [c['OFF_Q'], c['OFF_KV'], c['OFF_KR'], c['OFF_S5'], c['OFF_DN_QKV'], c['OFF_DN_Z'],
              c['OFF_DN_A'], c['OFF_GATE'], c['NIN']]
    tiles = []
    for a, b in zip(bounds[:-1], bounds[1:]):
        for c0, m in chunks(b - a, 128):
            tiles.append((a + c0, m))
    return tiles


def inproj_phase(kb, name, nT, w_in, pT, gT):
    c, S = kb.c, kb.S
    D, TG = c['D'], c['TG']
    nkt = D // 128
    jobs = []
    for c0, m in inproj_tiles(c):
        jobs.append(dict(groups=[[(w_in, c0, m, 0, nkt)]], c0=c0, m=m, gate=(c0 >= c['OFF_GATE'])))

    def alloc(kb, st):
        return dict(o=[kb.sb(st, name + '_o%d' % i, [128, TG], F32) for i in range(4)], n=[0])

    def ep(kb, job, ps, g, t0, ctx):
        i = ctx['n'][0]
        ctx['n'][0] += 1
        o = ctx['o'][i % 4]
        m, c0 = job['m'], job['c0']
        if job['gate']:
            S.op('act', lambda e: e.activation(o[0:m, :], ps[0][0:m, 0:TG], AF.Sigmoid),
                 reads=[ps[0].res], writes=[o.res])
        else:
            S.op('dve', lambda e: e.tensor_copy(o[0:m, :], ps[0][0:m, 0:TG]), reads=[ps[0].res], writes=[o.res])
        if job['gate']:
            r0 = c0 - c['OFF_GATE']
            S.dma('sp', gT[r0:r0 + m, t0:t0 + TG], o[0:m, :], reads=[o.res], writes=[kb.dres['gT']])
        else:
            S.dma('sp', pT[c0:c0 + m, t0:t0 + TG], o[0:m, :], reads=[o.res], writes=[kb.dres['pT']])
    gemm_phase(kb, name, [(nT, 'nT')], jobs, ep, SG=c.get('SG1', 4), NB=4, nbanks_per_job=1, ep_alloc=alloc)


def mla_phase(kb, name, l, W, pT, scr):
    c, S = kb.c, kb.S
    TG, NG, LP, NT = c['TG'], c['NG'], c['LP'], c['NT']
    H, QL, KVL, NOPE, ROPE, VD = c['H'], c['QL'], c['KVL'], c['NOPE'], c['ROPE'], c['VD']
    R2 = ROPE // 2
    QKD = NOPE + ROPE
    scale = QKD ** -0.5
    cqnT, ckvnT, qnT, qrT, knT, krT, Vtok, oT, csT = (scr[k] for k in
                                                       ('cqnT', 'ckvnT', 'qnT', 'qrT', 'knT', 'krT', 'Vtok', 'oT', 'csT'))
    with ExitStack() as st:
        gq = load_gain(kb, st, name + '_gq', W['mla_q_norm_g'], QL)
        gkv = load_gain(kb, st, name + '_gkv', W['mla_kv_norm_g'], KVL)
        norm_phase(kb, name + '_nq', QL, h=(pT, 'pT', c['OFF_Q']), g_pre=gq, n_out=(cqnT, 'cqnT', 0))
        norm_phase(kb, name + '_nkv', KVL, h=(pT, 'pT', c['OFF_KV']), g_pre=gkv, n_out=(ckvnT, 'ckvnT', 0))
    S.barrier()
    with ExitStack() as st:
        cs = kb.sb(st, name + '_cs', [R2, 2, LP], F32)
        x12 = kb.sb(st, name + '_x12', [R2, 2, LP], F32)
        t1 = kb.sb(st, name + '_t1', [R2, LP], F32)
        t2 = kb.sb(st, name + '_t2', [R2, LP], F32)
        kr = kb.sb(st, name + '_kr', [R2, 2, LP], BF16)
        S.dma('sp', cs[:], csT, reads=[kb.dres['csT']], writes=[cs.res])
        for j in range(2):
            S.dma('sp', x12[:, j, :], pT[c['OFF_KR'] + j * R2:c['OFF_KR'] + (j + 1) * R2, :],
                  reads=[kb.dres['pT']], writes=[x12.res])
        rd = [x12.res, cs.res]
        S.op('dve', lambda e: e.tensor_tensor(t1[:], x12[:, 0, :], cs[:, 0, :], ALU.mult), reads=rd, writes=[t1.res])
        S.op('dve', lambda e: e.tensor_tensor(t2[:], x12[:, 1, :], cs[:, 1, :], ALU.mult), reads=rd, writes=[t2.res])
        S.op('dve', lambda e: e.tensor_tensor(kr[:, 0, :], t1[:], t2[:], ALU.subtract),
             reads=[t1.res, t2.res], writes=[kr.res])
        S.op('dve', lambda e: e.tensor_tensor(t1[:], x12[:, 1, :], cs[:, 0, :], ALU.mult),
             reads=rd + [kr.res], writes=[t1.res])
        S.op('dve', lambda e: e.tensor_tensor(t2[:], x12[:, 0, :], cs[:, 1, :], ALU.mult),
             reads=rd + [kr.res], writes=[t2.res])
        S.op('dve', lambda e: e.tensor_tensor(kr[:, 1, :], t1[:], t2[:], ALU.add),
             reads=[t1.res, t2.res], writes=[kr.res])
        for j in range(2):
            S.dma('sp', krT[j * R2:(j + 1) * R2, :], kr[:, j, :], reads=[kr.res], writes=[kb.dres['krT']])
    S.barrier()
    nq = QL // 128
    jobs = []
    for h in range(H):
        jobs.append(dict(groups=[[(W['mla_w_uq'], h * QKD, NOPE, 0, nq)]], h=h, kind='n'))
        jobs.append(dict(groups=[[(W['mla_w_uq'], h * QKD + NOPE, R2, 0, nq)],
                                 [(W['mla_w_uq'], h * QKD + NOPE + R2, R2, 0, nq)]], h=h, kind='r'))

    def alloc_q(kb, st):
        cs = kb.sb(st, name + '_qcs', [R2, 2, LP], F32)
        S.dma('sp', cs[:], csT, reads=[kb.dres['csT']], writes=[cs.res])
        S.op('dve', lambda e: e.tensor_scalar(cs[:], cs[:], scale, None, ALU.mult), reads=[cs.res], writes=[cs.res])
        return dict(cs=cs, o=[kb.sb(st, name + '_qo%d' % i, [128, TG], BF16) for i in range(3)],
                    r=[kb.sb(st, name + '_qr%d' % i, [R2, 2, TG], BF16) for i in range(2)],
                    t=[kb.sb(st, name + '_qt%d' % i, [R2, TG], F32) for i in range(2)], n=[0])

    def ep_q(kb, job, ps, g, t0, ctx):
        i = ctx['n'][0]
        ctx['n'][0] += 1
        h = job['h']
        if job['kind'] == 'n':
            o = ctx['o'][i % 3]
            S.op('act', lambda e: e.activation(o[:], ps[0][:, 0:TG], AF.Copy, scale=scale),
                 reads=[ps[0].res], writes=[o.res])
            S.dma('sp', qnT[h * NOPE:(h + 1) * NOPE, t0:t0 + TG], o[:], reads=[o.res], writes=[kb.dres['qnT']])
        else:
            cs = ctx['cs']
            r = ctx['r'][i % 2]
            t1, t2 = ctx['t']
            x1, x2 = ps[0][0:R2, 0:TG], ps[1][0:R2, 0:TG]
            cc, ss = cs[:, 0, t0:t0 + TG], cs[:, 1, t0:t0 + TG]
            rd = [ps[0].res, ps[1].res, cs.res]
            S.op('dve', lambda e: e.tensor_tensor(t1[:], x1, cc, ALU.mult), reads=rd, writes=[t1.res])
            S.op('dve', lambda e: e.tensor_tensor(t2[:], x2, ss, ALU.mult), reads=rd, writes=[t2.res])
            S.op('dve', lambda e: e.tensor_tensor(r[:, 0, :], t1[:], t2[:], ALU.subtract),
                 reads=[t1.res, t2.res], writes=[r.res])
            S.op('dve', lambda e: e.tensor_tensor(t1[:], x2, cc, ALU.mult), reads=rd + [r.res], writes=[t1.res])
            S.op('dve', lambda e: e.tensor_tensor(t2[:], x1, ss, ALU.mult), reads=rd + [r.res], writes=[t2.res])
            S.op('dve', lambda e: e.tensor_tensor(r[:, 1, :], t1[:], t2[:], ALU.add),
                 reads=[t1.res, t2.res], writes=[r.res])
            for j in range(2):
                S.dma('sp', qrT[h * ROPE + j * R2:h * ROPE + (j + 1) * R2, t0:t0 + TG], r[:, j, :],
                      reads=[r.res], writes=[kb.dres['qrT']])
    gemm_phase(kb, name + '_q', [(cqnT, 'cqnT')], jobs, ep_q, SG=NG, NB=2, nbanks_per_job=2, ep_alloc=alloc_q)
    nkv = KVL // 128
    jobs = [dict(groups=[[(W['mla_w_ukv'], h * (NOPE + VD), NOPE, 0, nkv)]], h=h) for h in range(H)]

    def alloc_k(kb, st):
        return dict(o=[kb.sb(st, name + '_ko%d' % i, [128, TG], BF16) for i in range(3)], n=[0])

    def ep_k(kb, job, ps, g, t0, ctx):
        i = ctx['n'][0]
        ctx['n'][0] += 1
        o = ctx['o'][i % 3]
        h = job['h']
        S.op('act', lambda e: e.copy(o[:], ps[0][:, 0:TG]), reads=[ps[0].res], writes=[o.res])
        S.dma('sp', knT[h * NOPE:(h + 1) * NOPE, t0:t0 + TG], o[:], reads=[o.res], writes=[kb.dres['knT']])
    gemm_phase(kb, name + '_k', [(ckvnT, 'ckvnT')], jobs, ep_k, SG=NG, NB=4, nbanks_per_job=1, ep_alloc=alloc_k)
    with ExitStack() as st:
        X = kb.sb(st, name + '_vx', [128, nkv, LP], BF16)
        Wv = kb.sb(st, name + '_vw', [128, nkv, H * VD], BF16)
        vo = [kb.sb(st, name + '_vo%d' % i, [128, 512], BF16) for i in range(3)]
        pv = [kb.ps(st, name + '_vps%d' % i, [128, 512]) for i in range(4)]
        for k in range(nkv):
            S.dma('sp', X[:, k, :], ckvnT[k * 128:(k + 1) * 128, :], reads=[kb.dres['ckvnT']], writes=[X.res])
        for h in range(H):
            c0 = h * (NOPE + VD) + NOPE
            S.dma('pool', Wv[:, :, h * VD:(h + 1) * VD],
                  W['mla_w_ukv'][:, c0:c0 + VD].rearrange("(k p) m -> p k m", p=128), writes=[Wv.res])
        cnt = 0
        HV = H * VD
        for t in range(NT):
            for c0, cw in chunks(HV, 512):
                ps = pv[cnt % 4]
                o = vo[cnt % 3]
                cnt += 1

                def fn(pe, ps=ps, t=t, c0=c0, cw=cw):
                    ins = None
                    for k in range(nkv):
                        ins = pe.matmul(ps[:, 0:cw], X[:, k, t * 128:(t + 1) * 128], Wv[:, k, c0:c0 + cw],
                                        start=(k == 0), stop=(k == nkv - 1))
                    return ins
                S.op('pe', fn, reads=[X.res, Wv.res], writes=[ps.res])
                S.op('act', lambda e, o=o, ps=ps, cw=cw: e.copy(o[:, 0:cw], ps[:, 0:cw]), reads=[ps.res], writes=[o.res])
                S.dma('sp', Vtok[t * 128:(t + 1) * 128, c0:c0 + cw], o[:, 0:cw], reads=[o.res], writes=[kb.dres['Vtok']])
    S.barrier()
    gt = TG // 128
    npart = gt + 1
    with ExitStack() as st:
        masks = kb.sb(st, name + '_mask', [128, npart, TG], BF16)
        onesb = kb.sb(st, name + '_ones', [128, 128], BF16)
        Kr = kb.sb(st, name + '_Kr', [ROPE, LP], BF16)
        Qn = [kb.sb(st, name + '_Qn%d' % i, [128, LP], BF16) for i in range(2)]
        Qr = [kb.sb(st, name + '_Qr%d' % i, [ROPE, LP], BF16) for i in range(2)]
        Kn = [kb.sb(st, name + '_Kn%d' % i, [128, LP], BF16) for i in range(2)]
        Vh = [kb.sb(st, name + '_Vh%d' % i, [128, NT, VD], BF16) for i in range(2)]
        PT = [kb.sb(st, name + '_PT%d' % i, [128, TG], BF16) for i in range(3)]
        rden = kb.sb(st, name + '_rden', [128, TG], F32)
        oo = [kb.sb(st, name + '_oo%d' % i, [128, TG], BF16) for i in range(2)]
        pS = [kb.ps(st, name + '_pS%d' % i, [128, 512]) for i in range(3)]
        pO = [kb.ps(st, name + '_pO%d' % i, [128, 512]) for i in range(2)]
        pD = [kb.ps(st, name + '_pD%d' % i, [128, 512]) for i in range(2)]
        S.op('dve', lambda e: e.memset(onesb[:], 1.0), writes=[onesb.res])
        S.op('dve', lambda e: e.memset(masks[:], 0.0), writes=[masks.res])
        for j in range(npart):
            for m in range(-1, (TG - 16) // 64 + 1):
                q0 = max(0, 16 + 64 * m)
                q1 = min(TG, 80 + 64 * m)
                thr = min(128, 80 + 64 * m - 128 * j)
                if thr > 0 and q1 > q0:
                    S.op('dve', lambda e, j=j, thr=thr, q0=q0, q1=q1: e.memset(masks[0:thr, j, q0:q1], 1.0),
                         writes=[masks.res])
        S.dma('sp', Kr[:], krT, reads=[kb.dres['krT']], writes=[Kr.res])
        cnt = 0
        for h in range(H):
            b = h % 2
            S.dma('sp', Qn[b][:], qnT[h * NOPE:(h + 1) * NOPE, :], reads=[kb.dres['qnT']], writes=[Qn[b].res])
            S.dma('sp', Qr[b][:], qrT[h * ROPE:(h + 1) * ROPE, :], reads=[kb.dres['qrT']], writes=[Qr[b].res])
            S.dma('sp', Kn[b][:], knT[h * NOPE:(h + 1) * NOPE, :], reads=[kb.dres['knT']], writes=[Kn[b].res])
            S.dma('sp', Vh[b][:], Vtok[:, h * VD:(h + 1) * VD].rearrange("(t p) d -> p t d", p=128),
                  reads=[kb.dres['Vtok']], writes=[Vh[b].res])
            for g in range(NG):
                q0 = g * TG
                kts = list(range(0, min(gt * g + npart, NT)))
                po, pd = pO[g % 2], pD[g % 2]
                for ki, kt in enumerate(kts):
                    ps = pS[cnt % 3]
                    pt = PT[cnt % 3]
                    cnt += 1

                    def fs(pe, ps=ps, kt=kt, q0=q0, b=b):
                        pe.matmul(ps[:, 0:TG], Kn[b][:, kt * 128:(kt + 1) * 128], Qn[b][:, q0:q0 + TG],
                                  start=True, stop=False)
                        return pe.matmul(ps[:, 0:TG], Kr[:, kt * 128:(kt + 1) * 128], Qr[b][:, q0:q0 + TG],
                                         start=False, stop=True)
                    S.op('pe', fs, reads=[Kn[b].res, Qn[b].res, Kr.res, Qr[b].res], writes=[ps.res])
                    S.op('act', lambda e, pt=pt, ps=ps: e.activation(pt[:], ps[:, 0:TG], AF.Exp),
                         reads=[ps.res], writes=[pt.res])
                    j = kt - gt * g
                    if j >= 0:
                        S.op('dve', lambda e, pt=pt, j=j: e.tensor_tensor(pt[:], pt[:], masks[:, j, :], ALU.mult),
                             reads=[pt.res, masks.res], writes=[pt.res])
                    first, last = (ki == 0), (ki == len(kts) - 1)

                    def fo(pe, pt=pt, kt=kt, b=b, first=first, last=last, po=po, pd=pd):
                        pe.matmul(po[:, 0:TG], Vh[b][:, kt, :], pt[:], start=first, stop=last)
                        return pe.matmul(pd[:, 0:TG], onesb[:], pt[:], start=first, stop=last)
                    S.op('pe', fo, reads=[Vh[b].res, pt.res, onesb.res], writes=[po.res, pd.res])
                o = oo[g % 2]
                S.op('dve', lambda e, pd=pd: e.reciprocal(rden[:], pd[:, 0:TG]), reads=[pd.res], writes=[rden.res])
                S.op('dve', lambda e, o=o, po=po: e.tensor_tensor(o[:], po[:, 0:TG], rden[:], ALU.mult),
                     reads=[po.res, rden.res], writes=[o.res])
                S.dma('sp', oT[h * VD:(h + 1) * VD, q0:q0 + TG], o[:], reads=[o.res], writes=[kb.dres['oT']])
    S.barrier()


def s5_phase(kb, name, l, W, pT, s5hT):
    c, S = kb.c, kb.S
    LP = c['LP']
    G, P, NGr, SW = c['S5G'], c['S5P'], c['S5NG'], c['S5W']
    assert P == 64 and G == 16
    NST = NGr // 2
    CH = 512
    with ExitStack() as st:
        def sbt(n, shape, dt=F32):
            return kb.sb(st, name + '_' + n, shape, dt)
        ar, ai, dl = sbt('ar', [128, NST]), sbt('ai', [128, NST]), sbt('dl', [128, NST])
        mag, th, co, si = sbt('mag', [128, NST]), sbt('th', [128, NST]), sbt('co', [128, NST]), sbt('si', [128, NST])
        abr, abi, den = sbt('abr', [128, NST]), sbt('abi', [128, NST]), sbt('den', [128, NST])
        zr, zi, t1, t2 = sbt('zr', [128, NST]), sbt('zi', [128, NST]), sbt('t1', [128, NST]), sbt('t2', [128, NST])
        ki = sbt('ki', [128, NST], mybir.dt.int32)
        nl = W['s5_log_dt']
        with kb.nc.allow_non_contiguous_dma("small s5 parameter loads"):
            for two in range(2):
                S.dma('sp', ar[two * 64:(two + 1) * 64, :],
                      W['s5_a_re'].rearrange("(s two) p -> two p s", two=2)[two], writes=[ar.res])
                S.dma('sp', ai[two * 64:(two + 1) * 64, :],
                      W['s5_a_im'].rearrange("(s two) p -> two p s", two=2)[two], writes=[ai.res])
                S.dma('sp', dl[two * 64:(two + 1) * 64, :],
                      nl.rearrange("(s two) -> two s", two=2)[two:two + 1, :].broadcast_to([64, NST]),
                      writes=[dl.res])

        def ew(eng, fn, reads, writes):
            S.op(eng, fn, reads=[r.res for r in reads], writes=[w.res for w in writes])

        def sincos(th_, co_, si_):
            y, y2, p_ = t1, t2, zr
            ew('dve', lambda e: e.tensor_scalar(y[:], th_[:], 1.0 / (2 * math.pi), None, ALU.mult), [th_], [y])
            ew('dve', lambda e: e.tensor_copy(ki[:], y[:]), [y], [ki])
            ew('dve', lambda e: e.tensor_copy(y[:], ki[:]), [ki], [y])
            ew('dve', lambda e: e.scalar_tensor_tensor(y[:], y[:], -2 * math.pi, th_[:], ALU.mult, ALU.add), [y, th_], [y])
            ew('dve', lambda e: e.tensor_scalar(y[:], y[:], 0.125, None, ALU.mult), [y], [y])
            ew('dve', lambda e: e.tensor_tensor(y2[:], y[:], y[:], ALU.mult), [y], [y2])
            ew('dve', lambda e: e.tensor_scalar(p_[:], y2[:], 1.0 / 362880, None, ALU.mult), [y2], [p_])
            for a_ in (-1.0 / 5040, 1.0 / 120, -1.0 / 6):
                ew('dve', lambda e, a_=a_: e.scalar_tensor_tensor(p_[:], p_[:], a_, y2[:], ALU.add, ALU.mult), [p_, y2], [p_])
            ew('dve', lambda e: e.scalar_tensor_tensor(si_[:], p_[:], 1.0, y[:], ALU.add, ALU.mult), [p_, y], [si_])
            ew('dve', lambda e: e.tensor_scalar(p_[:], y2[:], -1.0 / 3628800, None, ALU.mult), [y2], [p_])
            for a_ in (1.0 / 40320, -1.0 / 720, 1.0 / 24, -0.5):
                ew('dve', lambda e, a_=a_: e.scalar_tensor_tensor(p_[:], p_[:], a_, y2[:], ALU.add, ALU.mult), [p_, y2], [p_])
            ew('dve', lambda e: e.tensor_scalar(co_[:], p_[:], 1.0, None, ALU.add), [p_], [co_])
            for _ in range(3):
                ew('dve', lambda e: e.tensor_tensor(y2[:], si_[:], si_[:], ALU.mult), [si_], [y2])
                ew('dve', lambda e: e.scalar_tensor_tensor(si_[:], si_[:], 2.0, co_[:], ALU.mult, ALU.mult), [si_, co_], [si_])
                ew('dve', lambda e: e.tensor_scalar(co_[:], y2[:], -2.0, 1.0, ALU.mult, ALU.add), [y2], [co_])

        ew('act', lambda e: e.activation(dl[:], dl[:], AF.Exp), [dl], [dl])
        ew('dve', lambda e: e.tensor_tensor(mag[:], ar[:], dl[:], ALU.mult), [ar, dl], [mag])
        ew('act', lambda e: e.activation(mag[:], mag[:], AF.Exp), [mag], [mag])
        ew('dve', lambda e: e.tensor_tensor(th[:], ai[:], dl[:], ALU.mult), [ai, dl], [th])
        sincos(th, co, si)
        ew('dve', lambda e: e.tensor_tensor(abr[:], mag[:], co[:], ALU.mult), [mag, co], [abr])
        ew('dve', lambda e: e.tensor_tensor(abi[:], mag[:], si[:], ALU.mult), [mag, si], [abi])
        ew('dve', lambda e: e.tensor_tensor(den[:], ar[:], ar[:], ALU.mult), [ar], [den])
        ew('dve', lambda e: e.tensor_tensor(t1[:], ai[:], ai[:], ALU.mult), [ai], [t1])
        ew('dve', lambda e: e.tensor_tensor(den[:], den[:], t1[:], ALU.add), [den, t1], [den])
        ew('dve', lambda e: e.reciprocal(den[:], den[:]), [den], [den])
        ew('dve', lambda e: e.tensor_scalar(t2[:], abr[:], -1.0, None, ALU.add), [abr], [t2])
        ew('dve', lambda e: e.tensor_tensor(zr[:], t2[:], ar[:], ALU.mult), [t2, ar], [zr])
        ew('dve', lambda e: e.tensor_tensor(t1[:], abi[:], ai[:], ALU.mult), [abi, ai], [t1])
        ew('dve', lambda e: e.tensor_tensor(zr[:], zr[:], t1[:], ALU.add), [zr, t1], [zr])
        ew('dve', lambda e: e.tensor_tensor(zr[:], zr[:], den[:], ALU.mult), [zr, den], [zr])
        ew('dve', lambda e: e.tensor_tensor(zi[:], abi[:], ar[:], ALU.mult), [abi, ar], [zi])
        ew('dve', lambda e: e.tensor_tensor(t1[:], t2[:], ai[:], ALU.mult), [t2, ai], [t1])
        ew('dve', lambda e: e.tensor_tensor(zi[:], zi[:], t1[:], ALU.subtract), [zi, t1], [zi])
        ew('dve', lambda e: e.tensor_tensor(zi[:], zi[:], den[:], ALU.mult), [zi, den], [zi])
        NLV = int(math.ceil(math.log2(LP)))
        pwr, pwi = sbt('pwr', [128, NLV, NST]), sbt('pwi', [128, NLV, NST])
        npwi = sbt('npwi', [128, NLV, NST])
        ew('dve', lambda e: e.tensor_copy(pwr[:, 0, :], abr[:]), [abr], [pwr])
        ew('dve', lambda e: e.tensor_copy(pwi[:, 0, :], abi[:]), [abi], [pwi])
        for k in range(1, NLV):
            ew('dve', lambda e, k=k: e.tensor_tensor(t1[:], pwr[:, k - 1, :], pwr[:, k - 1, :], ALU.mult), [pwr], [t1])
            ew('dve', lambda e, k=k: e.tensor_tensor(t2[:], pwi[:, k - 1, :], pwi[:, k - 1, :], ALU.mult), [pwi], [t2])
            ew('dve', lambda e, k=k: e.tensor_tensor(pwr[:, k, :], t1[:], t2[:], ALU.subtract), [t1, t2], [pwr])
            ew('dve', lambda e, k=k: e.tensor_tensor(t1[:], pwr[:, k - 1, :], pwi[:, k - 1, :], ALU.mult), [pwr, pwi], [t1])
            ew('dve', lambda e, k=k: e.tensor_scalar(pwi[:, k, :], t1[:], 2.0, None, ALU.mult), [t1], [pwi])
        ew('dve', lambda e: e.tensor_scalar(npwi[:], pwi[:], -1.0, None, ALU.mult), [pwi], [npwi])
        Bb_r, Bb_i = sbt('Bb_r', [128, NST, 32]), sbt('Bb_i', [128, NST, 32])
        Br, Bi = sbt('Br', [128, NST, 32]), sbt('Bi', [128, NST, 32])
        for t_ in (Br, Bi):
            ew('dve', lambda e, t_=t_: e.memset(t_[:], 0.0), [], [t_])
        with kb.nc.allow_non_contiguous_dma("small s5 parameter loads"):
            for two in range(2):
                for (src, dst) in ((W['s5_b_re'], Br), (W['s5_b_im'], Bi)):
                    v = src.rearrange("(s two) p c -> two p s c", two=2)[two]
                    for s0, sn in chunks(NST, 8):
                        S.dma('sp', dst[two * 64:(two + 1) * 64, s0:s0 + sn, two * 16:(two + 1) * 16],
                              v[:, s0:s0 + sn, :], writes=[dst.res])
        zrb = zr[:].unsqueeze(2).to_broadcast([128, NST, 32])
        zib = zi[:].unsqueeze(2).to_broadcast([128, NST, 32])
        tB = sbt('tB', [128, NST, 32])
        ew('dve', lambda e: e.tensor_tensor(Bb_r[:], Br[:], zrb, ALU.mult), [Br, zr], [Bb_r])
        ew('dve', lambda e: e.tensor_tensor(tB[:], Bi[:], zib, ALU.mult), [Bi, zi], [tB])
        ew('dve', lambda e: e.tensor_tensor(Bb_r[:], Bb_r[:], tB[:], ALU.subtract), [Bb_r, tB], [Bb_r])
        ew('dve', lambda e: e.tensor_tensor(Bb_i[:], Bi[:], zrb, ALU.mult), [Bi, zr], [Bb_i])
        ew('dve', lambda e: e.tensor_tensor(tB[:], Br[:], zib, ALU.mult), [Br, zi], [tB])
        ew('dve', lambda e: e.tensor_tensor(Bb_i[:], Bb_i[:], tB[:], ALU.add), [Bb_i, tB], [Bb_i])
        ident = sbt('ident', [128, 128])
        ew('dve', lambda e: e.memset(ident[:], 0.0), [], [ident])
        S.op('pool', lambda e: e.affine_select(ident[:], ident[:], [[-1, 128]], ALU.not_equal, 1.0, 0, 1),
             reads=[ident.res], writes=[ident.res])
        BT_r, BT_i = sbt('BT_r', [32, NST, 128]), sbt('BT_i', [32, NST, 128])
        ptr = [kb.ps(st, name + '_ptr%d' % i, [128, 512]) for i in range(2)]
        n = 0
        for (src, dst) in ((Bb_r, BT_r), (Bb_i, BT_i)):
            for s in range(NST):
                p_ = ptr[n % 2]
                n += 1
                S.op('pe', lambda e, p_=p_, src=src, s=s: e.transpose(p_[0:32, 0:128], src[:, s, :], ident[:]),
                     reads=[src.res, ident.res], writes=[p_.res])
                S.op('dve', lambda e, p_=p_, dst=dst, s=s: e.tensor_copy(dst[:, s, :], p_[0:32, 0:128]),
                     reads=[p_.res], writes=[dst.res])
        NT8 = NGr // 8
        Xc = sbt('Xc', [128, 2, 128])
        Cb_r, Cb_i = sbt('Cb_r', [128, NT8, 128]), sbt('Cb_i', [128, NT8, 128])
        for T8 in range(NT8):
            ew('dve', lambda e: e.memset(Xc[:], 0.0), [], [Xc])
            for ri, src in enumerate((W['s5_c_re'], W['s5_c_im'])):
                for gl in range(8):
                    S.dma('sp', Xc[16 * gl:16 * gl + 16, ri, (gl % 2) * 64:(gl % 2) * 64 + 64], src[8 * T8 + gl],
                          writes=[Xc.res])
            for ri, dst in enumerate((Cb_r, Cb_i)):
                p_ = ptr[n % 2]
                n += 1
                S.op('pe', lambda e, p_=p_, ri=ri: e.transpose(p_[:, 0:128], Xc[:, ri, :], ident[:]),
                     reads=[Xc.res, ident.res], writes=[p_.res])
                if ri == 0:
                    S.op('dve', lambda e, p_=p_, dst=dst, T8=T8: e.tensor_copy(dst[:, T8, :], p_[:, 0:128]),
                         reads=[p_.res], writes=[dst.res])
                else:
                    S.op('dve', lambda e, p_=p_, dst=dst, T8=T8: e.tensor_scalar(dst[:, T8, :], p_[:, 0:128], -1.0, None,
                                                                               ALU.mult),
                         reads=[p_.res], writes=[dst.res])
        dsk = sbt('dsk', [32, NST])
        with kb.nc.allow_non_contiguous_dma("small s5 parameter loads"):
            S.dma('sp', dsk[:], W['s5_d'].rearrange("(s two) c -> (two c) s", two=2), writes=[dsk.res])
        kb.dump('abr', abr, [128, NST]); kb.dump('abi', abi, [128, NST]); kb.dump('zr', zr, [128, NST]); kb.dump('zi', zi, [128, NST])
        kb.dump('Bb_r', Bb_r, [128, NST, 32]); kb.dump('BT_r', BT_r, [32, NST, 128]); kb.dump('Cb_r', Cb_r, [128, NT8, 128])
        kb.dump('pwr', pwr, [128, NLV, NST])
        u = [sbt('u%d' % i, [32, LP]) for i in range(2)]
        X = [[sbt('x%d%d' % (i, j), [128, LP]) for j in range(2)] for i in range(2)]
        tmp = sbt('ptmp', [128, LP])
        yo = [sbt('yo%d' % i, [32, CH]) for i in range(2)]
        y2 = sbt('y2', [32, CH])
        hb = [sbt('hb%d' % i, [32, CH], BF16) for i in range(2)]
        pb = [kb.ps(st, name + '_pb%d' % i, [128, 512]) for i in range(4)]
        pc = 0
        for s in range(NST):
            us = u[s % 2]
            S.dma('sp', us[:], pT[c['OFF_S5'] + 32 * s:c['OFF_S5'] + 32 * s + 32, :],
                  reads=[kb.dres['pT']], writes=[us.res])
            cur = 0
            for ri, BT in enumerate((BT_r, BT_i)):
                for c0, cw in chunks(LP, CH):
                    p_ = pb[pc % 4]
                    pc += 1
                    S.op('pe', lambda e, p_=p_, BT=BT, c0=c0, cw=cw, us=us, s=s:
                         e.matmul(p_[:, 0:cw], BT[:, s, :], us[:, c0:c0 + cw], start=True, stop=True),
                         reads=[BT.res, us.res], writes=[p_.res])
                    S.op('act', lambda e, p_=p_, ri=ri, c0=c0, cw=cw: e.copy(X[0][ri][:, c0:c0 + cw], p_[:, 0:cw]),
                         reads=[p_.res], writes=[X[0][ri].res])
            for k in range(NLV):
                d = 1 << k
                if d >= LP:
                    break
                a, b_ = X[cur], X[1 - cur]
                Pr, Pi, nPi = pwr[:, k, s:s + 1], pwi[:, k, s:s + 1], npwi[:, k, s:s + 1]
                n_ = LP - d
                ew('dve', lambda e: e.scalar_tensor_tensor(b_[0][:, d:LP], a[0][:, 0:n_], Pr, a[0][:, d:LP],
                                                           ALU.mult, ALU.add), [a[0], pwr], [b_[0]])
                ew('dve', lambda e: e.scalar_tensor_tensor(b_[0][:, d:LP], a[1][:, 0:n_], nPi, b_[0][:, d:LP],
                                                           ALU.mult, ALU.add), [a[1], npwi, b_[0]], [b_[0]])
                ew('act', lambda e: e.copy(b_[0][:, 0:d], a[0][:, 0:d]), [a[0]], [b_[0]])
                ew('dve', lambda e: e.scalar_tensor_tensor(b_[1][:, d:LP], a[1][:, 0:n_], Pr, a[1][:, d:LP],
                                                           ALU.mult, ALU.add), [a[1], pwr], [b_[1]])
                ew('dve', lambda e: e.scalar_tensor_tensor(b_[1][:, d:LP], a[0][:, 0:n_], Pi, b_[1][:, d:LP],
                                                           ALU.mult, ALU.add), [a[0], pwi, b_[1]], [b_[1]])
                ew('act', lambda e: e.copy(b_[1][:, 0:d], a[1][:, 0:d]), [a[1]], [b_[1]])
                cur = 1 - cur
            xr, xi = X[cur]
            if s == 0:
                kb.dump('xr0', xr, [128, LP]); kb.dump('bu0', X[0][0], [128, LP])
            T8, j4 = s // 4, s % 4
            for ci, (c0, cw) in enumerate(chunks(LP, CH)):
                p_ = pb[pc % 4]
                pc += 1
                yy, hh = yo[ci % 2], hb[ci % 2]

                def fy(pe, p_=p_, c0=c0, cw=cw, xr=xr, xi=xi, T8=T8, j4=j4):
                    pe.matmul(p_[0:32, 0:cw], Cb_r[:, T8, 32 * j4:32 * j4 + 32], xr[:, c0:c0 + cw], start=True, stop=False)
                    return pe.matmul(p_[0:32, 0:cw], Cb_i[:, T8, 32 * j4:32 * j4 + 32], xi[:, c0:c0 + cw],
                                     start=False, stop=True)
                S.op('pe', fy, reads=[Cb_r.res, Cb_i.res, xr.res, xi.res], writes=[p_.res])
                ew_r = [p_.res, us.res, dsk.res]
                S.op('dve', lambda e, p_=p_, yy=yy, c0=c0, cw=cw, us=us, s=s: e.scalar_tensor_tensor(
                    yy[:, 0:cw], us[:, c0:c0 + cw], dsk[:, s:s + 1], p_[0:32, 0:cw], ALU.mult, ALU.add),
                    reads=ew_r, writes=[yy.res])
                S.op('dve', lambda e, yy=yy, cw=cw: e.tensor_tensor(y2[:, 0:cw], yy[:, 0:cw], yy[:, 0:cw], ALU.mult),
                     reads=[yy.res], writes=[y2.res])
                S.op('dve', lambda e, cw=cw: e.tensor_scalar(y2[:, 0:cw], y2[:, 0:cw], 0.044715, 1.0, ALU.mult, ALU.add),
                     reads=[y2.res], writes=[y2.res])
                S.op('dve', lambda e, yy=yy, cw=cw: e.tensor_tensor(y2[:, 0:cw], y2[:, 0:cw], yy[:, 0:cw], ALU.mult),
                     reads=[y2.res, yy.res], writes=[y2.res])
                S.op('act', lambda e, cw=cw: e.activation(y2[:, 0:cw], y2[:, 0:cw], AF.Sigmoid, scale=1.5957691216057308),
                     reads=[y2.res], writes=[y2.res])
                S.op('dve', lambda e, yy=yy, hh=hh, cw=cw: e.tensor_tensor(hh[:, 0:cw], y2[:, 0:cw], yy[:, 0:cw], ALU.mult),
                     reads=[y2.res, yy.res], writes=[hh.res])
                S.dma('sp', s5hT[32 * s:32 * s + 32, c0:c0 + cw], hh[:, 0:cw], reads=[hh.res], writes=[kb.dres['s5hT']])
    S.barrier()


def make_ident(kb, st, name):
    ident = kb.sb(st, name, [128, 128], F32)
    kb.S.op('dve', lambda e: e.memset(ident[:], 0.0), writes=[ident.res])
    kb.S.op('pool', lambda e: e.affine_select(ident[:], ident[:], [[-1, 128]], ALU.not_equal, 1.0, 0, 1),
            reads=[ident.res], writes=[ident.res])
    return ident


def dn_phase(kb, name, l, W, pT, bgT, dnoT):
    c, S = kb.c, kb.S
    LP, NT = c['LP'], c['NT']
    NH, DK, DV = c['DNH'], c['DK'], c['DV']
    assert DK == 128 and DV == 128
    QKW = c['DNQK']
    CH = 512
    NLV = 7

    def ew(eng, fn, reads, writes):
        S.op(eng, fn, reads=[r.res for r in reads], writes=[w.res for w in writes])

    with ExitStack() as st:
        ab = kb.sb(st, name + '_ab', [NH, 2, LP], F32)
        gg = [kb.sb(st, name + '_gg%d' % i, [NH, LP], F32) for i in range(2)]
        prm = kb.sb(st, name + '_prm', [NH, 2], F32)
        for j, off in enumerate((c['OFF_DN_A'], c['OFF_DN_B'])):
            S.dma('sp', ab[:, j, :], pT[off:off + NH, :], reads=[kb.dres['pT']], writes=[ab.res])
        with kb.nc.allow_non_contiguous_dma("tiny"):
            S.dma('sp', prm[:, 0:1], W['dn_a_log'].rearrange("(h o) -> h o", o=1), writes=[prm.res])
            S.dma('sp', prm[:, 1:2], W['dn_dt_bias'].rearrange("(h o) -> h o", o=1), writes=[prm.res])
        ew('act', lambda e: e.activation(ab[:, 1, :], ab[:, 1, :], AF.Sigmoid), [ab], [ab])
        ew('act', lambda e: e.activation(prm[:, 0:1], prm[:, 0:1], AF.Exp), [prm], [prm])
        ew('dve', lambda e: e.tensor_scalar(prm[:, 0:1], prm[:, 0:1], -1.0, None, ALU.mult), [prm], [prm])
        ew('act', lambda e: e.activation(gg[0][:], ab[:, 0, :], AF.Exp, bias=prm[:, 1:2]), [ab, prm], [gg[0]])
        ew('act', lambda e: e.activation(gg[0][:], gg[0][:], AF.Ln, bias=1.0), [gg[0]], [gg[0]])
        ew('dve', lambda e: e.tensor_scalar(gg[0][:], gg[0][:], prm[:, 0:1], None, ALU.mult), [gg[0], prm], [gg[0]])
        cur = 0
        for k in range(NLV):
            d = 1 << k
            a3 = gg[cur][:].rearrange("h (t p) -> h t p", p=128)
            b3 = gg[1 - cur][:].rearrange("h (t p) -> h t p", p=128)
            ew('dve', lambda e: e.tensor_tensor(b3[:, :, d:128], a3[:, :, d:128], a3[:, :, 0:128 - d], ALU.add),
               [gg[cur]], [gg[1 - cur]])
            ew('dve', lambda e: e.tensor_copy(b3[:, :, 0:d], a3[:, :, 0:d]), [gg[cur]], [gg[1 - cur]])
            cur = 1 - cur
        S.dma('sp', bgT[0:NH, :], ab[:, 1, :], reads=[ab.res], writes=[kb.dres['bgT']])
        S.dma('sp', bgT[NH:2 * NH, :], gg[cur][:], reads=[gg[cur].res], writes=[kb.dres['bgT']])
    S.barrier()

    with ExitStack() as st:
        def sbt(n, shape, dt=F32):
            return kb.sb(st, name + '_' + n, shape, dt)
        ident = make_ident(kb, st, name + '_ident')
        ones = sbt('ones', [128, 128])
        ew('dve', lambda e: e.memset(ones[:], 1.0), [], [ones])
        m_incl, m_strict = sbt('m_incl', [128, 128]), sbt('m_strict', [128, 128])
        ew('dve', lambda e: e.memset(m_incl[:], 1.0), [], [m_incl])
        ew('dve', lambda e: e.memset(m_strict[:], 1.0), [], [m_strict])
        S.op('pool', lambda e: e.affine_select(m_incl[:], m_incl[:], [[-1, 128]], ALU.is_ge, 0.0, 0, 1),
             reads=[m_incl.res], writes=[m_incl.res])
        S.op('pool', lambda e: e.affine_select(m_strict[:], m_strict[:], [[-1, 128]], ALU.is_gt, 0.0, 0, 1),
             reads=[m_strict.res], writes=[m_strict.res])
        tm = sbt('tm', [128, NT, 2 * NH])
        with kb.nc.allow_non_contiguous_dma("token-major per-token scalars"):
            for t in range(NT):
                S.dma('sp', tm[:, t, :], bgT[:, t * 128:(t + 1) * 128].rearrange("r p -> p r"),
                      reads=[kb.dres['bgT']], writes=[tm.res])
        cwt = sbt('cwt', [128, 3, 4])
        gout = sbt('gout', [128, 1])
        with kb.nc.allow_non_contiguous_dma("tiny"):
            S.dma('sp', gout[:], W['dn_out_norm_g'].rearrange("(p o) -> p o", o=1), writes=[gout.res])
        qT, kT, vT = sbt('qT', [128, LP]), sbt('kT', [128, LP]), sbt('vT', [128, LP])
        tmp, gcrow, qe, oT = sbt('tmp', [128, LP]), sbt('gcrow', [128, LP]), sbt('qe', [128, LP]), sbt('oT', [128, LP])
        rn = sbt('rn', [128, CH])
        sc = sbt('sc', [128, 6, NT])
        Sst = [sbt('S%d' % i, [128, 128]) for i in range(2)]
        Rt = [sbt('R%d' % i, [128, 256]) for i in range(2)]
        kd, vn, wT = sbt('kd', [128, 128]), sbt('vn', [128, 128]), sbt('wT', [128, 128])
        E, Dm, Dms, QKm, QKT = (sbt(n, [128, 128]) for n in ('E', 'Dm', 'Dms', 'QKm', 'QKT'))
        Ak = [sbt('A%d' % i, [128, 128]) for i in range(2)]
        Bk = [sbt('B%d' % i, [128, 128]) for i in range(2)]
        ob = [sbt('ob%d' % i, [128, CH], BF16) for i in range(2)]
        PSL = [kb.ps(st, name + '_ps%d' % i, [128, 512]) for i in range(8)]
        pcn = [0]

        def nps():
            p = PSL[pcn[0] % 8]
            pcn[0] += 1
            return p

        def colsumsq(src, D_, eps, dst_rn, c0, cw):
            p_ = nps()
            ew('act', lambda e: e.activation(tmp[:, c0:c0 + cw], src[:, c0:c0 + cw], AF.Square), [src], [tmp])
            ew('pe', lambda e: e.matmul(p_[:, 0:cw], ones[:], tmp[:, c0:c0 + cw], start=True, stop=True), [ones, tmp], [p_])
            rstd_from_sumsq(kb, dst_rn[:, 0:cw], dst_rn.res, p_[:, 0:cw], p_.res, D_, eps)

        for h in range(NH):
            for ti, (dst, base) in enumerate(((qT, 0), (kT, QKW), (vT, 2 * QKW))):
                ch0 = base + h * 128
                S.dma('sp', tmp[:], pT[c['OFF_DN_QKV'] + ch0:c['OFF_DN_QKV'] + ch0 + 128, :],
                      reads=[kb.dres['pT']], writes=[tmp.res])
                with kb.nc.allow_non_contiguous_dma("tiny"):
                    S.dma('sp', cwt[:, ti, :], W['dn_conv_w'][:, ch0:ch0 + 128].rearrange("j p -> p j"), writes=[cwt.res])
                ew('dve', lambda e: e.tensor_scalar(dst[:], tmp[:], cwt[:, ti, 3:4], None, ALU.mult), [tmp, cwt], [dst])
                for sft in (1, 2, 3):
                    ew('dve', lambda e, sft=sft: e.scalar_tensor_tensor(
                        dst[:, sft:LP], tmp[:, 0:LP - sft], cwt[:, ti, 3 - sft:4 - sft], dst[:, sft:LP], ALU.mult, ALU.add),
                        [tmp, cwt, dst], [dst])
                ew('act', lambda e: e.activation(dst[:], dst[:], AF.Silu), [dst], [dst])
            for dst, mul in ((qT, DK ** -0.5), (kT, 1.0)):
                for c0, cw in chunks(LP, CH):
                    colsumsq(dst, 1.0, EPS, rn, c0, cw)
                    ew('dve', lambda e, c0=c0, cw=cw: e.scalar_tensor_tensor(
                        dst[:, c0:c0 + cw], dst[:, c0:c0 + cw], float(mul), rn[:, 0:cw], ALU.mult, ALU.mult), [dst, rn], [dst])
            S.dma('sp', gcrow[:], bgT[NH + h:NH + h + 1, :].broadcast_to([128, LP]),
                  reads=[kb.dres['bgT']], writes=[gcrow.res])
            ew('act', lambda e: e.activation(qe[:], gcrow[:], AF.Exp), [gcrow], [qe])
            ew('dve', lambda e: e.tensor_copy(sc[:, 0, :], tm[:, :, h]), [tm], [sc])
            ew('dve', lambda e: e.tensor_copy(sc[:, 1, :], tm[:, :, NH + h]), [tm], [sc])
            ew('dve', lambda e: e.tensor_copy(sc[:, 4, :], qe[:].rearrange("p (t k) -> p t k", k=128)[:, :, 127]), [qe], [sc])
            ew('act', lambda e: e.activation(sc[:, 2, :], sc[:, 1, :], AF.Exp), [sc], [sc])
            ew('dve', lambda e: e.tensor_tensor(sc[:, 2, :], sc[:, 2, :], sc[:, 0, :], ALU.mult), [sc], [sc])
            ew('dve', lambda e: e.tensor_tensor(sc[:, 5, :], gcrow[:].rearrange("p (t k) -> p t k", k=128)[:, :, 127],
                                                sc[:, 1, :], ALU.subtract), [gcrow, sc], [sc])
            ew('act', lambda e: e.activation(sc[:, 3, :], sc[:, 5, :], AF.Exp), [sc], [sc])
            ew('dve', lambda e: e.tensor_tensor(qe[:], qe[:], qT[:], ALU.mult), [qe, qT], [qe])
            ew('dve', lambda e: e.memset(Sst[0][:], 0.0), [], [Sst[0]])
            scur = 0
            for ci in range(NT):
                cs_ = slice(ci * 128, (ci + 1) * 128)
                beta_i, gc_i = sc[:, 0, ci:ci + 1], sc[:, 1, ci:ci + 1]
                beg_i, kds_i, egl = sc[:, 2, ci:ci + 1], sc[:, 3, ci:ci + 1], sc[:, 4, ci:ci + 1]
                pK, pV = nps(), nps()
                ew('pe', lambda e: e.transpose(pK[:, 0:128], kT[:, cs_], ident[:]), [kT, ident], [pK])
                ew('pe', lambda e: e.transpose(pV[:, 0:128], vT[:, cs_], ident[:]), [vT, ident], [pV])
                X0 = Rt[0]
                ew('act', lambda e: e.activation(X0[:, 0:128], pV[:, 0:128], AF.Copy, scale=beta_i), [pV, sc], [X0])
                ew('dve', lambda e: e.tensor_scalar(X0[:, 128:256], pK[:, 0:128], beg_i, None, ALU.mult), [pK, sc], [X0])
                ew('dve', lambda e: e.tensor_scalar(kd[:], pK[:, 0:128], kds_i, None, ALU.mult), [pK, sc], [kd])
                pG, pQ = nps(), nps()
                ew('pe', lambda e: e.matmul(pG[:, 0:128], kT[:, cs_], kT[:, cs_], start=True, stop=True), [kT], [pG])
                ew('pe', lambda e: e.matmul(pQ[:, 0:128], qT[:, cs_], kT[:, cs_], start=True, stop=True), [qT, kT], [pQ])
                ew('dve', lambda e: e.tensor_scalar(E[:], gcrow[:, cs_], gc_i, 0.0, ALU.subtract, ALU.max), [gcrow, sc], [E])
                ew('act', lambda e: e.activation(E[:], E[:], AF.Exp, scale=-1.0), [E], [E])
                ew('dve', lambda e: e.tensor_tensor(Dm[:], E[:], m_incl[:], ALU.mult), [E, m_incl], [Dm])
                ew('dve', lambda e: e.tensor_tensor(Dms[:], E[:], m_strict[:], ALU.mult), [E, m_strict], [Dms])
                A0, B0 = Ak[0], Bk[0]
                ew('dve', lambda e: e.scalar_tensor_tensor(A0[:], pG[:, 0:128], beta_i, Dms[:], ALU.mult, ALU.mult),
                   [pG, sc, Dms], [A0])
                ew('dve', lambda e: e.tensor_tensor(QKm[:], pQ[:, 0:128], Dm[:], ALU.mult), [pQ, Dm], [QKm])
                pB, pT_ = nps(), nps()
                ew('pe', lambda e: e.transpose(pB[:, 0:128], A0[:], ident[:]), [A0, ident], [pB])
                ew('pe', lambda e: e.transpose(pT_[:, 0:128], QKm[:], ident[:]), [QKm, ident], [pT_])
                ew('act', lambda e: e.copy(B0[:], pB[:, 0:128]), [pB], [B0])
                ew('act', lambda e: e.copy(QKT[:], pT_[:, 0:128]), [pT_], [QKT])
                xc = 0
                for k in range(NLV):
                    Ac, Bc = Ak[k % 2], Bk[k % 2]
                    An, Bn = Ak[(k + 1) % 2], Bk[(k + 1) % 2]
                    Xc, Xn = Rt[xc], Rt[1 - xc]
                    pY = nps()
                    ew('pe', lambda e: e.matmul(pY[:, 0:256], Bc[:], Xc[:], start=True, stop=True), [Bc, Xc], [pY])
                    ew('dve', lambda e: e.tensor_tensor(Xn[:], Xc[:], pY[:, 0:256], ALU.subtract if k == 0 else ALU.add),
                       [Xc, pY], [Xn])
                    xc = 1 - xc
                    if k < NLV - 1:
                        pA, pB2 = nps(), nps()
                        ew('pe', lambda e: e.matmul(pA[:, 0:128], Bc[:], Ac[:], start=True, stop=True), [Bc, Ac], [pA])
                        ew('pe', lambda e: e.matmul(pB2[:, 0:128], Ac[:], Bc[:], start=True, stop=True), [Ac, Bc], [pB2])
                        ew('act', lambda e: e.copy(An[:], pA[:, 0:128]), [pA], [An])
                        ew('dve', lambda e: e.tensor_copy(Bn[:], pB2[:, 0:128]), [pB2], [Bn])
                Xf = Rt[xc]
                pW = nps()
                ew('pe', lambda e: e.transpose(pW[:, 0:128], Xf[:, 128:256], ident[:]), [Xf, ident], [pW])
                ew('act', lambda e: e.copy(wT[:], pW[:, 0:128]), [pW], [wT])
                Sc, Sn = Sst[scur], Sst[1 - scur]
                pv_, po_, pS_ = nps(), nps(), nps()
                ew('pe', lambda e: e.matmul(pv_[:, 0:128], wT[:], Sc[:], start=True, stop=True), [wT, Sc], [pv_])
                ew('dve', lambda e: e.tensor_tensor(vn[:], Xf[:, 0:128], pv_[:, 0:128], ALU.subtract), [Xf, pv_], [vn])

                def fo(e):
                    e.matmul(po_[:, 0:128], Sc[:], qe[:, cs_], start=True, stop=False)
                    return e.matmul(po_[:, 0:128], vn[:], QKT[:], start=False, stop=True)
                ew('pe', fo, [Sc, qe, vn, QKT], [po_])
                ew('act', lambda e: e.copy(oT[:, cs_], po_[:, 0:128]), [po_], [oT])
                ew('pe', lambda e: e.matmul(pS_[:, 0:128], kd[:], vn[:], start=True, stop=True), [kd, vn], [pS_])
                ew('dve', lambda e: e.scalar_tensor_tensor(Sn[:], Sc[:], egl, pS_[:, 0:128], ALU.mult, ALU.add),
                   [Sc, sc, pS_], [Sn])
                scur = 1 - scur
            S.dma('sp', tmp[:], pT[c['OFF_DN_Z'] + h * 128:c['OFF_DN_Z'] + (h + 1) * 128, :],
                  reads=[kb.dres['pT']], writes=[tmp.res])
            ew('act', lambda e: e.activation(gcrow[:], tmp[:], AF.Silu), [tmp], [gcrow])
            for ci, (c0, cw) in enumerate(chunks(LP, CH)):
                colsumsq(oT, float(DV), EPS, rn, c0, cw)
                ew('dve', lambda e: e.scalar_tensor_tensor(oT[:, c0:c0 + cw], oT[:, c0:c0 + cw], gout[:, 0:1], rn[:, 0:cw],
                                                           ALU.mult, ALU.mult), [oT, gout, rn], [oT])
                o_ = ob[ci % 2]
                ew('dve', lambda e: e.tensor_tensor(o_[:, 0:cw], oT[:, c0:c0 + cw], gcrow[:, c0:c0 + cw], ALU.mult),
                   [oT, gcrow], [o_])
                S.dma('sp', dnoT[h * 128:(h + 1) * 128, c0:c0 + cw], o_[:, 0:cw], reads=[o_.res], writes=[kb.dres['dnoT']])
    S.barrier()


def merge_phase(kb, name, W, gT, oT, s5hT, dnoT, mergedT):
    c, S = kb.c, kb.S
    D, TG = c['D'], c['TG']
    k1, k2, k3 = c['MLAW'] // 128, c['S5W'] // 128, c['DNW'] // 128
    jobs = []
    for j in range(D // 128):
        jobs.append(dict(j=j, groups=[[(W['mla_w_o'], j * 128, 128, 0, k1)],
                                      [(W['s5_w_glu'], j * 128, 128, k1, k2)],
                                      [(W['s5_w_glu'], D + j * 128, 128, k1, k2)],
                                      [(W['dn_w_o'], j * 128, 128, k1 + k2, k3)]]))

    def alloc(kb, st):
        return dict(g=[kb.sb(st, name + '_g%d' % i, [128, 3, TG], F32) for i in range(2)],
                    t=[kb.sb(st, name + '_t%d' % i, [128, TG], F32) for i in range(3)],
                    o=[kb.sb(st, name + '_o%d' % i, [128, TG], BF16) for i in range(2)], n=[0])

    def ep(kb, job, ps, g, t0, ctx):
        i = ctx['n'][0]
        ctx['n'][0] += 1
        gt_ = ctx['g'][i % 2]
        t1, t2, t3 = ctx['t']
        o = ctx['o'][i % 2]
        j = job['j']
        for b in range(3):
            r0 = b * D + j * 128
            S.dma('sp', gt_[:, b, :], gT[r0:r0 + 128, t0:t0 + TG], reads=[kb.dres['gT']], writes=[gt_.res])
        S.op('act', lambda e: e.activation(t1[:], ps[2][:, 0:TG], AF.Sigmoid), reads=[ps[2].res], writes=[t1.res])
        S.op('dve', lambda e: e.tensor_tensor(t1[:], t1[:], ps[1][:, 0:TG], ALU.mult), reads=[t1.res, ps[1].res], writes=[t1.res])
        S.op('dve', lambda e: e.tensor_tensor(t1[:], t1[:], gt_[:, 1, :], ALU.mult), reads=[t1.res, gt_.res], writes=[t1.res])
        S.op('dve', lambda e: e.tensor_tensor(t2[:], ps[0][:, 0:TG], gt_[:, 0, :], ALU.mult), reads=[ps[0].res, gt_.res], writes=[t2.res])
        S.op('dve', lambda e: e.tensor_tensor(t3[:], ps[3][:, 0:TG], gt_[:, 2, :], ALU.mult), reads=[ps[3].res, gt_.res], writes=[t3.res])
        S.op('dve', lambda e: e.tensor_tensor(t1[:], t1[:], t2[:], ALU.add), reads=[t1.res, t2.res], writes=[t1.res])
        S.op('dve', lambda e: e.tensor_tensor(o[:], t1[:], t3[:], ALU.add), reads=[t1.res, t3.res], writes=[o.res])
        S.dma('sp', mergedT[j * 128:(j + 1) * 128, t0:t0 + TG], o[:], reads=[o.res], writes=[kb.dres['mergedT']])
    gemm_phase(kb, name, [(oT, 'oT'), (s5hT, 's5hT'), (dnoT, 'dnoT')], jobs, ep, SG=c.get('SG1', 4), NB=2,
               nbanks_per_job=4, ep_alloc=alloc)


def plain_gemm(kb, name, X, xname, Wd, K_, N_, out, oname, SG, NB):
    c, S = kb.c, kb.S
    TG = c['TG']
    nk = K_ // 128
    jobs = [dict(j=j, groups=[[(Wd, j * 128, 128, 0, nk)]]) for j in range(N_ // 128)]

    def alloc(kb, st):
        return dict(o=[kb.sb(st, name + '_o%d' % i, [128, TG], F32) for i in range(3)], n=[0])

    def ep(kb, job, ps, g, t0, ctx):
        i = ctx['n'][0]
        ctx['n'][0] += 1
        o = ctx['o'][i % 3]
        if i % 2 == 0:
            S.op('act', lambda e: e.copy(o[:], ps[0][:, 0:TG]), reads=[ps[0].res], writes=[o.res])
        else:
            S.op('dve', lambda e: e.tensor_copy(o[:], ps[0][:, 0:TG]), reads=[ps[0].res], writes=[o.res])
        j = job['j']
        S.dma('sp', out[j * 128:(j + 1) * 128, t0:t0 + TG], o[:], reads=[o.res], writes=[kb.dres[oname]])
    gemm_phase(kb, name, [(X, xname)], jobs, ep, SG=SG, NB=NB, nbanks_per_job=1, ep_alloc=alloc)


PER_LAYER = ['ffn1_w13', 'ffn1_w2', 'w_in', 'mla_q_norm_g', 'mla_kv_norm_g', 'mla_w_uq', 'mla_w_ukv', 'mla_w_o',
             's5_a_re', 's5_a_im', 's5_log_dt', 's5_b_re', 's5_b_im', 's5_c_re', 's5_c_im', 's5_d', 's5_w_glu',
             'dn_conv_w', 'dn_a_log', 'dn_dt_bias', 'dn_out_norm_g', 'dn_w_o', 'w_out', 'ffn2_w13', 'ffn2_w2']


RELAID = ('ffn1_w13', 'ffn1_w2', 'w_in', 'mla_w_o', 's5_w_glu', 'dn_w_o', 'w_out', 'ffn2_w13', 'ffn2_w2')


def relay_weight(W, tiles):
    K_, N_ = W.shape
    nkt = K_ // 128
    if all(m == 128 for _, m in tiles):
        return np.ascontiguousarray(W.reshape(nkt, 128, N_ // 128, 128).transpose(1, 2, 0, 3).reshape(128, nkt * N_))
    out = np.empty((128, nkt * N_), np.float32)
    W3 = W.reshape(nkt, 128, N_)
    for c0, m in tiles:
        out[:, nkt * c0:nkt * (c0 + m)] = W3[:, :, c0:c0 + m].transpose(1, 0, 2).reshape(128, nkt * m)
    return out


def weight_shapes(c):
    D, F = c['D'], c['F']
    return dict(ffn1_w13=[D, 2 * F], ffn1_w2=[F, D], w_in=[D, c['NIN']], mla_q_norm_g=[c['QL']],
                mla_kv_norm_g=[c['KVL']], mla_w_uq=[c['QL'], c['H'] * c['QK']],
                mla_w_ukv=[c['KVL'], c['H'] * (c['NOPE'] + c['VD'])], mla_w_o=[c['MLAW'], D],
                s5_a_re=[c['S5NG'], c['S5P']], s5_a_im=[c['S5NG'], c['S5P']], s5_log_dt=[c['S5NG']],
                s5_b_re=[c['S5NG'], c['S5P'], c['S5G']], s5_b_im=[c['S5NG'], c['S5P'], c['S5G']],
                s5_c_re=[c['S5NG'], c['S5G'], c['S5P']], s5_c_im=[c['S5NG'], c['S5G'], c['S5P']],
                s5_d=[c['S5NG'], c['S5G']], s5_w_glu=[c['S5W'], 2 * D],
                dn_conv_w=[c['DNCONV'], 2 * c['DNQK'] + c['DNW']], dn_a_log=[c['DNH']], dn_dt_bias=[c['DNH']],
                dn_out_norm_g=[c['DV']], dn_w_o=[c['DNW'], D], w_out=[D, D], ffn2_w13=[D, 2 * F], ffn2_w2=[F, D])


def build_program(cfg, debug=()):
    c = derive(cfg)
    kb = K(c)
    kb.dbg = debug
    S = kb.S
    D, F, LP, DEPTH = c['D'], c['F'], c['LP'], c['DEPTH']
    xT = kb.dram('xT', [D, LP], F32, kind="ExternalInput")
    sg = kb.dram('sandwich_g', [DEPTH * 6, D], F32, kind="ExternalInput")
    Wl = []
    shp = weight_shapes(c)
    for l in range(DEPTH):
        wd = {}
        for n in PER_LAYER:
            if n in RELAID:
                K_, N_ = shp[n]
                wd[n] = kb.dram('%s_%d' % (n, l), [128, (K_ // 128) * N_], F32, kind="ExternalInput")
                kb.relaid.add(id(wd[n]))
            else:
                wd[n] = kb.dram('%s_%d' % (n, l), shp[n], F32, kind="ExternalInput")
        Wl.append(wd)
    outT = kb.dram('outT', [D, LP], F32, kind="ExternalOutput")

    def scratch(name, shape, dt):
        return kb.dram(name, shape, dt, kind=("ExternalOutput" if name in debug else "Internal"))
    hT = scratch('hT', [D, LP], F32)
    nT = scratch('nT', [D, LP], BF16)
    hidT = scratch('hidT', [F, LP], BF16)
    fT = scratch('fT', [D, LP], F32)
    pT = scratch('pT', [c['OFF_GATE'], LP], F32)
    gT = scratch('gT', [3 * D, LP], F32)
    scr = dict(cqnT=scratch('cqnT', [c['QL'], LP], BF16), ckvnT=scratch('ckvnT', [c['KVL'], LP], BF16),
               qnT=scratch('qnT', [c['H'] * c['NOPE'], LP], BF16), qrT=scratch('qrT', [c['H'] * c['ROPE'], LP], BF16),
               knT=scratch('knT', [c['H'] * c['NOPE'], LP], BF16), krT=scratch('krT', [c['ROPE'], LP], BF16),
               Vtok=scratch('Vtok', [LP, c['MLAW']], BF16), oT=scratch('oT', [c['MLAW'], LP], BF16),
               csT=scratch('csT', [c['ROPE'] // 2, 2, LP], F32))
    s5hT = scratch('s5hT', [c['S5W'], LP], BF16)
    bgT = scratch('bgT', [2 * c['DNH'], LP], F32)
    dnoT = scratch('dnoT', [c['DNW'], LP], BF16)
    mergedT = scratch('mergedT', [D, LP], BF16)

    rope_tables(kb, scr['csT'])
    with ExitStack() as st:
        G = [load_gain(kb, st, 'sg%d' % i, sg[i, :], D) for i in range(DEPTH * 6)]
        hsrc = (xT, 'xT', 0)
        norm_phase(kb, 'n_in', D, h=hsrc, g_pre=G[0], n_out=(nT, 'nT', 0))
        for l in range(DEPTH):
            W = Wl[l]
            g = G[6 * l:6 * l + 6]
            last = (l == DEPTH - 1)
            ffn_phase(kb, 'f1_%d' % l, nT, W['ffn1_w13'], W['ffn1_w2'], hidT, fT)
            norm_phase(kb, 'n1_%d' % l, D, h=hsrc, y=(fT, 'fT', 0), coef=0.5, g_post=g[1], g_pre=g[2],
                       n_out=(nT, 'nT', 0), h_out=(hT, 'hT', 0))
            hsrc = (hT, 'hT', 0)
            inproj_phase(kb, 'ip_%d' % l, nT, W['w_in'], pT, gT)
            mla_phase(kb, 'mla_%d' % l, l, W, pT, scr)
            s5_phase(kb, 's5_%d' % l, l, W, pT, s5hT)
            dn_phase(kb, 'dn_%d' % l, l, W, pT, bgT, dnoT)
            merge_phase(kb, 'mg_%d' % l, W, gT, scr['oT'], s5hT, dnoT, mergedT)
            plain_gemm(kb, 'wo_%d' % l, mergedT, 'mergedT', W['w_out'], D, D, fT, 'fT', SG=c.get('SG1', 4), NB=4)
            norm_phase(kb, 'n3_%d' % l, D, h=hsrc, y=(fT, 'fT', 0), coef=1.0, g_post=g[3], g_pre=g[4],
                       n_out=(nT, 'nT', 0), h_out=(hT, 'hT', 0))
            ffn_phase(kb, 'f2_%d' % l, nT, W['ffn2_w13'], W['ffn2_w2'], hidT, fT)
            if last:
                norm_phase(kb, 'n5_%d' % l, D, h=hsrc, y=(fT, 'fT', 0), coef=0.5, g_post=g[5],
                           h_out=(outT, 'outT', 0))
            else:
                norm_phase(kb, 'n5_%d' % l, D, h=hsrc, y=(fT, 'fT', 0), coef=0.5, g_post=g[5], g_pre=G[6 * (l + 1)],
                           n_out=(nT, 'nT', 0), h_out=(hT, 'hT', 0))
    S.barrier()
    S.finish()
    return kb


def make_in_maps(c, inputs, ncores):
    c = derive(c)
    D, L, LP = c['D'], c['L'], c['LP']
    x = np.asarray(inputs['x'], dtype=np.float32)
    meta = np.asarray(inputs['meta_tokens'], dtype=np.float32)
    shared = {'sandwich_g': np.ascontiguousarray(np.asarray(inputs['sandwich_g'], np.float32).reshape(-1, D))}
    for l in range(c['DEPTH']):
        for n in PER_LAYER:
            w = np.asarray(inputs[n], np.float32)[l]
            if n in RELAID:
                tiles = inproj_tiles(c) if n == 'w_in' else [(i, 128) for i in range(0, w.shape[1], 128)]
                shared['%s_%d' % (n, l)] = relay_weight(w, tiles)
            else:
                shared['%s_%d' % (n, l)] = np.ascontiguousarray(w)
    maps = []
    for b in range(ncores):
        xT = np.zeros((D, LP), np.float32)
        xT[:, :c['NMETA']] = meta.T
        xT[:, c['NMETA']:L] = x[b].T
        m = dict(shared)
        m['xT'] = xT
        maps.append(m)
    return maps


_CACHE = {}


def kernel(**inputs):
    cfg = full_cfg()
    c = derive(cfg)
    B = np.asarray(inputs['x']).shape[0]
    if 'kb' not in _CACHE:
        _CACHE['kb'] = build_program(cfg)
    kb = _CACHE['kb']
    maps = make_in_maps(cfg, inputs, B)
    res = run_bass_kernel_spmd(kb.nc, maps, core_ids=list(range(B)))
    out = np.stack([np.ascontiguousarray(res.results[b]['outT'][:, c['NMETA']:c['L']].T) for b in range(B)], axis=0)
    return out.astype(np.float32)
```

```python
import math
from contextlib import ExitStack
import numpy as np
import concourse.bass as bass
import concourse.mybir as mybir
from concourse.bass_utils import run_bass_kernel_spmd

F32 = mybir.dt.float32
BF16 = mybir.dt.bfloat16
AF = mybir.ActivationFunctionType
ALU = mybir.AluOpType
EPS = 1e-6


def full_cfg():
    return dict(D=4096, SEQ=4096, DEPTH=2, NMETA=16, CHUNK=64, F=11008,
                H=16, QL=1024, KVL=512, NOPE=128, ROPE=64, VD=128,
                S5W=1024, S5G=16, S5P=64, DNH=8, DK=128, DV=128, DNCONV=4,
                TG=384, THETA=10000.0)


def derive(c):
    c = dict(c)
    c['L'] = c['SEQ'] + c['NMETA']
    c['LP'] = -(-c['L'] // c['TG']) * c['TG']
    c['NG'] = c['LP'] // c['TG']
    c['NT'] = c['LP'] // 128
    c['QK'] = c['NOPE'] + c['ROPE']
    c['MLAW'] = c['H'] * c['VD']
    c['S5NG'] = c['S5W'] // c['S5G']
    c['DNQK'] = c['DNH'] * c['DK']
    c['DNW'] = c['DNH'] * c['DV']
    o = 0
    c['OFF_Q'] = o; o += c['QL']
    c['OFF_KV'] = o; o += c['KVL']
    c['OFF_KR'] = o; o += c['ROPE']
    c['OFF_S5'] = o; o += c['S5W']
    c['OFF_DN_QKV'] = o; o += 2 * c['DNQK'] + c['DNW']
    c['OFF_DN_Z'] = o; o += c['DNW']
    c['OFF_DN_A'] = o; o += c['DNH']
    c['OFF_DN_B'] = o; o += c['DNH']
    c['OFF_GATE'] = o; o += 3 * c['D']
    c['NIN'] = o
    return c


class Res:
    __slots__ = ('w', 'r', 'name')

    def __init__(self, name=''):
        self.w = {}
        self.r = {}
        self.name = name


class Sched:
    NDS = 24

    def __init__(self, nc, stack):
        self.nc = nc
        self.eng = {'pe': nc.tensor, 'act': nc.scalar, 'dve': nc.vector,
                    'pool': nc.gpsimd, 'sp': nc.sync}
        self.sem = {}
        for k in ('pe', 'act', 'dve', 'pool'):
            self.sem[k] = stack.enter_context(nc.semaphore('s_' + k))
        for i in range(self.NDS):
            self.sem[('d', i)] = stack.enter_context(nc.semaphore('s_d%d' % i))
        self.cnt = {k: 0 for k in self.sem}
        self.waited = {e: {} for e in self.eng}
        self.nd = 0
        self.nins = 0

    def _hazards(self, eng, reads, writes, waits, is_dma=False):
        def need(tok):
            if tok is None:
                return
            k, v = tok
            if eng == 'pe' and k == 'pe':
                return
            if self.waited[eng].get(k, 0) >= v:
                return
            if waits.get(k, 0) < v:
                waits[k] = v
        for r in reads:
            for k, v in r.w.items():
                need((k, v))
        for w in writes:
            if w.r:
                for k, v in w.r.items():
                    need((k, v))
                for k, v in w.w.items():
                    need((k, v))
            else:
                for k, v in w.w.items():
                    if k == eng or (is_dma and isinstance(k, tuple)):
                        continue
                    need((k, v))

    def _emit_waits(self, eng, waits):
        E = self.eng[eng]
        for k, v in waits.items():
            E.wait_ge(self.sem[k], v)
            self.waited[eng][k] = v
            self.nins += 1

    def _commit(self, tok, reads, writes):
        k, v = tok
        for r in reads:
            if r.r.get(k, 0) < v:
                r.r[k] = v
        for w in writes:
            if w.r:
                w.w = {}
                w.r = {}
            if w.w.get(k, 0) < v:
                w.w[k] = v

    def op(self, eng, fn, reads=(), writes=()):
        waits = {}
        self._hazards(eng, reads, writes, waits)
        self._emit_waits(eng, waits)
        ins = fn(self.eng[eng])
        self.cnt[eng] += 1
        ins.then_inc(self.sem[eng], 1)
        self.nins += 1
        self._commit((eng, self.cnt[eng]), reads, writes)

    def dma(self, q, out, in_, reads=(), writes=()):
        i = self.nd % self.NDS
        self.nd += 1
        key = ('d', i)
        waits = {}
        self._hazards(q, reads, writes, waits, is_dma=True)
        prev = self.cnt[key]
        if prev > 0 and self.waited[q].get(key, 0) < prev and waits.get(key, 0) < prev:
            waits[key] = prev
        self._emit_waits(q, waits)
        ins = self.eng[q].dma_start(out=out, in_=in_)
        self.cnt[key] += 16
        ins.then_inc(self.sem[key], 16)
        self.nins += 1
        self._commit((key, self.cnt[key]), reads, writes)

    def barrier(self):
        for e in self.eng:
            for k, v in self.cnt.items():
                if v > 0 and self.waited[e].get(k, 0) < v and not (e == k):
                    self.eng[e].wait_ge(self.sem[k], v)
                    self.waited[e][k] = v
                    self.nins += 1

    def finish(self):
        E = self.eng['sp']
        for i in range(self.NDS):
            key = ('d', i)
            if self.cnt[key] > 0:
                E.wait_ge(self.sem[key], self.cnt[key])
        for k in ('pe', 'act', 'dve', 'pool'):
            if self.cnt[k] > 0:
                E.wait_ge(self.sem[k], self.cnt[k])


class T:
    def __init__(self, ap_handle, name=''):
        self.t = ap_handle
        self.res = Res(name)

    def __getitem__(self, k):
        return self.t[k]


class K:
    def __init__(self, cfg):
        self.c = cfg
        self.nc = bass.Bass("TRN2", target_bir_lowering=False)
        self.root = ExitStack()
        self.S = Sched(self.nc, self.root)
        self.dres = {}
        self.relaid = set()

    def dram(self, name, shape, dt, kind="Internal"):
        t = self.nc.dram_tensor(name, list(shape), dt, kind=kind).ap()
        self.dres[name] = Res(name)
        return t

    def dump(self, name, t, shape, dt=F32):
        if name not in getattr(self, 'dbg', ()):
            return
        d = self.nc.dram_tensor('dbg_' + name, list(shape), dt, kind="ExternalOutput").ap()
        self.dres['dbg_' + name] = Res(name)
        self.S.dma('sp', d, t[:], reads=[t.res], writes=[self.dres['dbg_' + name]])

    def sb(self, stack, name, shape, dt):
        return T(stack.enter_context(self.nc.sbuf_tensor(name, list(shape), dt)), name)

    def ps(self, stack, name, shape, dt=F32):
        return T(stack.enter_context(self.nc.psum_tensor(name, list(shape), dt)), name)


def rstd_from_sumsq(kb, out_ap, out_res, ss_ap, ss_res, D, eps=EPS):
    S = kb.S
    S.op('dve', lambda e: e.tensor_scalar(out_ap, ss_ap, 1.0 / D, eps, ALU.mult, ALU.add),
         reads=[ss_res], writes=[out_res])
    S.op('act', lambda e: e.activation(out_ap, out_ap, AF.Sqrt), reads=[out_res], writes=[out_res])
    S.op('dve', lambda e: e.reciprocal(out_ap, out_ap), reads=[out_res], writes=[out_res])


def chunks(n, m):
    return [(i, min(m, n - i)) for i in range(0, n, m)]


def gemm_phase(kb, name, xsrcs, jobs, epilogue, SG, NB, nbanks_per_job, ep_alloc=None):
    c, S, nc = kb.c, kb.S, kb.nc
    TG, NG, LP = c['TG'], c['NG'], c['LP']
    xk = []
    for ap, rn in xsrcs:
        Ki = ap.shape[0]
        for r0, rows in chunks(Ki, 128):
            xk.append((ap, rn, r0, rows))
    nxk = len(xk)
    for j in jobs:
        segs = []
        for gi, grp in enumerate(j['groups']):
            for (W, c0, M, xkt0, nkt) in grp:
                segs.append((gi, W, c0, M, xkt0, nkt))
        j['segs'] = segs
        j['wcols'] = sum(s[3] * s[5] for s in segs)
    maxw = max(sum(j['wcols'] for j in jobs[b:b + NB]) for b in range(0, len(jobs), NB))
    with ExitStack() as st:
        XS = kb.sb(st, name + '_xs', [128, nxk, SG * TG], BF16)
        WB = [kb.sb(st, name + '_wb%d' % i, [128, maxw], BF16) for i in range(2)]
        nps = 8 // nbanks_per_job
        nps = min(nps, 4)
        PS = [[kb.ps(st, name + '_ps%d_%d' % (i, b), [128, 512]) for b in range(nbanks_per_job)]
              for i in range(nps)]
        ctx = ep_alloc(kb, st) if ep_alloc else None
        pscnt = 0
        wbcnt = 0
        for sg0 in range(0, NG, SG):
            ngs = min(SG, NG - sg0)
            for i, (ap, rn, r0, rows) in enumerate(xk):
                S.dma('sp', XS[0:rows, i, 0:ngs * TG], ap[r0:r0 + rows, sg0 * TG:(sg0 + ngs) * TG],
                      reads=[kb.dres[rn]], writes=[XS.res])
            for b0 in range(0, len(jobs), NB):
                blk = jobs[b0:b0 + NB]
                wb = WB[wbcnt % 2]
                wbcnt += 1
                off = 0
                for j in blk:
                    j['woff'] = []
                    for (gi, W, c0, M, xkt0, nkt) in j['segs']:
                        j['woff'].append(off)
                        if id(W) in kb.relaid:
                            S.dma('pool', wb[:, off:off + nkt * M], W[:, nkt * c0:nkt * c0 + nkt * M],
                                  reads=[], writes=[wb.res])
                        else:
                            dst = wb[:, off:off + nkt * M].rearrange("p (k m) -> p k m", m=M)
                            for k0, kn in chunks(nkt, 8):
                                src = W[k0 * 128:(k0 + kn) * 128, c0:c0 + M].rearrange("(k p) m -> p k m", p=128)
                                S.dma('pool', dst[:, k0:k0 + kn, :], src, reads=[], writes=[wb.res])
                        off += nkt * M
                for j in blk:
                    for g in range(ngs):
                        ps = PS[pscnt % nps]
                        pscnt += 1
                        ng = len(j['groups'])
                        for gi in range(ng):
                            segs = [(s, o) for s, o in zip(j['segs'], j['woff']) if s[0] == gi]
                            mm = []
                            for (s, o) in segs:
                                (_, W, c0, M, xkt0, nkt) = s
                                for kt in range(nkt):
                                    rows = xk[xkt0 + kt][3]
                                    mm.append((o + kt * M, M, xkt0 + kt, rows))

                            def fn(pe, mm=mm, ps=ps, gi=gi, g=g, wb=wb):
                                ins = None
                                for idx, (wo, M, xi, rows) in enumerate(mm):
                                    ins = pe.matmul(ps[gi][0:M, 0:TG], wb[0:rows, wo:wo + M],
                                                    XS[0:rows, xi, g * TG:(g + 1) * TG],
                                                    start=(idx == 0), stop=(idx == len(mm) - 1))
                                return ins
                            S.op('pe', fn, reads=[wb.res, XS.res], writes=[ps[gi].res])
                        epilogue(kb, j, ps, sg0 + g, (sg0 + g) * TG, ctx)
    S.barrier()


def norm_phase(kb, name, D, h, y=None, coef=1.0, g_post=None, g_pre=None,
               n_out=None, h_out=None):
    c, S = kb.c, kb.S
    LP = c['LP']
    NW = c['TG'] // 2
    assert D % 128 == 0 and LP % NW == 0
    nkt = D // 128
    with ExitStack() as st:
        Hh = [kb.sb(st, name + '_h%d' % i, [128, nkt, NW], F32) for i in range(2)]
        Yy = [kb.sb(st, name + '_y%d' % i, [128, nkt, NW], F32) for i in range(2)] if y is not None else None
        SQs = [kb.sb(st, name + '_sq%d' % i, [128, nkt, NW], F32) for i in range(2)]
        Nns = [kb.sb(st, name + '_n%d' % i, [128, nkt, NW], BF16) for i in range(2)] if n_out is not None else None
        ones = kb.sb(st, name + '_ones', [128, 128], F32)
        rss = [kb.sb(st, name + '_rs%d' % i, [128, NW], F32) for i in range(2)]
        pss = [kb.ps(st, name + '_ps%d' % i, [128, 512]) for i in range(2)]
        S.op('dve', lambda e: e.memset(ones[:], 1.0), writes=[ones.res])

        def stats(src, SQ, ps, rs):
            for kt in range(nkt):
                S.op('act', lambda e, kt=kt: e.activation(SQ[:, kt, :], src[:, kt, :], AF.Square),
                     reads=[src.res], writes=[SQ.res])

            def fn(pe):
                ins = None
                for kt in range(nkt):
                    ins = pe.matmul(ps[:, 0:NW], ones[:], SQ[:, kt, :], start=(kt == 0), stop=(kt == nkt - 1))
                return ins
            S.op('pe', fn, reads=[ones.res, SQ.res], writes=[ps.res])
            rstd_from_sumsq(kb, rs[:], rs.res, ps[:, 0:NW], ps.res, D)

        def view(t, t0):
            ap, rn, r0 = t
            return ap[r0:r0 + D, t0:t0 + NW].rearrange("(k p) t -> p k t", p=128), kb.dres[rn]

        for g in range(LP // NW):
            t0 = g * NW
            bi = g % 2
            H_, SQ, ps, rs = Hh[bi], SQs[bi], pss[bi], rss[bi]
            v, r = view(h, t0)
            S.dma('sp', H_[:], v, reads=[r], writes=[H_.res])
            if y is not None:
                Y = Yy[bi]
                v, r = view(y, t0)
                S.dma('sp', Y[:], v, reads=[r], writes=[Y.res])
                stats(Y, SQ, ps, rs)
                for kt in range(nkt):
                    S.op('dve', lambda e, kt=kt: e.scalar_tensor_tensor(
                        SQ[:, kt, :], Y[:, kt, :], g_post[:, kt:kt + 1], rs[:], ALU.mult, ALU.mult),
                        reads=[Y.res, rs.res], writes=[SQ.res])
                    S.op('dve', lambda e, kt=kt: e.scalar_tensor_tensor(
                        H_[:, kt, :], SQ[:, kt, :], float(coef), H_[:, kt, :], ALU.mult, ALU.add),
                        reads=[SQ.res, H_.res], writes=[H_.res])
                if h_out is not None:
                    v, r = view(h_out, t0)
                    S.dma('sp', v, H_[:], reads=[H_.res], writes=[r])
            if n_out is not None:
                Nn = Nns[bi]
                stats(H_, SQ, ps, rs)
                for kt in range(nkt):
                    S.op('dve', lambda e, kt=kt: e.scalar_tensor_tensor(
                        Nn[:, kt, :], H_[:, kt, :], g_pre[:, kt:kt + 1], rs[:], ALU.mult, ALU.mult),
                        reads=[H_.res, rs.res], writes=[Nn.res])
                v, r = view(n_out, t0)
                S.dma('sp', v, Nn[:], reads=[Nn.res], writes=[r])
    S.barrier()


def ffn_phase(kb, name, nT, w13, w2, hidT, fT):
    c, S = kb.c, kb.S
    D, F, TG = c['D'], c['F'], c['TG']
    nkt = D // 128
    jobs = []
    for j in range(F // 128):
        jobs.append(dict(groups=[[(w13, j * 128, 128, 0, nkt)], [(w13, F + j * 128, 128, 0, nkt)]], j=j))

    def alloc1(kb, st):
        return dict(sa=[kb.sb(st, name + '_sa%d' % i, [128, TG], F32) for i in range(2)],
                    ho=[kb.sb(st, name + '_ho%d' % i, [128, TG], BF16) for i in range(3)], n=[0])

    def ep1(kb, job, ps, g, t0, ctx):
        i = ctx['n'][0]
        ctx['n'][0] += 1
        sa = ctx['sa'][i % 2]
        ho = ctx['ho'][i % 3]
        S.op('act', lambda e: e.activation(sa[:], ps[0][:, 0:TG], AF.Silu), reads=[ps[0].res], writes=[sa.res])
        S.op('dve', lambda e: e.tensor_tensor(ho[:], sa[:], ps[1][:, 0:TG], ALU.mult),
             reads=[sa.res, ps[1].res], writes=[ho.res])
        j = job['j']
        S.dma('sp', hidT[j * 128:(j + 1) * 128, t0:t0 + TG], ho[:], reads=[ho.res], writes=[kb.dres['hidT']])
    gemm_phase(kb, name + 'a', [(nT, 'nT')], jobs, ep1, SG=c.get('SG1', 4), NB=c.get('NB1', 2),
               nbanks_per_job=2, ep_alloc=alloc1)

    fkt = F // 128
    jobs2 = [dict(groups=[[(w2, j * 128, 128, 0, fkt)]], j=j) for j in range(D // 128)]

    def alloc2(kb, st):
        return dict(o=[kb.sb(st, name + '_fo%d' % i, [128, TG], F32) for i in range(3)], n=[0])

    def ep2(kb, job, ps, g, t0, ctx):
        i = ctx['n'][0]
        ctx['n'][0] += 1
        o = ctx['o'][i % 3]
        eng = 'act' if i % 2 == 0 else 'dve'
        if eng == 'act':
            S.op('act', lambda e: e.copy(o[:], ps[0][:, 0:TG]), reads=[ps[0].res], writes=[o.res])
        else:
            S.op('dve', lambda e: e.tensor_copy(o[:], ps[0][:, 0:TG]), reads=[ps[0].res], writes=[o.res])
        j = job['j']
        S.dma('sp', fT[j * 128:(j + 1) * 128, t0:t0 + TG], o[:], reads=[o.res], writes=[kb.dres['fT']])
    gemm_phase(kb, name + 'b', [(hidT, 'hidT')], jobs2, ep2, SG=c.get('SG2', 2), NB=1,
               nbanks_per_job=1, ep_alloc=alloc2)


def load_gain(kb, st, name, src_row_ap, D):
    nkt = D // 128
    t = kb.sb(st, name, [128, nkt], F32)
    v = src_row_ap.rearrange("(k p) -> p k", p=128)
    for k0, kn in chunks(nkt, 8):
        with kb.nc.allow_non_contiguous_dma("gain vector transpose load"):
            kb.S.dma('sp', t[:, k0:k0 + kn], v[:, k0:k0 + kn], writes=[t.res])
    return t


def rope_tables(kb, csT):
    c, S = kb.c, kb.S
    LP = c['LP']
    R2 = c['ROPE'] // 2
    I32 = mybir.dt.int32
    with ExitStack() as st:
        pi_ = kb.sb(st, 'rt_pi', [R2, LP], I32)
        pf = kb.sb(st, 'rt_pf', [R2, LP], F32)
        ii = kb.sb(st, 'rt_ii', [R2, 1], I32)
        inv = kb.sb(st, 'rt_inv', [R2, 1], F32)
        ang = kb.sb(st, 'rt_ang', [R2, LP], F32)
        cs = kb.sb(st, 'rt_cs', [R2, 2, LP], F32)
        ki = kb.sb(st, 'rt_ki', [R2, LP], I32)
        kf = kb.sb(st, 'rt_kf', [R2, LP], F32)
        S.op('pool', lambda e: e.iota(pi_[:], [[1, LP]], 0, 0), writes=[pi_.res])
        S.op('pool', lambda e: e.iota(ii[:], [[1, 1]], 0, 1), writes=[ii.res])
        S.op('dve', lambda e: e.tensor_copy(pf[:], pi_[:]), reads=[pi_.res], writes=[pf.res])
        S.op('dve', lambda e: e.tensor_copy(inv[:], ii[:]), reads=[ii.res], writes=[inv.res])
        S.op('act', lambda e: e.activation(inv[:], inv[:], AF.Exp, scale=-math.log(c['THETA']) / R2),
             reads=[inv.res], writes=[inv.res])
        S.op('dve', lambda e: e.tensor_scalar(ang[:], pf[:], inv[:, 0:1], None, ALU.mult),
             reads=[pf.res, inv.res], writes=[ang.res])
        for j, sh in enumerate((0.5 * math.pi, 0.0)):
            S.op('dve', lambda e, j=j, sh=sh: e.tensor_scalar(cs[:, j, :], ang[:], sh, None, ALU.add),
                 reads=[ang.res], writes=[cs.res])
            S.op('dve', lambda e, j=j: e.tensor_scalar(kf[:], cs[:, j, :], 1.0 / (2 * math.pi), None, ALU.mult),
                 reads=[cs.res], writes=[kf.res])
            S.op('dve', lambda e: e.tensor_copy(ki[:], kf[:]), reads=[kf.res], writes=[ki.res])
            S.op('dve', lambda e: e.tensor_copy(kf[:], ki[:]), reads=[ki.res], writes=[kf.res])
            S.op('dve', lambda e, j=j: e.scalar_tensor_tensor(cs[:, j, :], kf[:], -2 * math.pi, cs[:, j, :],
                                                              ALU.mult, ALU.add),
                 reads=[kf.res, cs.res], writes=[cs.res])
            S.op('dve', lambda e, j=j: e.tensor_scalar(cs[:, j, :], cs[:, j, :], -3.14159, 3.14159, ALU.max, ALU.min),
                 reads=[cs.res], writes=[cs.res])
            S.op('act', lambda e, j=j: e.activation(cs[:, j, :], cs[:, j, :], AF.Sin), reads=[cs.res], writes=[cs.res])
        S.dma('sp', csT, cs[:], reads=[cs.res], writes=[kb.dres['csT']])
    S.barrier()


def inproj_tiles(c):
    bounds = [c['OFF_Q'], c['OFF_KV'], c['OFF_KR'], c['OFF_S5'], c['OFF_DN_QKV'], c['OFF_DN_Z'],
              c['OFF_DN_A'], c['OFF_GATE'], c['NIN']]
    tiles = []
    for a, b in zip(bounds[:-1], bounds[1:]):
        for c0, m in chunks(b - a, 128):
            tiles.append((a + c0, m))
    return tiles


def inproj_phase(kb, name, nT, w_in, pT, gT):
    c, S = kb.c, kb.S
    D, TG = c['D'], c['TG']
    nkt = D // 128
    jobs = []
    for c0, m in inproj_tiles(c):
        jobs.append(dict(groups=[[(w_in, c0, m, 0, nkt)]], c0=c0, m=m, gate=(c0 >= c['OFF_GATE'])))

    def alloc(kb, st):
        return dict(o=[kb.sb(st, name + '_o%d' % i, [128, TG], F32) for i in range(4)], n=[0])

    def ep(kb, job, ps, g, t0, ctx):
        i = ctx['n'][0]
        ctx['n'][0] += 1
        o = ctx['o'][i % 4]
        m, c0 = job['m'], job['c0']
        if job['gate']:
            S.op('act', lambda e: e.activation(o[0:m, :], ps[0][0:m, 0:TG], AF.Sigmoid),
                 reads=[ps[0].res], writes=[o.res])
        else:
            S.op('dve', lambda e: e.tensor_copy(o[0:m, :], ps[0][0:m, 0:TG]), reads=[ps[0].res], writes=[o.res])
        if job['gate']:
            r0 = c0 - c['OFF_GATE']
            S.dma('sp', gT[r0:r0 + m, t0:t0 + TG], o[0:m, :], reads=[o.res], writes=[kb.dres['gT']])
        else:
            S.dma('sp', pT[c0:c0 + m, t0:t0 + TG], o[0:m, :], reads=[o.res], writes=[kb.dres['pT']])
    gemm_phase(kb, name, [(nT, 'nT')], jobs, ep, SG=c.get('SG1', 4), NB=4, nbanks_per_job=1, ep_alloc=alloc)


def mla_phase(kb, name, l, W, pT, scr):
    c, S = kb.c, kb.S
    TG, NG, LP, NT = c['TG'], c['NG'], c['LP'], c['NT']
    H, QL, KVL, NOPE, ROPE, VD = c['H'], c['QL'], c['KVL'], c['NOPE'], c['ROPE'], c['VD']
    R2 = ROPE // 2
    QKD = NOPE + ROPE
    scale = QKD ** -0.5
    cqnT, ckvnT, qnT, qrT, knT, krT, Vtok, oT, csT = (scr[k] for k in
                                                       ('cqnT', 'ckvnT', 'qnT', 'qrT', 'knT', 'krT', 'Vtok', 'oT', 'csT'))
    with ExitStack() as st:
        gq = load_gain(kb, st, name + '_gq', W['mla_q_norm_g'], QL)
        gkv = load_gain(kb, st, name + '_gkv', W['mla_kv_norm_g'], KVL)
        norm_phase(kb, name + '_nq', QL, h=(pT, 'pT', c['OFF_Q']), g_pre=gq, n_out=(cqnT, 'cqnT', 0))
        norm_phase(kb, name + '_nkv', KVL, h=(pT, 'pT', c['OFF_KV']), g_pre=gkv, n_out=(ckvnT, 'ckvnT', 0))
    S.barrier()
    with ExitStack() as st:
        cs = kb.sb(st, name + '_cs', [R2, 2, LP], F32)
        x12 = kb.sb(st, name + '_x12', [R2, 2, LP], F32)
        t1 = kb.sb(st, name + '_t1', [R2, LP], F32)
        t2 = kb.sb(st, name + '_t2', [R2, LP], F32)
        kr = kb.sb(st, name + '_kr', [R2, 2, LP], BF16)
        S.dma('sp', cs[:], csT, reads=[kb.dres['csT']], writes=[cs.res])
        for j in range(2):
            S.dma('sp', x12[:, j, :], pT[c['OFF_KR'] + j * R2:c['OFF_KR'] + (j + 1) * R2, :],
                  reads=[kb.dres['pT']], writes=[x12.res])
        rd = [x12.res, cs.res]
        S.op('dve', lambda e: e.tensor_tensor(t1[:], x12[:, 0, :], cs[:, 0, :], ALU.mult), reads=rd, writes=[t1.res])
        S.op('dve', lambda e: e.tensor_tensor(t2[:], x12[:, 1, :], cs[:, 1, :], ALU.mult), reads=rd, writes=[t2.res])
        S.op('dve', lambda e: e.tensor_tensor(kr[:, 0, :], t1[:], t2[:], ALU.subtract),
             reads=[t1.res, t2.res], writes=[kr.res])
        S.op('dve', lambda e: e.tensor_tensor(t1[:], x12[:, 1, :], cs[:, 0, :], ALU.mult),
             reads=rd + [kr.res], writes=[t1.res])
        S.op('dve', lambda e: e.tensor_tensor(t2[:], x12[:, 0, :], cs[:, 1, :], ALU.mult),
             reads=rd + [kr.res], writes=[t2.res])
        S.op('dve', lambda e: e.tensor_tensor(kr[:, 1, :], t1[:], t2[:], ALU.add),
             reads=[t1.res, t2.res], writes=[kr.res])
        for j in range(2):
            S.dma('sp', krT[j * R2:(j + 1) * R2, :], kr[:, j, :], reads=[kr.res], writes=[kb.dres['krT']])
    S.barrier()
    nq = QL // 128
    jobs = []
    for h in range(H):
        jobs.append(dict(groups=[[(W['mla_w_uq'], h * QKD, NOPE, 0, nq)]], h=h, kind='n'))
        jobs.append(dict(groups=[[(W['mla_w_uq'], h * QKD + NOPE, R2, 0, nq)],
                                 [(W['mla_w_uq'], h * QKD + NOPE + R2, R2, 0, nq)]], h=h, kind='r'))

    def alloc_q(kb, st):
        cs = kb.sb(st, name + '_qcs', [R2, 2, LP], F32)
        S.dma('sp', cs[:], csT, reads=[kb.dres['csT']], writes=[cs.res])
        S.op('dve', lambda e: e.tensor_scalar(cs[:], cs[:], scale, None, ALU.mult), reads=[cs.res], writes=[cs.res])
        return dict(cs=cs, o=[kb.sb(st, name + '_qo%d' % i, [128, TG], BF16) for i in range(3)],
                    r=[kb.sb(st, name + '_qr%d' % i, [R2, 2, TG], BF16) for i in range(2)],
                    t=[kb.sb(st, name + '_qt%d' % i, [R2, TG], F32) for i in range(2)], n=[0])

    def ep_q(kb, job, ps, g, t0, ctx):
        i = ctx['n'][0]
        ctx['n'][0] += 1
        h = job['h']
        if job['kind'] == 'n':
            o = ctx['o'][i % 3]
            S.op('act', lambda e: e.activation(o[:], ps[0][:, 0:TG], AF.Copy, scale=scale),
                 reads=[ps[0].res], writes=[o.res])
            S.dma('sp', qnT[h * NOPE:(h + 1) * NOPE, t0:t0 + TG], o[:], reads=[o.res], writes=[kb.dres['qnT']])
        else:
            cs = ctx['cs']
            r = ctx['r'][i % 2]
            t1, t2 = ctx['t']
            x1, x2 = ps[0][0:R2, 0:TG], ps[1][0:R2, 0:TG]
            cc, ss = cs[:, 0, t0:t0 + TG], cs[:, 1, t0:t0 + TG]
            rd = [ps[0].res, ps[1].res, cs.res]
            S.op('dve', lambda e: e.tensor_tensor(t1[:], x1, cc, ALU.mult), reads=rd, writes=[t1.res])
            S.op('dve', lambda e: e.tensor_tensor(t2[:], x2, ss, ALU.mult), reads=rd, writes=[t2.res])
            S.op('dve', lambda e: e.tensor_tensor(r[:, 0, :], t1[:], t2[:], ALU.subtract),
                 reads=[t1.res, t2.res], writes=[r.res])
            S.op('dve', lambda e: e.tensor_tensor(t1[:], x2, cc, ALU.mult), reads=rd + [r.res], writes=[t1.res])
            S.op('dve', lambda e: e.tensor_tensor(t2[:], x1, ss, ALU.mult), reads=rd + [r.res], writes=[t2.res])
            S.op('dve', lambda e: e.tensor_tensor(r[:, 1, :], t1[:], t2[:], ALU.add),
                 reads=[t1.res, t2.res], writes=[r.res])
            for j in range(2):
                S.dma('sp', qrT[h * ROPE + j * R2:h * ROPE + (j + 1) * R2, t0:t0 + TG], r[:, j, :],
                      reads=[r.res], writes=[kb.dres['qrT']])
    gemm_phase(kb, name + '_q', [(cqnT, 'cqnT')], jobs, ep_q, SG=NG, NB=2, nbanks_per_job=2, ep_alloc=alloc_q)
    nkv = KVL // 128
    jobs = [dict(groups=[[(W['mla_w_ukv'], h * (NOPE + VD), NOPE, 0, nkv)]], h=h) for h in range(H)]

    def alloc_k(kb, st):
        return dict(o=[kb.sb(st, name + '_ko%d' % i, [128, TG], BF16) for i in range(3)], n=[0])

    def ep_k(kb, job, ps, g, t0, ctx):
        i = ctx['n'][0]
        ctx['n'][0] += 1
        o = ctx['o'][i % 3]
        h = job['h']
        S.op('act', lambda e: e.copy(o[:], ps[0][:, 0:TG]), reads=[ps[0].res], writes=[o.res])
        S.dma('sp', knT[h * NOPE:(h + 1) * NOPE, t0:t0 + TG], o[:], reads=[o.res], writes=[kb.dres['knT']])
    gemm_phase(kb, name + '_k', [(ckvnT, 'ckvnT')], jobs, ep_k, SG=NG, NB=4, nbanks_per_job=1, ep_alloc=alloc_k)
    with ExitStack() as st:
        X = kb.sb(st, name + '_vx', [128, nkv, LP], BF16)
        Wv = kb.sb(st, name + '_vw', [128, nkv, H * VD], BF16)
        vo = [kb.sb(st, name + '_vo%d' % i, [128, 512], BF16) for i in range(3)]
        pv = [kb.ps(st, name + '_vps%d' % i, [128, 512]) for i in range(4)]
        for k in range(nkv):
            S.dma('sp', X[:, k, :], ckvnT[k * 128:(k + 1) * 128, :], reads=[kb.dres['ckvnT']], writes=[X.res])
        for h in range(H):
            c0 = h * (NOPE + VD) + NOPE
            S.dma('pool', Wv[:, :, h * VD:(h + 1) * VD],
                  W['mla_w_ukv'][:, c0:c0 + VD].rearrange("(k p) m -> p k m", p=128), writes=[Wv.res])
        cnt = 0
        HV = H * VD
        for t in range(NT):
            for c0, cw in chunks(HV, 512):
                ps = pv[cnt % 4]
                o = vo[cnt % 3]
                cnt += 1

                def fn(pe, ps=ps, t=t, c0=c0, cw=cw):
                    ins = None
                    for k in range(nkv):
                        ins = pe.matmul(ps[:, 0:cw], X[:, k, t * 128:(t + 1) * 128], Wv[:, k, c0:c0 + cw],
                                        start=(k == 0), stop=(k == nkv - 1))
                    return ins
                S.op('pe', fn, reads=[X.res, Wv.res], writes=[ps.res])
                S.op('act', lambda e, o=o, ps=ps, cw=cw: e.copy(o[:, 0:cw], ps[:, 0:cw]), reads=[ps.res], writes=[o.res])
                S.dma('sp', Vtok[t * 128:(t + 1) * 128, c0:c0 + cw], o[:, 0:cw], reads=[o.res], writes=[kb.dres['Vtok']])
    S.barrier()
    gt = TG // 128
    npart = gt + 1
    with ExitStack() as st:
        masks = kb.sb(st, name + '_mask', [128, npart, TG], BF16)
        onesb = kb.sb(st, name + '_ones', [128, 128], BF16)
        Kr = kb.sb(st, name + '_Kr', [ROPE, LP], BF16)
        Qn = [kb.sb(st, name + '_Qn%d' % i, [128, LP], BF16) for i in range(2)]
        Qr = [kb.sb(st, name + '_Qr%d' % i, [ROPE, LP], BF16) for i in range(2)]
        Kn = [kb.sb(st, name + '_Kn%d' % i, [128, LP], BF16) for i in range(2)]
        Vh = [kb.sb(st, name + '_Vh%d' % i, [128, NT, VD], BF16) for i in range(2)]
        PT = [kb.sb(st, name + '_PT%d' % i, [128, TG], BF16) for i in range(3)]
        rden = kb.sb(st, name + '_rden', [128, TG], F32)
        oo = [kb.sb(st, name + '_oo%d' % i, [128, TG], BF16) for i in range(2)]
        pS = [kb.ps(st, name + '_pS%d' % i, [128, 512]) for i in range(3)]
        pO = [kb.ps(st, name + '_pO%d' % i, [128, 512]) for i in range(2)]
        pD = [kb.ps(st, name + '_pD%d' % i, [128, 512]) for i in range(2)]
        S.op('dve', lambda e: e.memset(onesb[:], 1.0), writes=[onesb.res])
        S.op('dve', lambda e: e.memset(masks[:], 0.0), writes=[masks.res])
        for j in range(npart):
            for m in range(-1, (TG - 16) // 64 + 1):
                q0 = max(0, 16 + 64 * m)
                q1 = min(TG, 80 + 64 * m)
                thr = min(128, 80 + 64 * m - 128 * j)
                if thr > 0 and q1 > q0:
                    S.op('dve', lambda e, j=j, thr=thr, q0=q0, q1=q1: e.memset(masks[0:thr, j, q0:q1], 1.0),
                         writes=[masks.res])
        S.dma('sp', Kr[:], krT, reads=[kb.dres['krT']], writes=[Kr.res])
        cnt = 0
        for h in range(H):
            b = h % 2
            S.dma('sp', Qn[b][:], qnT[h * NOPE:(h + 1) * NOPE, :], reads=[kb.dres['qnT']], writes=[Qn[b].res])
            S.dma('sp', Qr[b][:], qrT[h * ROPE:(h + 1) * ROPE, :], reads=[kb.dres['qrT']], writes=[Qr[b].res])
            S.dma('sp', Kn[b][:], knT[h * NOPE:(h + 1) * NOPE, :], reads=[kb.dres['knT']], writes=[Kn[b].res])
            S.dma('sp', Vh[b][:], Vtok[:, h * VD:(h + 1) * VD].rearrange("(t p) d -> p t d", p=128),
                  reads=[kb.dres['Vtok']], writes=[Vh[b].res])
            steps = []
            for g in range(NG):
                kts = list(range(0, min(gt * g + npart, NT)))
                for ki, kt in enumerate(kts):
                    steps.append((g, ki, kt, len(kts)))

            def emit_score(step, idx):
                g, ki, kt, nk = step
                q0 = g * TG
                ps = pS[idx % 3]
                pt = PT[idx % 3]

                def fs(pe):
                    pe.matmul(ps[:, 0:TG], Kn[b][:, kt * 128:(kt + 1) * 128], Qn[b][:, q0:q0 + TG],
                              start=True, stop=False)
                    return pe.matmul(ps[:, 0:TG], Kr[:, kt * 128:(kt + 1) * 128], Qr[b][:, q0:q0 + TG],
                                     start=False, stop=True)
                S.op('pe', fs, reads=[Kn[b].res, Qn[b].res, Kr.res, Qr[b].res], writes=[ps.res])
                S.op('act', lambda e: e.activation(pt[:], ps[:, 0:TG], AF.Exp), reads=[ps.res], writes=[pt.res])
                j = kt - gt * g
                if j >= 0:
                    S.op('dve', lambda e: e.tensor_tensor(pt[:], pt[:], masks[:, j, :], ALU.mult),
                         reads=[pt.res, masks.res], writes=[pt.res])

            def emit_pv(step, idx):
                g, ki, kt, nk = step
                q0 = g * TG
                pt = PT[idx % 3]
                po, pd = pO[g % 2], pD[g % 2]
                first, last = (ki == 0), (ki == nk - 1)

                def fo(pe):
                    pe.matmul(po[:, 0:TG], Vh[b][:, kt, :], pt[:], start=first, stop=last)
                    return pe.matmul(pd[:, 0:TG], onesb[:], pt[:], start=first, stop=last)
                S.op('pe', fo, reads=[Vh[b].res, pt.res, onesb.res], writes=[po.res, pd.res])
                if last:
                    o = oo[g % 2]
                    S.op('dve', lambda e: e.reciprocal(rden[:], pd[:, 0:TG]), reads=[pd.res], writes=[rden.res])
                    S.op('dve', lambda e: e.tensor_tensor(o[:], po[:, 0:TG], rden[:], ALU.mult),
                         reads=[po.res, rden.res], writes=[o.res])
                    S.dma('sp', oT[h * VD:(h + 1) * VD, q0:q0 + TG], o[:], reads=[o.res], writes=[kb.dres['oT']])

            for i, stp in enumerate(steps):
                emit_score(stp, cnt + i)
                if i > 0:
                    emit_pv(steps[i - 1], cnt + i - 1)
            emit_pv(steps[-1], cnt + len(steps) - 1)
            cnt += len(steps)
    S.barrier()


def s5_phase(kb, name, l, W, pT, s5hT):
    c, S = kb.c, kb.S
    LP = c['LP']
    G, P, NGr, SW = c['S5G'], c['S5P'], c['S5NG'], c['S5W']
    assert P == 64 and G == 16
    NST = NGr // 2
    CH = 512
    with ExitStack() as st:
        def sbt(n, shape, dt=F32):
            return kb.sb(st, name + '_' + n, shape, dt)
        ar, ai, dl = sbt('ar', [128, NST]), sbt('ai', [128, NST]), sbt('dl', [128, NST])
        mag, th, co, si = sbt('mag', [128, NST]), sbt('th', [128, NST]), sbt('co', [128, NST]), sbt('si', [128, NST])
        abr, abi, den = sbt('abr', [128, NST]), sbt('abi', [128, NST]), sbt('den', [128, NST])
        zr, zi, t1, t2 = sbt('zr', [128, NST]), sbt('zi', [128, NST]), sbt('t1', [128, NST]), sbt('t2', [128, NST])
        ki = sbt('ki', [128, NST], mybir.dt.int32)
        nl = W['s5_log_dt']
        with kb.nc.allow_non_contiguous_dma("small s5 parameter loads"):
            for two in range(2):
                S.dma('sp', ar[two * 64:(two + 1) * 64, :],
                      W['s5_a_re'].rearrange("(s two) p -> two p s", two=2)[two], writes=[ar.res])
                S.dma('sp', ai[two * 64:(two + 1) * 64, :],
                      W['s5_a_im'].rearrange("(s two) p -> two p s", two=2)[two], writes=[ai.res])
                S.dma('sp', dl[two * 64:(two + 1) * 64, :],
                      nl.rearrange("(s two) -> two s", two=2)[two:two + 1, :].broadcast_to([64, NST]),
                      writes=[dl.res])

        def ew(eng, fn, reads, writes):
            S.op(eng, fn, reads=[r.res for r in reads], writes=[w.res for w in writes])

        def sincos(th_, co_, si_):
            y, y2, p_ = t1, t2, zr
            ew('dve', lambda e: e.tensor_scalar(y[:], th_[:], 1.0 / (2 * math.pi), None, ALU.mult), [th_], [y])
            ew('dve', lambda e: e.tensor_copy(ki[:], y[:]), [y], [ki])
            ew('dve', lambda e: e.tensor_copy(y[:], ki[:]), [ki], [y])
            ew('dve', lambda e: e.scalar_tensor_tensor(y[:], y[:], -2 * math.pi, th_[:], ALU.mult, ALU.add), [y, th_], [y])
            ew('dve', lambda e: e.tensor_scalar(y[:], y[:], 0.125, None, ALU.mult), [y], [y])
            ew('dve', lambda e: e.tensor_tensor(y2[:], y[:], y[:], ALU.mult), [y], [y2])
            ew('dve', lambda e: e.tensor_scalar(p_[:], y2[:], 1.0 / 362880, None, ALU.mult), [y2], [p_])
            for a_ in (-1.0 / 5040, 1.0 / 120, -1.0 / 6):
                ew('dve', lambda e, a_=a_: e.scalar_tensor_tensor(p_[:], p_[:], a_, y2[:], ALU.add, ALU.mult), [p_, y2], [p_])
            ew('dve', lambda e: e.scalar_tensor_tensor(si_[:], p_[:], 1.0, y[:], ALU.add, ALU.mult), [p_, y], [si_])
            ew('dve', lambda e: e.tensor_scalar(p_[:], y2[:], -1.0 / 3628800, None, ALU.mult), [y2], [p_])
            for a_ in (1.0 / 40320, -1.0 / 720, 1.0 / 24, -0.5):
                ew('dve', lambda e, a_=a_: e.scalar_tensor_tensor(p_[:], p_[:], a_, y2[:], ALU.add, ALU.mult), [p_, y2], [p_])
            ew('dve', lambda e: e.tensor_scalar(co_[:], p_[:], 1.0, None, ALU.add), [p_], [co_])
            for _ in range(3):
                ew('dve', lambda e: e.tensor_tensor(y2[:], si_[:], si_[:], ALU.mult), [si_], [y2])
                ew('dve', lambda e: e.scalar_tensor_tensor(si_[:], si_[:], 2.0, co_[:], ALU.mult, ALU.mult), [si_, co_], [si_])
                ew('dve', lambda e: e.tensor_scalar(co_[:], y2[:], -2.0, 1.0, ALU.mult, ALU.add), [y2], [co_])

        ew('act', lambda e: e.activation(dl[:], dl[:], AF.Exp), [dl], [dl])
        ew('dve', lambda e: e.tensor_tensor(mag[:], ar[:], dl[:], ALU.mult), [ar, dl], [mag])
        ew('act', lambda e: e.activation(mag[:], mag[:], AF.Exp), [mag], [mag])
        ew('dve', lambda e: e.tensor_tensor(th[:], ai[:], dl[:], ALU.mult), [ai, dl], [th])
        sincos(th, co, si)
        ew('dve', lambda e: e.tensor_tensor(abr[:], mag[:], co[:], ALU.mult), [mag, co], [abr])
        ew('dve', lambda e: e.tensor_tensor(abi[:], mag[:], si[:], ALU.mult), [mag, si], [abi])
        ew('dve', lambda e: e.tensor_tensor(den[:], ar[:], ar[:], ALU.mult), [ar], [den])
        ew('dve', lambda e: e.tensor_tensor(t1[:], ai[:], ai[:], ALU.mult), [ai], [t1])
        ew('dve', lambda e: e.tensor_tensor(den[:], den[:], t1[:], ALU.add), [den, t1], [den])
        ew('dve', lambda e: e.reciprocal(den[:], den[:]), [den], [den])
        ew('dve', lambda e: e.tensor_scalar(t2[:], abr[:], -1.0, None, ALU.add), [abr], [t2])
        ew('dve', lambda e: e.tensor_tensor(zr[:], t2[:], ar[:], ALU.mult), [t2, ar], [zr])
        ew('dve', lambda e: e.tensor_tensor(t1[:], abi[:], ai[:], ALU.mult), [abi, ai], [t1])
        ew('dve', lambda e: e.tensor_tensor(zr[:], zr[:], t1[:], ALU.add), [zr, t1], [zr])
        ew('dve', lambda e: e.tensor_tensor(zr[:], zr[:], den[:], ALU.mult), [zr, den], [zr])
        ew('dve', lambda e: e.tensor_tensor(zi[:], abi[:], ar[:], ALU.mult), [abi, ar], [zi])
        ew('dve', lambda e: e.tensor_tensor(t1[:], t2[:], ai[:], ALU.mult), [t2, ai], [t1])
        ew('dve', lambda e: e.tensor_tensor(zi[:], zi[:], t1[:], ALU.subtract), [zi, t1], [zi])
        ew('dve', lambda e: e.tensor_tensor(zi[:], zi[:], den[:], ALU.mult), [zi, den], [zi])
        NLV = int(math.ceil(math.log2(LP)))
        pwr, pwi = sbt('pwr', [128, NLV, NST]), sbt('pwi', [128, NLV, NST])
        npwi = sbt('npwi', [128, NLV, NST])
        ew('dve', lambda e: e.tensor_copy(pwr[:, 0, :], abr[:]), [abr], [pwr])
        ew('dve', lambda e: e.tensor_copy(pwi[:, 0, :], abi[:]), [abi], [pwi])
        for k in range(1, NLV):
            ew('dve', lambda e, k=k: e.tensor_tensor(t1[:], pwr[:, k - 1, :], pwr[:, k - 1, :], ALU.mult), [pwr], [t1])
            ew('dve', lambda e, k=k: e.tensor_tensor(t2[:], pwi[:, k - 1, :], pwi[:, k - 1, :], ALU.mult), [pwi], [t2])
            ew('dve', lambda e, k=k: e.tensor_tensor(pwr[:, k, :], t1[:], t2[:], ALU.subtract), [t1, t2], [pwr])
            ew('dve', lambda e, k=k: e.tensor_tensor(t1[:], pwr[:, k - 1, :], pwi[:, k - 1, :], ALU.mult), [pwr, pwi], [t1])
            ew('dve', lambda e, k=k: e.tensor_scalar(pwi[:, k, :], t1[:], 2.0, None, ALU.mult), [t1], [pwi])
        ew('dve', lambda e: e.tensor_scalar(npwi[:], pwi[:], -1.0, None, ALU.mult), [pwi], [npwi])
        Bb_r, Bb_i = sbt('Bb_r', [128, NST, 32]), sbt('Bb_i', [128, NST, 32])
        Br, Bi = sbt('Br', [128, NST, 32]), sbt('Bi', [128, NST, 32])
        for t_ in (Br, Bi):
            ew('dve', lambda e, t_=t_: e.memset(t_[:], 0.0), [], [t_])
        with kb.nc.allow_non_contiguous_dma("small s5 parameter loads"):
            for two in range(2):
                for (src, dst) in ((W['s5_b_re'], Br), (W['s5_b_im'], Bi)):
                    v = src.rearrange("(s two) p c -> two p s c", two=2)[two]
                    for s0, sn in chunks(NST, 8):
                        S.dma('sp', dst[two * 64:(two + 1) * 64, s0:s0 + sn, two * 16:(two + 1) * 16],
                              v[:, s0:s0 + sn, :], writes=[dst.res])
        zrb = zr[:].unsqueeze(2).to_broadcast([128, NST, 32])
        zib = zi[:].unsqueeze(2).to_broadcast([128, NST, 32])
        tB = sbt('tB', [128, NST, 32])
        ew('dve', lambda e: e.tensor_tensor(Bb_r[:], Br[:], zrb, ALU.mult), [Br, zr], [Bb_r])
        ew('dve', lambda e: e.tensor_tensor(tB[:], Bi[:], zib, ALU.mult), [Bi, zi], [tB])
        ew('dve', lambda e: e.tensor_tensor(Bb_r[:], Bb_r[:], tB[:], ALU.subtract), [Bb_r, tB], [Bb_r])
        ew('dve', lambda e: e.tensor_tensor(Bb_i[:], Bi[:], zrb, ALU.mult), [Bi, zr], [Bb_i])
        ew('dve', lambda e: e.tensor_tensor(tB[:], Br[:], zib, ALU.mult), [Br, zi], [tB])
        ew('dve', lambda e: e.tensor_tensor(Bb_i[:], Bb_i[:], tB[:], ALU.add), [Bb_i, tB], [Bb_i])
        ident = sbt('ident', [128, 128])
        ew('dve', lambda e: e.memset(ident[:], 0.0), [], [ident])
        S.op('pool', lambda e: e.affine_select(ident[:], ident[:], [[-1, 128]], ALU.not_equal, 1.0, 0, 1),
             reads=[ident.res], writes=[ident.res])
        BT_r, BT_i = sbt('BT_r', [32, NST, 128]), sbt('BT_i', [32, NST, 128])
        ptr = [kb.ps(st, name + '_ptr%d' % i, [128, 512]) for i in range(2)]
        n = 0
        for (src, dst) in ((Bb_r, BT_r), (Bb_i, BT_i)):
            for s in range(NST):
                p_ = ptr[n % 2]
                n += 1
                S.op('pe', lambda e, p_=p_, src=src, s=s: e.transpose(p_[0:32, 0:128], src[:, s, :], ident[:]),
                     reads=[src.res, ident.res], writes=[p_.res])
                S.op('dve', lambda e, p_=p_, dst=dst, s=s: e.tensor_copy(dst[:, s, :], p_[0:32, 0:128]),
                     reads=[p_.res], writes=[dst.res])
        NT8 = NGr // 8
        Xc = sbt('Xc', [128, 2, 128])
        Cb_r, Cb_i = sbt('Cb_r', [128, NT8, 128]), sbt('Cb_i', [128, NT8, 128])
        for T8 in range(NT8):
            ew('dve', lambda e: e.memset(Xc[:], 0.0), [], [Xc])
            for ri, src in enumerate((W['s5_c_re'], W['s5_c_im'])):
                for gl in range(8):
                    S.dma('sp', Xc[16 * gl:16 * gl + 16, ri, (gl % 2) * 64:(gl % 2) * 64 + 64], src[8 * T8 + gl],
                          writes=[Xc.res])
            for ri, dst in enumerate((Cb_r, Cb_i)):
                p_ = ptr[n % 2]
                n += 1
                S.op('pe', lambda e, p_=p_, ri=ri: e.transpose(p_[:, 0:128], Xc[:, ri, :], ident[:]),
                     reads=[Xc.res, ident.res], writes=[p_.res])
                if ri == 0:
                    S.op('dve', lambda e, p_=p_, dst=dst, T8=T8: e.tensor_copy(dst[:, T8, :], p_[:, 0:128]),
                         reads=[p_.res], writes=[dst.res])
                else:
                    S.op('dve', lambda e, p_=p_, dst=dst, T8=T8: e.tensor_scalar(dst[:, T8, :], p_[:, 0:128], -1.0, None,
                                                                               ALU.mult),
                         reads=[p_.res], writes=[dst.res])
        dsk = sbt('dsk', [32, NST])
        with kb.nc.allow_non_contiguous_dma("small s5 parameter loads"):
            S.dma('sp', dsk[:], W['s5_d'].rearrange("(s two) c -> (two c) s", two=2), writes=[dsk.res])
        kb.dump('abr', abr, [128, NST]); kb.dump('abi', abi, [128, NST]); kb.dump('zr', zr, [128, NST]); kb.dump('zi', zi, [128, NST])
        kb.dump('Bb_r', Bb_r, [128, NST, 32]); kb.dump('BT_r', BT_r, [32, NST, 128]); kb.dump('Cb_r', Cb_r, [128, NT8, 128])
        kb.dump('pwr', pwr, [128, NLV, NST])
        u = [sbt('u%d' % i, [32, LP]) for i in range(2)]
        X = [[sbt('x%d%d' % (i, j), [128, LP]) for j in range(2)] for i in range(2)]
        tmp = sbt('ptmp', [128, LP])
        yo = [sbt('yo%d' % i, [32, CH]) for i in range(2)]
        y2 = sbt('y2', [32, CH])
        hb = [sbt('hb%d' % i, [32, CH], BF16) for i in range(2)]
        pb = [kb.ps(st, name + '_pb%d' % i, [128, 512]) for i in range(4)]
        pc = 0
        for s in range(NST):
            us = u[s % 2]
            S.dma('sp', us[:], pT[c['OFF_S5'] + 32 * s:c['OFF_S5'] + 32 * s + 32, :],
                  reads=[kb.dres['pT']], writes=[us.res])
            cur = 0
            for ri, BT in enumerate((BT_r, BT_i)):
                for c0, cw in chunks(LP, CH):
                    p_ = pb[pc % 4]
                    pc += 1
                    S.op('pe', lambda e, p_=p_, BT=BT, c0=c0, cw=cw, us=us, s=s:
                         e.matmul(p_[:, 0:cw], BT[:, s, :], us[:, c0:c0 + cw], start=True, stop=True),
                         reads=[BT.res, us.res], writes=[p_.res])
                    S.op('act', lambda e, p_=p_, ri=ri, c0=c0, cw=cw: e.copy(X[0][ri][:, c0:c0 + cw], p_[:, 0:cw]),
                         reads=[p_.res], writes=[X[0][ri].res])
            for k in range(NLV):
                d = 1 << k
                if d >= LP:
                    break
                a, b_ = X[cur], X[1 - cur]
                Pr, Pi, nPi = pwr[:, k, s:s + 1], pwi[:, k, s:s + 1], npwi[:, k, s:s + 1]
                n_ = LP - d
                ew('dve', lambda e: e.scalar_tensor_tensor(b_[0][:, d:LP], a[0][:, 0:n_], Pr, a[0][:, d:LP],
                                                           ALU.mult, ALU.add), [a[0], pwr], [b_[0]])
                ew('dve', lambda e: e.scalar_tensor_tensor(b_[0][:, d:LP], a[1][:, 0:n_], nPi, b_[0][:, d:LP],
                                                           ALU.mult, ALU.add), [a[1], npwi, b_[0]], [b_[0]])
                ew('act', lambda e: e.copy(b_[0][:, 0:d], a[0][:, 0:d]), [a[0]], [b_[0]])
                ew('dve', lambda e: e.scalar_tensor_tensor(b_[1][:, d:LP], a[1][:, 0:n_], Pr, a[1][:, d:LP],
                                                           ALU.mult, ALU.add), [a[1], pwr], [b_[1]])
                ew('dve', lambda e: e.scalar_tensor_tensor(b_[1][:, d:LP], a[0][:, 0:n_], Pi, b_[1][:, d:LP],
                                                           ALU.mult, ALU.add), [a[0], pwi, b_[1]], [b_[1]])
                ew('act', lambda e: e.copy(b_[1][:, 0:d], a[1][:, 0:d]), [a[1]], [b_[1]])
                cur = 1 - cur
            xr, xi = X[cur]
            if s == 0:
                kb.dump('xr0', xr, [128, LP]); kb.dump('bu0', X[0][0], [128, LP])
            T8, j4 = s // 4, s % 4
            for ci, (c0, cw) in enumerate(chunks(LP, CH)):
                p_ = pb[pc % 4]
                pc += 1
                yy, hh = yo[ci % 2], hb[ci % 2]

                def fy(pe, p_=p_, c0=c0, cw=cw, xr=xr, xi=xi, T8=T8, j4=j4):
                    pe.matmul(p_[0:32, 0:cw], Cb_r[:, T8, 32 * j4:32 * j4 + 32], xr[:, c0:c0 + cw], start=True, stop=False)
                    return pe.matmul(p_[0:32, 0:cw], Cb_i[:, T8, 32 * j4:32 * j4 + 32], xi[:, c0:c0 + cw],
                                     start=False, stop=True)
                S.op('pe', fy, reads=[Cb_r.res, Cb_i.res, xr.res, xi.res], writes=[p_.res])
                ew_r = [p_.res, us.res, dsk.res]
                S.op('dve', lambda e, p_=p_, yy=yy, c0=c0, cw=cw, us=us, s=s: e.scalar_tensor_tensor(
                    yy[:, 0:cw], us[:, c0:c0 + cw], dsk[:, s:s + 1], p_[0:32, 0:cw], ALU.mult, ALU.add),
                    reads=ew_r, writes=[yy.res])
                S.op('dve', lambda e, yy=yy, cw=cw: e.tensor_tensor(y2[:, 0:cw], yy[:, 0:cw], yy[:, 0:cw], ALU.mult),
                     reads=[yy.res], writes=[y2.res])
                S.op('dve', lambda e, cw=cw: e.tensor_scalar(y2[:, 0:cw], y2[:, 0:cw], 0.044715, 1.0, ALU.mult, ALU.add),
                     reads=[y2.res], writes=[y2.res])
                S.op('dve', lambda e, yy=yy, cw=cw: e.tensor_tensor(y2[:, 0:cw], y2[:, 0:cw], yy[:, 0:cw], ALU.mult),
                     reads=[y2.res, yy.res], writes=[y2.res])
                S.op('act', lambda e, cw=cw: e.activation(y2[:, 0:cw], y2[:, 0:cw], AF.Sigmoid, scale=1.5957691216057308),
                     reads=[y2.res], writes=[y2.res])
                S.op('dve', lambda e, yy=yy, hh=hh, cw=cw: e.tensor_tensor(hh[:, 0:cw], y2[:, 0:cw], yy[:, 0:cw], ALU.mult),
                     reads=[y2.res, yy.res], writes=[hh.res])
                S.dma('sp', s5hT[32 * s:32 * s + 32, c0:c0 + cw], hh[:, 0:cw], reads=[hh.res], writes=[kb.dres['s5hT']])
    S.barrier()


def make_ident(kb, st, name):
    ident = kb.sb(st, name, [128, 128], F32)
    kb.S.op('dve', lambda e: e.memset(ident[:], 0.0), writes=[ident.res])
    kb.S.op('pool', lambda e: e.affine_select(ident[:], ident[:], [[-1, 128]], ALU.not_equal, 1.0, 0, 1),
            reads=[ident.res], writes=[ident.res])
    return ident


def dn_phase(kb, name, l, W, pT, bgT, dnoT):
    c, S = kb.c, kb.S
    LP, NT = c['LP'], c['NT']
    NH, DK, DV = c['DNH'], c['DK'], c['DV']
    assert DK == 128 and DV == 128
    QKW = c['DNQK']
    CH = 512
    NLV = 7

    def ew(eng, fn, reads, writes):
        S.op(eng, fn, reads=[r.res for r in reads], writes=[w.res for w in writes])

    with ExitStack() as st:
        ab = kb.sb(st, name + '_ab', [NH, 2, LP], F32)
        gg = [kb.sb(st, name + '_gg%d' % i, [NH, LP], F32) for i in range(2)]
        prm = kb.sb(st, name + '_prm', [NH, 2], F32)
        for j, off in enumerate((c['OFF_DN_A'], c['OFF_DN_B'])):
            S.dma('sp', ab[:, j, :], pT[off:off + NH, :], reads=[kb.dres['pT']], writes=[ab.res])
        with kb.nc.allow_non_contiguous_dma("tiny"):
            S.dma('sp', prm[:, 0:1], W['dn_a_log'].rearrange("(h o) -> h o", o=1), writes=[prm.res])
            S.dma('sp', prm[:, 1:2], W['dn_dt_bias'].rearrange("(h o) -> h o", o=1), writes=[prm.res])
        ew('act', lambda e: e.activation(ab[:, 1, :], ab[:, 1, :], AF.Sigmoid), [ab], [ab])
        ew('act', lambda e: e.activation(prm[:, 0:1], prm[:, 0:1], AF.Exp), [prm], [prm])
        ew('dve', lambda e: e.tensor_scalar(prm[:, 0:1], prm[:, 0:1], -1.0, None, ALU.mult), [prm], [prm])
        ew('act', lambda e: e.activation(gg[0][:], ab[:, 0, :], AF.Exp, bias=prm[:, 1:2]), [ab, prm], [gg[0]])
        ew('act', lambda e: e.activation(gg[0][:], gg[0][:], AF.Ln, bias=1.0), [gg[0]], [gg[0]])
        ew('dve', lambda e: e.tensor_scalar(gg[0][:], gg[0][:], prm[:, 0:1], None, ALU.mult), [gg[0], prm], [gg[0]])
        cur = 0
        for k in range(NLV):
            d = 1 << k
            a3 = gg[cur][:].rearrange("h (t p) -> h t p", p=128)
            b3 = gg[1 - cur][:].rearrange("h (t p) -> h t p", p=128)
            ew('dve', lambda e: e.tensor_tensor(b3[:, :, d:128], a3[:, :, d:128], a3[:, :, 0:128 - d], ALU.add),
               [gg[cur]], [gg[1 - cur]])
            ew('dve', lambda e: e.tensor_copy(b3[:, :, 0:d], a3[:, :, 0:d]), [gg[cur]], [gg[1 - cur]])
            cur = 1 - cur
        S.dma('sp', bgT[0:NH, :], ab[:, 1, :], reads=[ab.res], writes=[kb.dres['bgT']])
        S.dma('sp', bgT[NH:2 * NH, :], gg[cur][:], reads=[gg[cur].res], writes=[kb.dres['bgT']])
    S.barrier()

    with ExitStack() as st:
        def sbt(n, shape, dt=F32):
            return kb.sb(st, name + '_' + n, shape, dt)
        ident = make_ident(kb, st, name + '_ident')
        ones = sbt('ones', [128, 128])
        ew('dve', lambda e: e.memset(ones[:], 1.0), [], [ones])
        m_incl, m_strict = sbt('m_incl', [128, 128]), sbt('m_strict', [128, 128])
        ew('dve', lambda e: e.memset(m_incl[:], 1.0), [], [m_incl])
        ew('dve', lambda e: e.memset(m_strict[:], 1.0), [], [m_strict])
        S.op('pool', lambda e: e.affine_select(m_incl[:], m_incl[:], [[-1, 128]], ALU.is_ge, 0.0, 0, 1),
             reads=[m_incl.res], writes=[m_incl.res])
        S.op('pool', lambda e: e.affine_select(m_strict[:], m_strict[:], [[-1, 128]], ALU.is_gt, 0.0, 0, 1),
             reads=[m_strict.res], writes=[m_strict.res])
        tm = sbt('tm', [128, NT, 2 * NH])
        with kb.nc.allow_non_contiguous_dma("token-major per-token scalars"):
            for t in range(NT):
                S.dma('sp', tm[:, t, :], bgT[:, t * 128:(t + 1) * 128].rearrange("r p -> p r"),
                      reads=[kb.dres['bgT']], writes=[tm.res])
        cwt = sbt('cwt', [128, 3, 4])
        gout = sbt('gout', [128, 1])
        with kb.nc.allow_non_contiguous_dma("tiny"):
            S.dma('sp', gout[:], W['dn_out_norm_g'].rearrange("(p o) -> p o", o=1), writes=[gout.res])
        qT, kT, vT = sbt('qT', [128, LP]), sbt('kT', [128, LP]), sbt('vT', [128, LP])
        tmp, gcrow, qe, oT = sbt('tmp', [128, LP]), sbt('gcrow', [128, LP]), sbt('qe', [128, LP]), sbt('oT', [128, LP])
        rn = sbt('rn', [128, CH])
        sc = sbt('sc', [128, 6, NT])
        Sst = [sbt('S%d' % i, [128, 128]) for i in range(2)]
        CS = []
        for q_ in range(2):
            CS.append(dict(
                Rt=[sbt('R%d_%d' % (i, q_), [128, 256]) for i in range(2)],
                kd=sbt('kd_%d' % q_, [128, 128]), vn=sbt('vn_%d' % q_, [128, 128]), wT=sbt('wT_%d' % q_, [128, 128]),
                E=sbt('E_%d' % q_, [128, 128]), Dm=sbt('Dm_%d' % q_, [128, 128]), Dms=sbt('Dms_%d' % q_, [128, 128]),
                QKm=sbt('QKm_%d' % q_, [128, 128]), QKT=sbt('QKT_%d' % q_, [128, 128]),
                Ak=[sbt('A%d_%d' % (i, q_), [128, 128]) for i in range(2)],
                Bk=[sbt('B%d_%d' % (i, q_), [128, 128]) for i in range(2)]))
        ob = [sbt('ob%d' % i, [128, CH], BF16) for i in range(2)]
        PSL = [kb.ps(st, name + '_ps%d' % i, [128, 512]) for i in range(8)]
        pcn = [0]

        def nps():
            p = PSL[pcn[0] % 8]
            pcn[0] += 1
            return p

        def colsumsq(src, D_, eps, dst_rn, c0, cw):
            p_ = nps()
            ew('act', lambda e: e.activation(tmp[:, c0:c0 + cw], src[:, c0:c0 + cw], AF.Square), [src], [tmp])
            ew('pe', lambda e: e.matmul(p_[:, 0:cw], ones[:], tmp[:, c0:c0 + cw], start=True, stop=True), [ones, tmp], [p_])
            rstd_from_sumsq(kb, dst_rn[:, 0:cw], dst_rn.res, p_[:, 0:cw], p_.res, D_, eps)

        for h in range(NH):
            for ti, (dst, base) in enumerate(((qT, 0), (kT, QKW), (vT, 2 * QKW))):
                ch0 = base + h * 128
                S.dma('sp', tmp[:], pT[c['OFF_DN_QKV'] + ch0:c['OFF_DN_QKV'] + ch0 + 128, :],
                      reads=[kb.dres['pT']], writes=[tmp.res])
                with kb.nc.allow_non_contiguous_dma("tiny"):
                    S.dma('sp', cwt[:, ti, :], W['dn_conv_w'][:, ch0:ch0 + 128].rearrange("j p -> p j"), writes=[cwt.res])
                ew('dve', lambda e: e.tensor_scalar(dst[:], tmp[:], cwt[:, ti, 3:4], None, ALU.mult), [tmp, cwt], [dst])
                for sft in (1, 2, 3):
                    ew('dve', lambda e, sft=sft: e.scalar_tensor_tensor(
                        dst[:, sft:LP], tmp[:, 0:LP - sft], cwt[:, ti, 3 - sft:4 - sft], dst[:, sft:LP], ALU.mult, ALU.add),
                        [tmp, cwt, dst], [dst])
                ew('act', lambda e: e.activation(dst[:], dst[:], AF.Silu), [dst], [dst])
            for dst, mul in ((qT, DK ** -0.5), (kT, 1.0)):
                for c0, cw in chunks(LP, CH):
                    colsumsq(dst, 1.0, EPS, rn, c0, cw)
                    ew('dve', lambda e, c0=c0, cw=cw: e.scalar_tensor_tensor(
                        dst[:, c0:c0 + cw], dst[:, c0:c0 + cw], float(mul), rn[:, 0:cw], ALU.mult, ALU.mult), [dst, rn], [dst])
            S.dma('sp', gcrow[:], bgT[NH + h:NH + h + 1, :].broadcast_to([128, LP]),
                  reads=[kb.dres['bgT']], writes=[gcrow.res])
            ew('act', lambda e: e.activation(qe[:], gcrow[:], AF.Exp), [gcrow], [qe])
            ew('dve', lambda e: e.tensor_copy(sc[:, 0, :], tm[:, :, h]), [tm], [sc])
            ew('dve', lambda e: e.tensor_copy(sc[:, 1, :], tm[:, :, NH + h]), [tm], [sc])
            ew('dve', lambda e: e.tensor_copy(sc[:, 4, :], qe[:].rearrange("p (t k) -> p t k", k=128)[:, :, 127]), [qe], [sc])
            ew('act', lambda e: e.activation(sc[:, 2, :], sc[:, 1, :], AF.Exp), [sc], [sc])
            ew('dve', lambda e: e.tensor_tensor(sc[:, 2, :], sc[:, 2, :], sc[:, 0, :], ALU.mult), [sc], [sc])
            ew('dve', lambda e: e.tensor_tensor(sc[:, 5, :], gcrow[:].rearrange("p (t k) -> p t k", k=128)[:, :, 127],
                                                sc[:, 1, :], ALU.subtract), [gcrow, sc], [sc])
            ew('act', lambda e: e.activation(sc[:, 3, :], sc[:, 5, :], AF.Exp), [sc], [sc])
            ew('dve', lambda e: e.tensor_tensor(qe[:], qe[:], qT[:], ALU.mult), [qe, qT], [qe])
            ew('dve', lambda e: e.memset(Sst[0][:], 0.0), [], [Sst[0]])
            def chunk_gen(ci):
                T_ = CS[ci % 2]
                Rt, kd, vn, wT = T_['Rt'], T_['kd'], T_['vn'], T_['wT']
                E, Dm, Dms, QKm, QKT, Ak, Bk = T_['E'], T_['Dm'], T_['Dms'], T_['QKm'], T_['QKT'], T_['Ak'], T_['Bk']
                cs_ = slice(ci * 128, (ci + 1) * 128)
                beta_i, gc_i = sc[:, 0, ci:ci + 1], sc[:, 1, ci:ci + 1]
                beg_i, kds_i, egl = sc[:, 2, ci:ci + 1], sc[:, 3, ci:ci + 1], sc[:, 4, ci:ci + 1]
                pK, pV = nps(), nps()
                ew('pe', lambda e: e.transpose(pK[:, 0:128], kT[:, cs_], ident[:]), [kT, ident], [pK])
                ew('pe', lambda e: e.transpose(pV[:, 0:128], vT[:, cs_], ident[:]), [vT, ident], [pV])
                pG, pQ = nps(), nps()
                ew('pe', lambda e: e.matmul(pG[:, 0:128], kT[:, cs_], kT[:, cs_], start=True, stop=True), [kT], [pG])
                ew('pe', lambda e: e.matmul(pQ[:, 0:128], qT[:, cs_], kT[:, cs_], start=True, stop=True), [qT, kT], [pQ])
                ew('dve', lambda e: e.tensor_scalar(E[:], gcrow[:, cs_], gc_i, 0.0, ALU.subtract, ALU.max), [gcrow, sc], [E])
                ew('act', lambda e: e.activation(E[:], E[:], AF.Exp, scale=-1.0), [E], [E])
                yield
                X0 = Rt[0]
                ew('act', lambda e: e.activation(X0[:, 0:128], pV[:, 0:128], AF.Copy, scale=beta_i), [pV, sc], [X0])
                ew('dve', lambda e: e.tensor_scalar(X0[:, 128:256], pK[:, 0:128], beg_i, None, ALU.mult), [pK, sc], [X0])
                ew('dve', lambda e: e.tensor_scalar(kd[:], pK[:, 0:128], kds_i, None, ALU.mult), [pK, sc], [kd])
                ew('dve', lambda e: e.tensor_tensor(Dm[:], E[:], m_incl[:], ALU.mult), [E, m_incl], [Dm])
                ew('dve', lambda e: e.tensor_tensor(Dms[:], E[:], m_strict[:], ALU.mult), [E, m_strict], [Dms])
                A0, B0 = Ak[0], Bk[0]
                ew('dve', lambda e: e.scalar_tensor_tensor(A0[:], pG[:, 0:128], beta_i, Dms[:], ALU.mult, ALU.mult),
                   [pG, sc, Dms], [A0])
                ew('dve', lambda e: e.tensor_tensor(QKm[:], pQ[:, 0:128], Dm[:], ALU.mult), [pQ, Dm], [QKm])
                yield
                pB, pT_ = nps(), nps()
                ew('pe', lambda e: e.transpose(pB[:, 0:128], A0[:], ident[:]), [A0, ident], [pB])
                ew('pe', lambda e: e.transpose(pT_[:, 0:128], QKm[:], ident[:]), [QKm, ident], [pT_])
                yield
                ew('act', lambda e: e.copy(B0[:], pB[:, 0:128]), [pB], [B0])
                ew('act', lambda e: e.copy(QKT[:], pT_[:, 0:128]), [pT_], [QKT])
                yield
                xc = 0
                for k in range(NLV):
                    Ac, Bc = Ak[k % 2], Bk[k % 2]
                    An, Bn = Ak[(k + 1) % 2], Bk[(k + 1) % 2]
                    Xc, Xn = Rt[xc], Rt[1 - xc]
                    pY = nps()
                    ew('pe', lambda e: e.matmul(pY[:, 0:256], Bc[:], Xc[:], start=True, stop=True), [Bc, Xc], [pY])
                    if k < NLV - 1:
                        pA, pB2 = nps(), nps()
                        ew('pe', lambda e: e.matmul(pA[:, 0:128], Bc[:], Ac[:], start=True, stop=True), [Bc, Ac], [pA])
                        ew('pe', lambda e: e.matmul(pB2[:, 0:128], Ac[:], Bc[:], start=True, stop=True), [Ac, Bc], [pB2])
                    yield
                    ew('dve', lambda e: e.tensor_tensor(Xn[:], Xc[:], pY[:, 0:256], ALU.subtract if k == 0 else ALU.add),
                       [Xc, pY], [Xn])
                    xc = 1 - xc
                    if k < NLV - 1:
                        ew('act', lambda e: e.copy(An[:], pA[:, 0:128]), [pA], [An])
                        ew('dve', lambda e: e.tensor_copy(Bn[:], pB2[:, 0:128]), [pB2], [Bn])
                    yield
                Xf = Rt[xc]
                pW = nps()
                ew('pe', lambda e: e.transpose(pW[:, 0:128], Xf[:, 128:256], ident[:]), [Xf, ident], [pW])
                yield
                ew('act', lambda e: e.copy(wT[:], pW[:, 0:128]), [pW], [wT])
                yield
                Sc, Sn = Sst[ci % 2], Sst[(ci + 1) % 2]
                pv_, po_, pS_ = nps(), nps(), nps()
                ew('pe', lambda e: e.matmul(pv_[:, 0:128], wT[:], Sc[:], start=True, stop=True), [wT, Sc], [pv_])
                ew('dve', lambda e: e.tensor_tensor(vn[:], Xf[:, 0:128], pv_[:, 0:128], ALU.subtract), [Xf, pv_], [vn])

                def fo(e):
                    e.matmul(po_[:, 0:128], Sc[:], qe[:, cs_], start=True, stop=False)
                    return e.matmul(po_[:, 0:128], vn[:], QKT[:], start=False, stop=True)
                ew('pe', fo, [Sc, qe, vn, QKT], [po_])
                ew('pe', lambda e: e.matmul(pS_[:, 0:128], kd[:], vn[:], start=True, stop=True), [kd, vn], [pS_])
                ew('dve', lambda e: e.scalar_tensor_tensor(Sn[:], Sc[:], egl, pS_[:, 0:128], ALU.mult, ALU.add),
                   [Sc, sc, pS_], [Sn])
                ew('act', lambda e: e.copy(oT[:, cs_], po_[:, 0:128]), [po_], [oT])
                yield

            for c0_ in range(0, NT, 2):
                gens = [chunk_gen(ci) for ci in range(c0_, min(c0_ + 2, NT))]
                while gens:
                    for g_ in list(gens):
                        try:
                            next(g_)
                        except StopIteration:
                            gens.remove(g_)
            S.dma('sp', tmp[:], pT[c['OFF_DN_Z'] + h * 128:c['OFF_DN_Z'] + (h + 1) * 128, :],
                  reads=[kb.dres['pT']], writes=[tmp.res])
            ew('act', lambda e: e.activation(gcrow[:], tmp[:], AF.Silu), [tmp], [gcrow])
            for ci, (c0, cw) in enumerate(chunks(LP, CH)):
                colsumsq(oT, float(DV), EPS, rn, c0, cw)
                ew('dve', lambda e: e.scalar_tensor_tensor(oT[:, c0:c0 + cw], oT[:, c0:c0 + cw], gout[:, 0:1], rn[:, 0:cw],
                                                           ALU.mult, ALU.mult), [oT, gout, rn], [oT])
                o_ = ob[ci % 2]
                ew('dve', lambda e: e.tensor_tensor(o_[:, 0:cw], oT[:, c0:c0 + cw], gcrow[:, c0:c0 + cw], ALU.mult),
                   [oT, gcrow], [o_])
                S.dma('sp', dnoT[h * 128:(h + 1) * 128, c0:c0 + cw], o_[:, 0:cw], reads=[o_.res], writes=[kb.dres['dnoT']])
    S.barrier()


def merge_phase(kb, name, W, gT, oT, s5hT, dnoT, mergedT):
    c, S = kb.c, kb.S
    D, TG = c['D'], c['TG']
    k1, k2, k3 = c['MLAW'] // 128, c['S5W'] // 128, c['DNW'] // 128
    jobs = []
    for j in range(D // 128):
        jobs.append(dict(j=j, groups=[[(W['mla_w_o'], j * 128, 128, 0, k1)],
                                      [(W['s5_w_glu'], j * 128, 128, k1, k2)],
                                      [(W['s5_w_glu'], D + j * 128, 128, k1, k2)],
                                      [(W['dn_w_o'], j * 128, 128, k1 + k2, k3)]]))

    def alloc(kb, st):
        return dict(g=[kb.sb(st, name + '_g%d' % i, [128, 3, TG], F32) for i in range(2)],
                    t=[kb.sb(st, name + '_t%d' % i, [128, TG], F32) for i in range(3)],
                    o=[kb.sb(st, name + '_o%d' % i, [128, TG], BF16) for i in range(2)], n=[0])

    def ep(kb, job, ps, g, t0, ctx):
        i = ctx['n'][0]
        ctx['n'][0] += 1
        gt_ = ctx['g'][i % 2]
        t1, t2, t3 = ctx['t']
        o = ctx['o'][i % 2]
        j = job['j']
        for b in range(3):
            r0 = b * D + j * 128
            S.dma('sp', gt_[:, b, :], gT[r0:r0 + 128, t0:t0 + TG], reads=[kb.dres['gT']], writes=[gt_.res])
        S.op('act', lambda e: e.activation(t1[:], ps[2][:, 0:TG], AF.Sigmoid), reads=[ps[2].res], writes=[t1.res])
        S.op('dve', lambda e: e.tensor_tensor(t1[:], t1[:], ps[1][:, 0:TG], ALU.mult), reads=[t1.res, ps[1].res], writes=[t1.res])
        S.op('dve', lambda e: e.tensor_tensor(t1[:], t1[:], gt_[:, 1, :], ALU.mult), reads=[t1.res, gt_.res], writes=[t1.res])
        S.op('dve', lambda e: e.tensor_tensor(t2[:], ps[0][:, 0:TG], gt_[:, 0, :], ALU.mult), reads=[ps[0].res, gt_.res], writes=[t2.res])
        S.op('dve', lambda e: e.tensor_tensor(t3[:], ps[3][:, 0:TG], gt_[:, 2, :], ALU.mult), reads=[ps[3].res, gt_.res], writes=[t3.res])
        S.op('dve', lambda e: e.tensor_tensor(t1[:], t1[:], t2[:], ALU.add), reads=[t1.res, t2.res], writes=[t1.res])
        S.op('dve', lambda e: e.tensor_tensor(o[:], t1[:], t3[:], ALU.add), reads=[t1.res, t3.res], writes=[o.res])
        S.dma('sp', mergedT[j * 128:(j + 1) * 128, t0:t0 + TG], o[:], reads=[o.res], writes=[kb.dres['mergedT']])
    gemm_phase(kb, name, [(oT, 'oT'), (s5hT, 's5hT'), (dnoT, 'dnoT')], jobs, ep, SG=c.get('SG1', 4), NB=2,
               nbanks_per_job=4, ep_alloc=alloc)


def plain_gemm(kb, name, X, xname, Wd, K_, N_, out, oname, SG, NB):
    c, S = kb.c, kb.S
    TG = c['TG']
    nk = K_ // 128
    jobs = [dict(j=j, groups=[[(Wd, j * 128, 128, 0, nk)]]) for j in range(N_ // 128)]

    def alloc(kb, st):
        return dict(o=[kb.sb(st, name + '_o%d' % i, [128, TG], F32) for i in range(3)], n=[0])

    def ep(kb, job, ps, g, t0, ctx):
        i = ctx['n'][0]
        ctx['n'][0] += 1
        o = ctx['o'][i % 3]
        if i % 2 == 0:
            S.op('act', lambda e: e.copy(o[:], ps[0][:, 0:TG]), reads=[ps[0].res], writes=[o.res])
        else:
            S.op('dve', lambda e: e.tensor_copy(o[:], ps[0][:, 0:TG]), reads=[ps[0].res], writes=[o.res])
        j = job['j']
        S.dma('sp', out[j * 128:(j + 1) * 128, t0:t0 + TG], o[:], reads=[o.res], writes=[kb.dres[oname]])
    gemm_phase(kb, name, [(X, xname)], jobs, ep, SG=SG, NB=NB, nbanks_per_job=1, ep_alloc=alloc)


PER_LAYER = ['ffn1_w13', 'ffn1_w2', 'w_in', 'mla_q_norm_g', 'mla_kv_norm_g', 'mla_w_uq', 'mla_w_ukv', 'mla_w_o',
             's5_a_re', 's5_a_im', 's5_log_dt', 's5_b_re', 's5_b_im', 's5_c_re', 's5_c_im', 's5_d', 's5_w_glu',
             'dn_conv_w', 'dn_a_log', 'dn_dt_bias', 'dn_out_norm_g', 'dn_w_o', 'w_out', 'ffn2_w13', 'ffn2_w2']


RELAID = ('ffn1_w13', 'ffn1_w2', 'w_in', 'mla_w_o', 's5_w_glu', 'dn_w_o', 'w_out', 'ffn2_w13', 'ffn2_w2')


def relay_weight(W, tiles):
    K_, N_ = W.shape
    nkt = K_ // 128
    if all(m == 128 for _, m in tiles):
        return np.ascontiguousarray(W.reshape(nkt, 128, N_ // 128, 128).transpose(1, 2, 0, 3).reshape(128, nkt * N_))
    out = np.empty((128, nkt * N_), np.float32)
    W3 = W.reshape(nkt, 128, N_)
    for c0, m in tiles:
        out[:, nkt * c0:nkt * (c0 + m)] = W3[:, :, c0:c0 + m].transpose(1, 0, 2).reshape(128, nkt * m)
    return out


def weight_shapes(c):
    D, F = c['D'], c['F']
    return dict(ffn1_w13=[D, 2 * F], ffn1_w2=[F, D], w_in=[D, c['NIN']], mla_q_norm_g=[c['QL']],
                mla_kv_norm_g=[c['KVL']], mla_w_uq=[c['QL'], c['H'] * c['QK']],
                mla_w_ukv=[c['KVL'], c['H'] * (c['NOPE'] + c['VD'])], mla_w_o=[c['MLAW'], D],
                s5_a_re=[c['S5NG'], c['S5P']], s5_a_im=[c['S5NG'], c['S5P']], s5_log_dt=[c['S5NG']],
                s5_b_re=[c['S5NG'], c['S5P'], c['S5G']], s5_b_im=[c['S5NG'], c['S5P'], c['S5G']],
                s5_c_re=[c['S5NG'], c['S5G'], c['S5P']], s5_c_im=[c['S5NG'], c['S5G'], c['S5P']],
                s5_d=[c['S5NG'], c['S5G']], s5_w_glu=[c['S5W'], 2 * D],
                dn_conv_w=[c['DNCONV'], 2 * c['DNQK'] + c['DNW']], dn_a_log=[c['DNH']], dn_dt_bias=[c['DNH']],
                dn_out_norm_g=[c['DV']], dn_w_o=[c['DNW'], D], w_out=[D, D], ffn2_w13=[D, 2 * F], ffn2_w2=[F, D])


def build_program(cfg, debug=()):
    c = derive(cfg)
    kb = K(c)
    kb.dbg = debug
    S = kb.S
    D, F, LP, DEPTH = c['D'], c['F'], c['LP'], c['DEPTH']
    xT = kb.dram('xT', [D, LP], F32, kind="ExternalInput")
    sg = kb.dram('sandwich_g', [DEPTH * 6, D], F32, kind="ExternalInput")
    Wl = []
    shp = weight_shapes(c)
    for l in range(DEPTH):
        wd = {}
        for n in PER_LAYER:
            if n in RELAID:
                K_, N_ = shp[n]
                wd[n] = kb.dram('%s_%d' % (n, l), [128, (K_ // 128) * N_], F32, kind="ExternalInput")
                kb.relaid.add(id(wd[n]))
            else:
                wd[n] = kb.dram('%s_%d' % (n, l), shp[n], F32, kind="ExternalInput")
        Wl.append(wd)
    outT = kb.dram('outT', [D, LP], F32, kind="ExternalOutput")

    def scratch(name, shape, dt):
        return kb.dram(name, shape, dt, kind=("ExternalOutput" if name in debug else "Internal"))
    hT = scratch('hT', [D, LP], F32)
    nT = scratch('nT', [D, LP], BF16)
    hidT = scratch('hidT', [F, LP], BF16)
    fT = scratch('fT', [D, LP], F32)
    pT = scratch('pT', [c['OFF_GATE'], LP], F32)
    gT = scratch('gT', [3 * D, LP], F32)
    scr = dict(cqnT=scratch('cqnT', [c['QL'], LP], BF16), ckvnT=scratch('ckvnT', [c['KVL'], LP], BF16),
               qnT=scratch('qnT', [c['H'] * c['NOPE'], LP], BF16), qrT=scratch('qrT', [c['H'] * c['ROPE'], LP], BF16),
               knT=scratch('knT', [c['H'] * c['NOPE'], LP], BF16), krT=scratch('krT', [c['ROPE'], LP], BF16),
               Vtok=scratch('Vtok', [LP, c['MLAW']], BF16), oT=scratch('oT', [c['MLAW'], LP], BF16),
               csT=scratch('csT', [c['ROPE'] // 2, 2, LP], F32))
    s5hT = scratch('s5hT', [c['S5W'], LP], BF16)
    bgT = scratch('bgT', [2 * c['DNH'], LP], F32)
    dnoT = scratch('dnoT', [c['DNW'], LP], BF16)
    mergedT = scratch('mergedT', [D, LP], BF16)

    rope_tables(kb, scr['csT'])
    with ExitStack() as st:
        G = [load_gain(kb, st, 'sg%d' % i, sg[i, :], D) for i in range(DEPTH * 6)]
        hsrc = (xT, 'xT', 0)
        norm_phase(kb, 'n_in', D, h=hsrc, g_pre=G[0], n_out=(nT, 'nT', 0))
        for l in range(DEPTH):
            W = Wl[l]
            g = G[6 * l:6 * l + 6]
            last = (l == DEPTH - 1)
            ffn_phase(kb, 'f1_%d' % l, nT, W['ffn1_w13'], W['ffn1_w2'], hidT, fT)
            norm_phase(kb, 'n1_%d' % l, D, h=hsrc, y=(fT, 'fT', 0), coef=0.5, g_post=g[1], g_pre=g[2],
                       n_out=(nT, 'nT', 0), h_out=(hT, 'hT', 0))
            hsrc = (hT, 'hT', 0)
            inproj_phase(kb, 'ip_%d' % l, nT, W['w_in'], pT, gT)
            mla_phase(kb, 'mla_%d' % l, l, W, pT, scr)
            s5_phase(kb, 's5_%d' % l, l, W, pT, s5hT)
            dn_phase(kb, 'dn_%d' % l, l, W, pT, bgT, dnoT)
            merge_phase(kb, 'mg_%d' % l, W, gT, scr['oT'], s5hT, dnoT, mergedT)
            plain_gemm(kb, 'wo_%d' % l, mergedT, 'mergedT', W['w_out'], D, D, fT, 'fT', SG=c.get('SG1', 4), NB=4)
            norm_phase(kb, 'n3_%d' % l, D, h=hsrc, y=(fT, 'fT', 0), coef=1.0, g_post=g[3], g_pre=g[4],
                       n_out=(nT, 'nT', 0), h_out=(hT, 'hT', 0))
            ffn_phase(kb, 'f2_%d' % l, nT, W['ffn2_w13'], W['ffn2_w2'], hidT, fT)
            if last:
                norm_phase(kb, 'n5_%d' % l, D, h=hsrc, y=(fT, 'fT', 0), coef=0.5, g_post=g[5],
                           h_out=(outT, 'outT', 0))
            else:
                norm_phase(kb, 'n5_%d' % l, D, h=hsrc, y=(fT, 'fT', 0), coef=0.5, g_post=g[5], g_pre=G[6 * (l + 1)],
                           n_out=(nT, 'nT', 0), h_out=(hT, 'hT', 0))
    S.barrier()
    S.finish()
    return kb


def make_in_maps(c, inputs, ncores):
    c = derive(c)
    D, L, LP = c['D'], c['L'], c['LP']
    x = np.asarray(inputs['x'], dtype=np.float32)
    meta = np.asarray(inputs['meta_tokens'], dtype=np.float32)
    shared = {'sandwich_g': np.ascontiguousarray(np.asarray(inputs['sandwich_g'], np.float32).reshape(-1, D))}
    for l in range(c['DEPTH']):
        for n in PER_LAYER:
            w = np.asarray(inputs[n], np.float32)[l]
            if n in RELAID:
                tiles = inproj_tiles(c) if n == 'w_in' else [(i, 128) for i in range(0, w.shape[1], 128)]
                shared['%s_%d' % (n, l)] = relay_weight(w, tiles)
            else:
                shared['%s_%d' % (n, l)] = np.ascontiguousarray(w)
    maps = []
    for b in range(ncores):
        xT = np.zeros((D, LP), np.float32)
        xT[:, :c['NMETA']] = meta.T
        xT[:, c['NMETA']:L] = x[b].T
        m = dict(shared)
        m['xT'] = xT
        maps.append(m)
    return maps


_CACHE = {}


def kernel(**inputs):
    cfg = full_cfg()
    c = derive(cfg)
    B = np.asarray(inputs['x']).shape[0]
    if 'kb' not in _CACHE:
        _CACHE['kb'] = build_program(cfg)
    kb = _CACHE['kb']
    maps = make_in_maps(cfg, inputs, B)
    res = run_bass_kernel_spmd(kb.nc, maps, core_ids=list(range(B)))
    out = np.stack([np.ascontiguousarray(res.results[b]['outT'][:, c['NMETA']:c['L']].T) for b in range(B)], axis=0)
    return out.astype(np.float32)
```

```python
import math
from contextlib import ExitStack
import numpy as np
import concourse.bass as bass
import concourse.mybir as mybir
from concourse.bass_utils import run_bass_kernel_spmd

F32 = mybir.dt.float32
BF16 = mybir.dt.bfloat16
AF = mybir.ActivationFunctionType
ALU = mybir.AluOpType
EPS = 1e-6


def full_cfg():
    return dict(D=4096, SEQ=4096, DEPTH=2, NMETA=16, CHUNK=64, F=11008,
                H=16, QL=1024, KVL=512, NOPE=128, ROPE=64, VD=128,
                S5W=1024, S5G=16, S5P=64, DNH=8, DK=128, DV=128, DNCONV=4,
                TG=384, THETA=10000.0)


def derive(c):
    c = dict(c)
    c['L'] = c['SEQ'] + c['NMETA']
    c['LP'] = -(-c['L'] // c['TG']) * c['TG']
    c['NG'] = c['LP'] // c['TG']
    c['NT'] = c['LP'] // 128
    c['QK'] = c['NOPE'] + c['ROPE']
    c['MLAW'] = c['H'] * c['VD']
    c['S5NG'] = c['S5W'] // c['S5G']
    c['DNQK'] = c['DNH'] * c['DK']
    c['DNW'] = c['DNH'] * c['DV']
    o = 0
    c['OFF_Q'] = o; o += c['QL']
    c['OFF_KV'] = o; o += c['KVL']
    c['OFF_KR'] = o; o += c['ROPE']
    c['OFF_S5'] = o; o += c['S5W']
    c['OFF_DN_QKV'] = o; o += 2 * c['DNQK'] + c['DNW']
    c['OFF_DN_Z'] = o; o += c['DNW']
    c['OFF_DN_A'] = o; o += c['DNH']
    c['OFF_DN_B'] = o; o += c['DNH']
    c['OFF_GATE'] = o; o += 3 * c['D']
    c['NIN'] = o
    return c


class Res:
    __slots__ = ('w', 'r', 'name')

    def __init__(self, name=''):
        self.w = {}
        self.r = {}
        self.name = name


class Sched:
    NDS = 24

    def __init__(self, nc, stack):
        self.nc = nc
        self.eng = {'pe': nc.tensor, 'act': nc.scalar, 'dve': nc.vector,
                    'pool': nc.gpsimd, 'sp': nc.sync}
        self.sem = {}
        for k in ('pe', 'act', 'dve', 'pool'):
            self.sem[k] = stack.enter_context(nc.semaphore('s_' + k))
        for i in range(self.NDS):
            self.sem[('d', i)] = stack.enter_context(nc.semaphore('s_d%d' % i))
        self.cnt = {k: 0 for k in self.sem}
        self.waited = {e: {} for e in self.eng}
        self.nd = 0
        self.nins = 0

    def _hazards(self, eng, reads, writes, waits, is_dma=False):
        def need(tok):
            if tok is None:
                return
            k, v = tok
            if eng == 'pe' and k == 'pe':
                return
            if self.waited[eng].get(k, 0) >= v:
                return
            if waits.get(k, 0) < v:
                waits[k] = v
        for r in reads:
            for k, v in r.w.items():
                need((k, v))
        for w in writes:
            if w.r:
                for k, v in w.r.items():
                    need((k, v))
                for k, v in w.w.items():
                    need((k, v))
            else:
                for k, v in w.w.items():
                    if k == eng or (is_dma and isinstance(k, tuple)):
                        continue
                    need((k, v))

    def _emit_waits(self, eng, waits):
        E = self.eng[eng]
        for k, v in waits.items():
            E.wait_ge(self.sem[k], v)
            self.waited[eng][k] = v
            self.nins += 1

    def _commit(self, tok, reads, writes):
        k, v = tok
        for r in reads:
            if r.r.get(k, 0) < v:
                r.r[k] = v
        for w in writes:
            if w.r:
                w.w = {}
                w.r = {}
            if w.w.get(k, 0) < v:
                w.w[k] = v

    def op(self, eng, fn, reads=(), writes=()):
        waits = {}
        self._hazards(eng, reads, writes, waits)
        self._emit_waits(eng, waits)
        ins = fn(self.eng[eng])
        self.cnt[eng] += 1
        ins.then_inc(self.sem[eng], 1)
        self.nins += 1
        self._commit((eng, self.cnt[eng]), reads, writes)

    def dma(self, q, out, in_, reads=(), writes=()):
        i = self.nd % self.NDS
        self.nd += 1
        key = ('d', i)
        waits = {}
        self._hazards(q, reads, writes, waits, is_dma=True)
        prev = self.cnt[key]
        if prev > 0 and self.waited[q].get(key, 0) < prev and waits.get(key, 0) < prev:
            waits[key] = prev
        self._emit_waits(q, waits)
        ins = self.eng[q].dma_start(out=out, in_=in_)
        self.cnt[key] += 16
        ins.then_inc(self.sem[key], 16)
        self.nins += 1
        self._commit((key, self.cnt[key]), reads, writes)

    def barrier(self):
        for e in self.eng:
            for k, v in self.cnt.items():
                if v > 0 and self.waited[e].get(k, 0) < v and not (e == k):
                    self.eng[e].wait_ge(self.sem[k], v)
                    self.waited[e][k] = v
                    self.nins += 1

    def finish(self):
        E = self.eng['sp']
        for i in range(self.NDS):
            key = ('d', i)
            if self.cnt[key] > 0:
                E.wait_ge(self.sem[key], self.cnt[key])
        for k in ('pe', 'act', 'dve', 'pool'):
            if self.cnt[k] > 0:
                E.wait_ge(self.sem[k], self.cnt[k])


class T:
    def __init__(self, ap_handle, name=''):
        self.t = ap_handle
        self.res = Res(name)

    def __getitem__(self, k):
        return self.t[k]


class K:
    def __init__(self, cfg):
        self.c = cfg
        self.nc = bass.Bass("TRN2", target_bir_lowering=False)
        self.root = ExitStack()
        self.S = Sched(self.nc, self.root)
        self.dres = {}
        self.relaid = set()
        self.blocked = set()

    def dram(self, name, shape, dt, kind="Internal"):
        t = self.nc.dram_tensor(name, list(shape), dt, kind=kind).ap()
        self.dres[name] = Res(name)
        return t

    def dump(self, name, t, shape, dt=F32):
        if name not in getattr(self, 'dbg', ()):
            return
        d = self.nc.dram_tensor('dbg_' + name, list(shape), dt, kind="ExternalOutput").ap()
        self.dres['dbg_' + name] = Res(name)
        self.S.dma('sp', d, t[:], reads=[t.res], writes=[self.dres['dbg_' + name]])

    def sb(self, stack, name, shape, dt):
        return T(stack.enter_context(self.nc.sbuf_tensor(name, list(shape), dt)), name)

    def ps(self, stack, name, shape, dt=F32):
        return T(stack.enter_context(self.nc.psum_tensor(name, list(shape), dt)), name)


def rstd_from_sumsq(kb, out_ap, out_res, ss_ap, ss_res, D, eps=EPS):
    S = kb.S
    S.op('dve', lambda e: e.tensor_scalar(out_ap, ss_ap, 1.0 / D, eps, ALU.mult, ALU.add),
         reads=[ss_res], writes=[out_res])
    S.op('act', lambda e: e.activation(out_ap, out_ap, AF.Sqrt), reads=[out_res], writes=[out_res])
    S.op('dve', lambda e: e.reciprocal(out_ap, out_ap), reads=[out_res], writes=[out_res])


def chunks(n, m):
    return [(i, min(m, n - i)) for i in range(0, n, m)]


def gemm_phase(kb, name, xsrcs, jobs, epilogue, SG, NB, nbanks_per_job, ep_alloc=None):
    c, S, nc = kb.c, kb.S, kb.nc
    TG, NG, LP = c['TG'], c['NG'], c['LP']
    xk = []
    for ap, rn in xsrcs:
        Ki = ap.shape[0]
        for r0, rows in chunks(Ki, 128):
            xk.append((ap, rn, r0, rows))
    nxk = len(xk)
    for j in jobs:
        segs = []
        for gi, grp in enumerate(j['groups']):
            for (W, c0, M, xkt0, nkt) in grp:
                segs.append((gi, W, c0, M, xkt0, nkt))
        j['segs'] = segs
        j['wcols'] = sum(s[3] * s[5] for s in segs)
    maxw = max(sum(j['wcols'] for j in jobs[b:b + NB]) for b in range(0, len(jobs), NB))
    with ExitStack() as st:
        XS = kb.sb(st, name + '_xs', [128, nxk, SG * TG], BF16)
        WB = [kb.sb(st, name + '_wb%d' % i, [128, maxw], BF16) for i in range(2)]
        nps = 8 // nbanks_per_job
        nps = min(nps, 4)
        PS = [[kb.ps(st, name + '_ps%d_%d' % (i, b), [128, 512]) for b in range(nbanks_per_job)]
              for i in range(nps)]
        ctx = ep_alloc(kb, st) if ep_alloc else None
        pscnt = 0
        wbcnt = 0
        for sg0 in range(0, NG, SG):
            ngs = min(SG, NG - sg0)
            for i, (ap, rn, r0, rows) in enumerate(xk):
                S.dma('sp', XS[0:rows, i, 0:ngs * TG], ap[r0:r0 + rows, sg0 * TG:(sg0 + ngs) * TG],
                      reads=[kb.dres[rn]], writes=[XS.res])
            for b0 in range(0, len(jobs), NB):
                blk = jobs[b0:b0 + NB]
                wb = WB[wbcnt % 2]
                wbcnt += 1
                off = 0
                for j in blk:
                    j['woff'] = []
                    for (gi, W, c0, M, xkt0, nkt) in j['segs']:
                        j['woff'].append(off)
                        if id(W) in kb.relaid:
                            S.dma('pool', wb[:, off:off + nkt * M], W[:, nkt * c0:nkt * c0 + nkt * M],
                                  reads=[], writes=[wb.res])
                        else:
                            dst = wb[:, off:off + nkt * M].rearrange("p (k m) -> p k m", m=M)
                            for k0, kn in chunks(nkt, 8):
                                src = W[k0 * 128:(k0 + kn) * 128, c0:c0 + M].rearrange("(k p) m -> p k m", p=128)
                                S.dma('pool', dst[:, k0:k0 + kn, :], src, reads=[], writes=[wb.res])
                        off += nkt * M
                for j in blk:
                    for g in range(ngs):
                        ps = PS[pscnt % nps]
                        pscnt += 1
                        ng = len(j['groups'])
                        for gi in range(ng):
                            segs = [(s, o) for s, o in zip(j['segs'], j['woff']) if s[0] == gi]
                            mm = []
                            for (s, o) in segs:
                                (_, W, c0, M, xkt0, nkt) = s
                                for kt in range(nkt):
                                    rows = xk[xkt0 + kt][3]
                                    mm.append((o + kt * M, M, xkt0 + kt, rows))

                            def fn(pe, mm=mm, ps=ps, gi=gi, g=g, wb=wb):
                                ins = None
                                for idx, (wo, M, xi, rows) in enumerate(mm):
                                    ins = pe.matmul(ps[gi][0:M, 0:TG], wb[0:rows, wo:wo + M],
                                                    XS[0:rows, xi, g * TG:(g + 1) * TG],
                                                    start=(idx == 0), stop=(idx == len(mm) - 1))
                                return ins
                            S.op('pe', fn, reads=[wb.res, XS.res], writes=[ps[gi].res])
                        epilogue(kb, j, ps, sg0 + g, (sg0 + g) * TG, ctx)
    S.barrier()


def norm_phase(kb, name, D, h, y=None, coef=1.0, g_post=None, g_pre=None,
               n_out=None, h_out=None):
    c, S = kb.c, kb.S
    LP = c['LP']
    NW = c['TG'] // 2
    assert D % 128 == 0 and LP % NW == 0
    nkt = D // 128
    with ExitStack() as st:
        Hh = [kb.sb(st, name + '_h%d' % i, [128, nkt, NW], F32) for i in range(2)]
        Yy = [kb.sb(st, name + '_y%d' % i, [128, nkt, NW], F32) for i in range(2)] if y is not None else None
        SQs = [kb.sb(st, name + '_sq%d' % i, [128, nkt, NW], F32) for i in range(2)]
        Nns = [kb.sb(st, name + '_n%d' % i, [128, nkt, NW], BF16) for i in range(2)] if n_out is not None else None
        ones = kb.sb(st, name + '_ones', [128, 128], F32)
        rss = [kb.sb(st, name + '_rs%d' % i, [128, NW], F32) for i in range(2)]
        pss = [kb.ps(st, name + '_ps%d' % i, [128, 512]) for i in range(2)]
        S.op('dve', lambda e: e.memset(ones[:], 1.0), writes=[ones.res])

        def stats(src, SQ, ps, rs):
            for kt in range(nkt):
                S.op('act', lambda e, kt=kt: e.activation(SQ[:, kt, :], src[:, kt, :], AF.Square),
                     reads=[src.res], writes=[SQ.res])

            def fn(pe):
                ins = None
                for kt in range(nkt):
                    ins = pe.matmul(ps[:, 0:NW], ones[:], SQ[:, kt, :], start=(kt == 0), stop=(kt == nkt - 1))
                return ins
            S.op('pe', fn, reads=[ones.res, SQ.res], writes=[ps.res])
            rstd_from_sumsq(kb, rs[:], rs.res, ps[:, 0:NW], ps.res, D)

        def view(t, t0):
            ap, rn, r0 = t
            if rn in kb.blocked:
                return ap[t0 // NW].rearrange("p (k t) -> p k t", t=NW), kb.dres[rn]
            return ap[r0:r0 + D, t0:t0 + NW].rearrange("(k p) t -> p k t", p=128), kb.dres[rn]

        for g in range(LP // NW):
            t0 = g * NW
            bi = g % 2
            H_, SQ, ps, rs = Hh[bi], SQs[bi], pss[bi], rss[bi]
            v, r = view(h, t0)
            S.dma('sp', H_[:], v, reads=[r], writes=[H_.res])
            if y is not None:
                Y = Yy[bi]
                v, r = view(y, t0)
                S.dma('sp', Y[:], v, reads=[r], writes=[Y.res])
                stats(Y, SQ, ps, rs)
                for kt in range(nkt):
                    S.op('dve', lambda e, kt=kt: e.scalar_tensor_tensor(
                        SQ[:, kt, :], Y[:, kt, :], g_post[:, kt:kt + 1], rs[:], ALU.mult, ALU.mult),
                        reads=[Y.res, rs.res], writes=[SQ.res])
                    S.op('dve', lambda e, kt=kt: e.scalar_tensor_tensor(
                        H_[:, kt, :], SQ[:, kt, :], float(coef), H_[:, kt, :], ALU.mult, ALU.add),
                        reads=[SQ.res, H_.res], writes=[H_.res])
                if h_out is not None:
                    v, r = view(h_out, t0)
                    S.dma('sp', v, H_[:], reads=[H_.res], writes=[r])
            if n_out is not None:
                Nn = Nns[bi]
                stats(H_, SQ, ps, rs)
                for kt in range(nkt):
                    S.op('dve', lambda e, kt=kt: e.scalar_tensor_tensor(
                        Nn[:, kt, :], H_[:, kt, :], g_pre[:, kt:kt + 1], rs[:], ALU.mult, ALU.mult),
                        reads=[H_.res, rs.res], writes=[Nn.res])
                v, r = view(n_out, t0)
                S.dma('sp', v, Nn[:], reads=[Nn.res], writes=[r])
    S.barrier()


def ffn_phase(kb, name, nT, w13, w2, hidT, fT):
    c, S = kb.c, kb.S
    D, F, TG = c['D'], c['F'], c['TG']
    nkt = D // 128
    jobs = []
    for j in range(F // 128):
        jobs.append(dict(groups=[[(w13, j * 128, 128, 0, nkt)], [(w13, F + j * 128, 128, 0, nkt)]], j=j))

    def alloc1(kb, st):
        return dict(sa=[kb.sb(st, name + '_sa%d' % i, [128, TG], F32) for i in range(2)],
                    ho=[kb.sb(st, name + '_ho%d' % i, [128, TG], BF16) for i in range(3)], n=[0])

    def ep1(kb, job, ps, g, t0, ctx):
        i = ctx['n'][0]
        ctx['n'][0] += 1
        sa = ctx['sa'][i % 2]
        ho = ctx['ho'][i % 3]
        S.op('act', lambda e: e.activation(sa[:], ps[0][:, 0:TG], AF.Silu), reads=[ps[0].res], writes=[sa.res])
        S.op('dve', lambda e: e.tensor_tensor(ho[:], sa[:], ps[1][:, 0:TG], ALU.mult),
             reads=[sa.res, ps[1].res], writes=[ho.res])
        j = job['j']
        S.dma('sp', hidT[j * 128:(j + 1) * 128, t0:t0 + TG], ho[:], reads=[ho.res], writes=[kb.dres['hidT']])
    gemm_phase(kb, name + 'a', [(nT, 'nT')], jobs, ep1, SG=c.get('SG1', 4), NB=c.get('NB1', 2),
               nbanks_per_job=2, ep_alloc=alloc1)

    fkt = F // 128
    jobs2 = [dict(groups=[[(w2, j * 128, 128, 0, fkt)]], j=j) for j in range(D // 128)]

    def alloc2(kb, st):
        return dict(o=[kb.sb(st, name + '_fo%d' % i, [128, TG], F32) for i in range(3)], n=[0])

    def ep2(kb, job, ps, g, t0, ctx):
        i = ctx['n'][0]
        ctx['n'][0] += 1
        o = ctx['o'][i % 3]
        eng = 'act' if i % 2 == 0 else 'dve'
        if eng == 'act':
            S.op('act', lambda e: e.copy(o[:], ps[0][:, 0:TG]), reads=[ps[0].res], writes=[o.res])
        else:
            S.op('dve', lambda e: e.tensor_copy(o[:], ps[0][:, 0:TG]), reads=[ps[0].res], writes=[o.res])
        j = job['j']
        store_blocked(kb, fT, 'fT', j, t0, o)
    gemm_phase(kb, name + 'b', [(hidT, 'hidT')], jobs2, ep2, SG=c.get('SG2', 2), NB=1,
               nbanks_per_job=1, ep_alloc=alloc2)


def store_blocked(kb, dst, dname, j, t0, o):
    TG = kb.c['TG']
    NW = TG // 2
    for hh in range(2):
        g = (t0 + hh * NW) // NW
        kb.S.dma('sp', dst[g][:, j * NW:(j + 1) * NW], o[:, hh * NW:(hh + 1) * NW],
                 reads=[o.res], writes=[kb.dres[dname]])


def load_gain(kb, st, name, src_row_ap, D):
    nkt = D // 128
    t = kb.sb(st, name, [128, nkt], F32)
    v = src_row_ap.rearrange("(k p) -> p k", p=128)
    for k0, kn in chunks(nkt, 8):
        with kb.nc.allow_non_contiguous_dma("gain vector transpose load"):
            kb.S.dma('sp', t[:, k0:k0 + kn], v[:, k0:k0 + kn], writes=[t.res])
    return t


def rope_tables(kb, csT):
    c, S = kb.c, kb.S
    LP = c['LP']
    R2 = c['ROPE'] // 2
    I32 = mybir.dt.int32
    with ExitStack() as st:
        pi_ = kb.sb(st, 'rt_pi', [R2, LP], I32)
        pf = kb.sb(st, 'rt_pf', [R2, LP], F32)
        ii = kb.sb(st, 'rt_ii', [R2, 1], I32)
        inv = kb.sb(st, 'rt_inv', [R2, 1], F32)
        ang = kb.sb(st, 'rt_ang', [R2, LP], F32)
        cs = kb.sb(st, 'rt_cs', [R2, 2, LP], F32)
        ki = kb.sb(st, 'rt_ki', [R2, LP], I32)
        kf = kb.sb(st, 'rt_kf', [R2, LP], F32)
        S.op('pool', lambda e: e.iota(pi_[:], [[1, LP]], 0, 0), writes=[pi_.res])
        S.op('pool', lambda e: e.iota(ii[:], [[1, 1]], 0, 1), writes=[ii.res])
        S.op('dve', lambda e: e.tensor_copy(pf[:], pi_[:]), reads=[pi_.res], writes=[pf.res])
        S.op('dve', lambda e: e.tensor_copy(inv[:], ii[:]), reads=[ii.res], writes=[inv.res])
        S.op('act', lambda e: e.activation(inv[:], inv[:], AF.Exp, scale=-math.log(c['THETA']) / R2),
             reads=[inv.res], writes=[inv.res])
        S.op('dve', lambda e: e.tensor_scalar(ang[:], pf[:], inv[:, 0:1], None, ALU.mult),
             reads=[pf.res, inv.res], writes=[ang.res])
        for j, sh in enumerate((0.5 * math.pi, 0.0)):
            S.op('dve', lambda e, j=j, sh=sh: e.tensor_scalar(cs[:, j, :], ang[:], sh, None, ALU.add),
                 reads=[ang.res], writes=[cs.res])
            S.op('dve', lambda e, j=j: e.tensor_scalar(kf[:], cs[:, j, :], 1.0 / (2 * math.pi), None, ALU.mult),
                 reads=[cs.res], writes=[kf.res])
            S.op('dve', lambda e: e.tensor_copy(ki[:], kf[:]), reads=[kf.res], writes=[ki.res])
            S.op('dve', lambda e: e.tensor_copy(kf[:], ki[:]), reads=[ki.res], writes=[kf.res])
            S.op('dve', lambda e, j=j: e.scalar_tensor_tensor(cs[:, j, :], kf[:], -2 * math.pi, cs[:, j, :],
                                                              ALU.mult, ALU.add),
                 reads=[kf.res, cs.res], writes=[cs.res])
            S.op('dve', lambda e, j=j: e.tensor_scalar(cs[:, j, :], cs[:, j, :], -3.14159, 3.14159, ALU.max, ALU.min),
                 reads=[cs.res], writes=[cs.res])
            S.op('act', lambda e, j=j: e.activation(cs[:, j, :], cs[:, j, :], AF.Sin), reads=[cs.res], writes=[cs.res])
        S.dma('sp', csT, cs[:], reads=[cs.res], writes=[kb.dres['csT']])
    S.barrier()


def inproj_tiles(c):
    bounds = [c['OFF_Q'], c['OFF_KV'], c['OFF_KR'], c['OFF_S5'], c['OFF_DN_QKV'], c['OFF_DN_Z'],
              c['OFF_DN_A'], c['OFF_GATE'], c['NIN']]
    tiles = []
    for a, b in zip(bounds[:-1], bounds[1:]):
        for c0, m in chunks(b - a, 128):
            tiles.append((a + c0, m))
    return tiles


def inproj_phase(kb, name, nT, w_in, pT, gT):
    c, S = kb.c, kb.S
    D, TG = c['D'], c['TG']
    nkt = D // 128
    jobs = []
    for c0, m in inproj_tiles(c):
        jobs.append(dict(groups=[[(w_in, c0, m, 0, nkt)]], c0=c0, m=m, gate=(c0 >= c['OFF_GATE'])))

    def alloc(kb, st):
        return dict(o=[kb.sb(st, name + '_o%d' % i, [128, TG], F32) for i in range(4)], n=[0])

    def ep(kb, job, ps, g, t0, ctx):
        i = ctx['n'][0]
        ctx['n'][0] += 1
        o = ctx['o'][i % 4]
        m, c0 = job['m'], job['c0']
        if job['gate']:
            S.op('act', lambda e: e.activation(o[0:m, :], ps[0][0:m, 0:TG], AF.Sigmoid),
                 reads=[ps[0].res], writes=[o.res])
        else:
            S.op('dve', lambda e: e.tensor_copy(o[0:m, :], ps[0][0:m, 0:TG]), reads=[ps[0].res], writes=[o.res])
        if job['gate']:
            r0 = c0 - c['OFF_GATE']
            S.dma('sp', gT[r0:r0 + m, t0:t0 + TG], o[0:m, :], reads=[o.res], writes=[kb.dres['gT']])
        else:
            S.dma('sp', pT[c0:c0 + m, t0:t0 + TG], o[0:m, :], reads=[o.res], writes=[kb.dres['pT']])
    gemm_phase(kb, name, [(nT, 'nT')], jobs, ep, SG=c.get('SG1', 4), NB=4, nbanks_per_job=1, ep_alloc=alloc)


def mla_phase(kb, name, l, W, pT, scr):
    c, S = kb.c, kb.S
    TG, NG, LP, NT = c['TG'], c['NG'], c['LP'], c['NT']
    H, QL, KVL, NOPE, ROPE, VD = c['H'], c['QL'], c['KVL'], c['NOPE'], c['ROPE'], c['VD']
    R2 = ROPE // 2
    QKD = NOPE + ROPE
    scale = QKD ** -0.5
    cqnT, ckvnT, qnT, qrT, knT, krT, Vtok, oT, csT = (scr[k] for k in
                                                       ('cqnT', 'ckvnT', 'qnT', 'qrT', 'knT', 'krT', 'Vtok', 'oT', 'csT'))
    with ExitStack() as st:
        gq = load_gain(kb, st, name + '_gq', W['mla_q_norm_g'], QL)
        gkv = load_gain(kb, st, name + '_gkv', W['mla_kv_norm_g'], KVL)
        norm_phase(kb, name + '_nq', QL, h=(pT, 'pT', c['OFF_Q']), g_pre=gq, n_out=(cqnT, 'cqnT', 0))
        norm_phase(kb, name + '_nkv', KVL, h=(pT, 'pT', c['OFF_KV']), g_pre=gkv, n_out=(ckvnT, 'ckvnT', 0))
    S.barrier()
    with ExitStack() as st:
        cs = kb.sb(st, name + '_cs', [R2, 2, LP], F32)
        x12 = kb.sb(st, name + '_x12', [R2, 2, LP], F32)
        t1 = kb.sb(st, name + '_t1', [R2, LP], F32)
        t2 = kb.sb(st, name + '_t2', [R2, LP], F32)
        kr = kb.sb(st, name + '_kr', [R2, 2, LP], BF16)
        S.dma('sp', cs[:], csT, reads=[kb.dres['csT']], writes=[cs.res])
        for j in range(2):
            S.dma('sp', x12[:, j, :], pT[c['OFF_KR'] + j * R2:c['OFF_KR'] + (j + 1) * R2, :],
                  reads=[kb.dres['pT']], writes=[x12.res])
        rd = [x12.res, cs.res]
        S.op('dve', lambda e: e.tensor_tensor(t1[:], x12[:, 0, :], cs[:, 0, :], ALU.mult), reads=rd, writes=[t1.res])
        S.op('dve', lambda e: e.tensor_tensor(t2[:], x12[:, 1, :], cs[:, 1, :], ALU.mult), reads=rd, writes=[t2.res])
        S.op('dve', lambda e: e.tensor_tensor(kr[:, 0, :], t1[:], t2[:], ALU.subtract),
             reads=[t1.res, t2.res], writes=[kr.res])
        S.op('dve', lambda e: e.tensor_tensor(t1[:], x12[:, 1, :], cs[:, 0, :], ALU.mult),
             reads=rd + [kr.res], writes=[t1.res])
        S.op('dve', lambda e: e.tensor_tensor(t2[:], x12[:, 0, :], cs[:, 1, :], ALU.mult),
             reads=rd + [kr.res], writes=[t2.res])
        S.op('dve', lambda e: e.tensor_tensor(kr[:, 1, :], t1[:], t2[:], ALU.add),
             reads=[t1.res, t2.res], writes=[kr.res])
        for j in range(2):
            S.dma('sp', krT[j * R2:(j + 1) * R2, :], kr[:, j, :], reads=[kr.res], writes=[kb.dres['krT']])
    S.barrier()
    nq = QL // 128
    jobs = []
    for h in range(H):
        jobs.append(dict(groups=[[(W['mla_w_uq'], h * QKD, NOPE, 0, nq)]], h=h, kind='n'))
        jobs.append(dict(groups=[[(W['mla_w_uq'], h * QKD + NOPE, R2, 0, nq)],
                                 [(W['mla_w_uq'], h * QKD + NOPE + R2, R2, 0, nq)]], h=h, kind='r'))

    def alloc_q(kb, st):
        cs = kb.sb(st, name + '_qcs', [R2, 2, LP], F32)
        S.dma('sp', cs[:], csT, reads=[kb.dres['csT']], writes=[cs.res])
        S.op('dve', lambda e: e.tensor_scalar(cs[:], cs[:], scale, None, ALU.mult), reads=[cs.res], writes=[cs.res])
        return dict(cs=cs, o=[kb.sb(st, name + '_qo%d' % i, [128, TG], BF16) for i in range(3)],
                    r=[kb.sb(st, name + '_qr%d' % i, [R2, 2, TG], BF16) for i in range(2)],
                    t=[kb.sb(st, name + '_qt%d' % i, [R2, TG], F32) for i in range(2)], n=[0])

    def ep_q(kb, job, ps, g, t0, ctx):
        i = ctx['n'][0]
        ctx['n'][0] += 1
        h = job['h']
        if job['kind'] == 'n':
            o = ctx['o'][i % 3]
            S.op('act', lambda e: e.activation(o[:], ps[0][:, 0:TG], AF.Copy, scale=scale),
                 reads=[ps[0].res], writes=[o.res])
            S.dma('sp', qnT[h * NOPE:(h + 1) * NOPE, t0:t0 + TG], o[:], reads=[o.res], writes=[kb.dres['qnT']])
        else:
            cs = ctx['cs']
            r = ctx['r'][i % 2]
            t1, t2 = ctx['t']
            x1, x2 = ps[0][0:R2, 0:TG], ps[1][0:R2, 0:TG]
            cc, ss = cs[:, 0, t0:t0 + TG], cs[:, 1, t0:t0 + TG]
            rd = [ps[0].res, ps[1].res, cs.res]
            S.op('dve', lambda e: e.tensor_tensor(t1[:], x1, cc, ALU.mult), reads=rd, writes=[t1.res])
            S.op('dve', lambda e: e.tensor_tensor(t2[:], x2, ss, ALU.mult), reads=rd, writes=[t2.res])
            S.op('dve', lambda e: e.tensor_tensor(r[:, 0, :], t1[:], t2[:], ALU.subtract),
                 reads=[t1.res, t2.res], writes=[r.res])
            S.op('dve', lambda e: e.tensor_tensor(t1[:], x2, cc, ALU.mult), reads=rd + [r.res], writes=[t1.res])
            S.op('dve', lambda e: e.tensor_tensor(t2[:], x1, ss, ALU.mult), reads=rd + [r.res], writes=[t2.res])
            S.op('dve', lambda e: e.tensor_tensor(r[:, 1, :], t1[:], t2[:], ALU.add),
                 reads=[t1.res, t2.res], writes=[r.res])
            for j in range(2):
                S.dma('sp', qrT[h * ROPE + j * R2:h * ROPE + (j + 1) * R2, t0:t0 + TG], r[:, j, :],
                      reads=[r.res], writes=[kb.dres['qrT']])
    gemm_phase(kb, name + '_q', [(cqnT, 'cqnT')], jobs, ep_q, SG=NG, NB=2, nbanks_per_job=2, ep_alloc=alloc_q)
    nkv = KVL // 128
    jobs = [dict(groups=[[(W['mla_w_ukv'], h * (NOPE + VD), NOPE, 0, nkv)]], h=h) for h in range(H)]

    def alloc_k(kb, st):
        return dict(o=[kb.sb(st, name + '_ko%d' % i, [128, TG], BF16) for i in range(3)], n=[0])

    def ep_k(kb, job, ps, g, t0, ctx):
        i = ctx['n'][0]
        ctx['n'][0] += 1
        o = ctx['o'][i % 3]
        h = job['h']
        S.op('act', lambda e: e.copy(o[:], ps[0][:, 0:TG]), reads=[ps[0].res], writes=[o.res])
        S.dma('sp', knT[h * NOPE:(h + 1) * NOPE, t0:t0 + TG], o[:], reads=[o.res], writes=[kb.dres['knT']])
    gemm_phase(kb, name + '_k', [(ckvnT, 'ckvnT')], jobs, ep_k, SG=NG, NB=4, nbanks_per_job=1, ep_alloc=alloc_k)
    with ExitStack() as st:
        X = kb.sb(st, name + '_vx', [128, nkv, LP], BF16)
        Wv = kb.sb(st, name + '_vw', [128, nkv, H * VD], BF16)
        vo = [kb.sb(st, name + '_vo%d' % i, [128, 512], BF16) for i in range(3)]
        pv = [kb.ps(st, name + '_vps%d' % i, [128, 512]) for i in range(4)]
        for k in range(nkv):
            S.dma('sp', X[:, k, :], ckvnT[k * 128:(k + 1) * 128, :], reads=[kb.dres['ckvnT']], writes=[X.res])
        for h in range(H):
            c0 = h * (NOPE + VD) + NOPE
            S.dma('pool', Wv[:, :, h * VD:(h + 1) * VD],
                  W['mla_w_ukv'][:, c0:c0 + VD].rearrange("(k p) m -> p k m", p=128), writes=[Wv.res])
        cnt = 0
        HV = H * VD
        for t in range(NT):
            for c0, cw in chunks(HV, 512):
                ps = pv[cnt % 4]
                o = vo[cnt % 3]
                cnt += 1

                def fn(pe, ps=ps, t=t, c0=c0, cw=cw):
                    ins = None
                    for k in range(nkv):
                        ins = pe.matmul(ps[:, 0:cw], X[:, k, t * 128:(t + 1) * 128], Wv[:, k, c0:c0 + cw],
                                        start=(k == 0), stop=(k == nkv - 1))
                    return ins
                S.op('pe', fn, reads=[X.res, Wv.res], writes=[ps.res])
                S.op('act', lambda e, o=o, ps=ps, cw=cw: e.copy(o[:, 0:cw], ps[:, 0:cw]), reads=[ps.res], writes=[o.res])
                S.dma('sp', Vtok[t * 128:(t + 1) * 128, c0:c0 + cw], o[:, 0:cw], reads=[o.res], writes=[kb.dres['Vtok']])
    S.barrier()
    gt = TG // 128
    npart = gt + 1
    with ExitStack() as st:
        masks = kb.sb(st, name + '_mask', [128, npart, TG], BF16)
        onesb = kb.sb(st, name + '_ones', [128, 128], BF16)
        Kr = kb.sb(st, name + '_Kr', [ROPE, LP], BF16)
        Qn = [kb.sb(st, name + '_Qn%d' % i, [128, LP], BF16) for i in range(2)]
        Qr = [kb.sb(st, name + '_Qr%d' % i, [ROPE, LP], BF16) for i in range(2)]
        Kn = [kb.sb(st, name + '_Kn%d' % i, [128, LP], BF16) for i in range(2)]
        Vh = [kb.sb(st, name + '_Vh%d' % i, [128, NT, VD], BF16) for i in range(2)]
        PT = [kb.sb(st, name + '_PT%d' % i, [128, TG], BF16) for i in range(3)]
        rden = kb.sb(st, name + '_rden', [128, TG], F32)
        oo = [kb.sb(st, name + '_oo%d' % i, [128, TG], BF16) for i in range(2)]
        pS = [kb.ps(st, name + '_pS%d' % i, [128, 512]) for i in range(3)]
        pO = [kb.ps(st, name + '_pO%d' % i, [128, 512]) for i in range(2)]
        pD = [kb.ps(st, name + '_pD%d' % i, [128, 512]) for i in range(2)]
        S.op('dve', lambda e: e.memset(onesb[:], 1.0), writes=[onesb.res])
        S.op('dve', lambda e: e.memset(masks[:], 0.0), writes=[masks.res])
        for j in range(npart):
            for m in range(-1, (TG - 16) // 64 + 1):
                q0 = max(0, 16 + 64 * m)
                q1 = min(TG, 80 + 64 * m)
                thr = min(128, 80 + 64 * m - 128 * j)
                if thr > 0 and q1 > q0:
                    S.op('dve', lambda e, j=j, thr=thr, q0=q0, q1=q1: e.memset(masks[0:thr, j, q0:q1], 1.0),
                         writes=[masks.res])
        S.dma('sp', Kr[:], krT, reads=[kb.dres['krT']], writes=[Kr.res])
        cnt = 0
        for h in range(H):
            b = h % 2
            S.dma('sp', Qn[b][:], qnT[h * NOPE:(h + 1) * NOPE, :], reads=[kb.dres['qnT']], writes=[Qn[b].res])
            S.dma('sp', Qr[b][:], qrT[h * ROPE:(h + 1) * ROPE, :], reads=[kb.dres['qrT']], writes=[Qr[b].res])
            S.dma('sp', Kn[b][:], knT[h * NOPE:(h + 1) * NOPE, :], reads=[kb.dres['knT']], writes=[Kn[b].res])
            S.dma('sp', Vh[b][:], Vtok[:, h * VD:(h + 1) * VD].rearrange("(t p) d -> p t d", p=128),
                  reads=[kb.dres['Vtok']], writes=[Vh[b].res])
            steps = []
            for g in range(NG):
                kts = list(range(0, min(gt * g + npart, NT)))
                for ki, kt in enumerate(kts):
                    steps.append((g, ki, kt, len(kts)))

            def emit_score(step, idx):
                g, ki, kt, nk = step
                q0 = g * TG
                ps = pS[idx % 3]
                pt = PT[idx % 3]

                def fs(pe):
                    pe.matmul(ps[:, 0:TG], Kn[b][:, kt * 128:(kt + 1) * 128], Qn[b][:, q0:q0 + TG],
                              start=True, stop=False)
                    return pe.matmul(ps[:, 0:TG], Kr[:, kt * 128:(kt + 1) * 128], Qr[b][:, q0:q0 + TG],
                                     start=False, stop=True)
                S.op('pe', fs, reads=[Kn[b].res, Qn[b].res, Kr.res, Qr[b].res], writes=[ps.res])
                S.op('act', lambda e: e.activation(pt[:], ps[:, 0:TG], AF.Exp), reads=[ps.res], writes=[pt.res])
                j = kt - gt * g
                if j >= 0:
                    S.op('dve', lambda e: e.tensor_tensor(pt[:], pt[:], masks[:, j, :], ALU.mult),
                         reads=[pt.res, masks.res], writes=[pt.res])

            def emit_pv(step, idx):
                g, ki, kt, nk = step
                q0 = g * TG
                pt = PT[idx % 3]
                po, pd = pO[g % 2], pD[g % 2]
                first, last = (ki == 0), (ki == nk - 1)

                def fo(pe):
                    pe.matmul(po[:, 0:TG], Vh[b][:, kt, :], pt[:], start=first, stop=last)
                    return pe.matmul(pd[:, 0:TG], onesb[:], pt[:], start=first, stop=last)
                S.op('pe', fo, reads=[Vh[b].res, pt.res, onesb.res], writes=[po.res, pd.res])
                if last:
                    o = oo[g % 2]
                    S.op('dve', lambda e: e.reciprocal(rden[:], pd[:, 0:TG]), reads=[pd.res], writes=[rden.res])
                    S.op('dve', lambda e: e.tensor_tensor(o[:], po[:, 0:TG], rden[:], ALU.mult),
                         reads=[po.res, rden.res], writes=[o.res])
                    S.dma('sp', oT[h * VD:(h + 1) * VD, q0:q0 + TG], o[:], reads=[o.res], writes=[kb.dres['oT']])

            for i, stp in enumerate(steps):
                emit_score(stp, cnt + i)
                if i > 0:
                    emit_pv(steps[i - 1], cnt + i - 1)
            emit_pv(steps[-1], cnt + len(steps) - 1)
            cnt += len(steps)
    S.barrier()


def s5_phase(kb, name, l, W, pT, s5hT):
    c, S = kb.c, kb.S
    LP = c['LP']
    G, P, NGr, SW = c['S5G'], c['S5P'], c['S5NG'], c['S5W']
    assert P == 64 and G == 16
    NST = NGr // 2
    CH = 512
    with ExitStack() as st:
        def sbt(n, shape, dt=F32):
            return kb.sb(st, name + '_' + n, shape, dt)
        ar, ai, dl = sbt('ar', [128, NST]), sbt('ai', [128, NST]), sbt('dl', [128, NST])
        mag, th, co, si = sbt('mag', [128, NST]), sbt('th', [128, NST]), sbt('co', [128, NST]), sbt('si', [128, NST])
        abr, abi, den = sbt('abr', [128, NST]), sbt('abi', [128, NST]), sbt('den', [128, NST])
        zr, zi, t1, t2 = sbt('zr', [128, NST]), sbt('zi', [128, NST]), sbt('t1', [128, NST]), sbt('t2', [128, NST])
        ki = sbt('ki', [128, NST], mybir.dt.int32)
        nl = W['s5_log_dt']
        with kb.nc.allow_non_contiguous_dma("small s5 parameter loads"):
            for two in range(2):
                S.dma('sp', ar[two * 64:(two + 1) * 64, :],
                      W['s5_a_re'].rearrange("(s two) p -> two p s", two=2)[two], writes=[ar.res])
                S.dma('sp', ai[two * 64:(two + 1) * 64, :],
                      W['s5_a_im'].rearrange("(s two) p -> two p s", two=2)[two], writes=[ai.res])
                S.dma('sp', dl[two * 64:(two + 1) * 64, :],
                      nl.rearrange("(s two) -> two s", two=2)[two:two + 1, :].broadcast_to([64, NST]),
                      writes=[dl.res])

        def ew(eng, fn, reads, writes):
            S.op(eng, fn, reads=[r.res for r in reads], writes=[w.res for w in writes])

        def sincos(th_, co_, si_):
            y, y2, p_ = t1, t2, zr
            ew('dve', lambda e: e.tensor_scalar(y[:], th_[:], 1.0 / (2 * math.pi), None, ALU.mult), [th_], [y])
            ew('dve', lambda e: e.tensor_copy(ki[:], y[:]), [y], [ki])
            ew('dve', lambda e: e.tensor_copy(y[:], ki[:]), [ki], [y])
            ew('dve', lambda e: e.scalar_tensor_tensor(y[:], y[:], -2 * math.pi, th_[:], ALU.mult, ALU.add), [y, th_], [y])
            ew('dve', lambda e: e.tensor_scalar(y[:], y[:], 0.125, None, ALU.mult), [y], [y])
            ew('dve', lambda e: e.tensor_tensor(y2[:], y[:], y[:], ALU.mult), [y], [y2])
            ew('dve', lambda e: e.tensor_scalar(p_[:], y2[:], 1.0 / 362880, None, ALU.mult), [y2], [p_])
            for a_ in (-1.0 / 5040, 1.0 / 120, -1.0 / 6):
                ew('dve', lambda e, a_=a_: e.scalar_tensor_tensor(p_[:], p_[:], a_, y2[:], ALU.add, ALU.mult), [p_, y2], [p_])
            ew('dve', lambda e: e.scalar_tensor_tensor(si_[:], p_[:], 1.0, y[:], ALU.add, ALU.mult), [p_, y], [si_])
            ew('dve', lambda e: e.tensor_scalar(p_[:], y2[:], -1.0 / 3628800, None, ALU.mult), [y2], [p_])
            for a_ in (1.0 / 40320, -1.0 / 720, 1.0 / 24, -0.5):
                ew('dve', lambda e, a_=a_: e.scalar_tensor_tensor(p_[:], p_[:], a_, y2[:], ALU.add, ALU.mult), [p_, y2], [p_])
            ew('dve', lambda e: e.tensor_scalar(co_[:], p_[:], 1.0, None, ALU.add), [p_], [co_])
            for _ in range(3):
                ew('dve', lambda e: e.tensor_tensor(y2[:], si_[:], si_[:], ALU.mult), [si_], [y2])
                ew('dve', lambda e: e.scalar_tensor_tensor(si_[:], si_[:], 2.0, co_[:], ALU.mult, ALU.mult), [si_, co_], [si_])
                ew('dve', lambda e: e.tensor_scalar(co_[:], y2[:], -2.0, 1.0, ALU.mult, ALU.add), [y2], [co_])

        ew('act', lambda e: e.activation(dl[:], dl[:], AF.Exp), [dl], [dl])
        ew('dve', lambda e: e.tensor_tensor(mag[:], ar[:], dl[:], ALU.mult), [ar, dl], [mag])
        ew('act', lambda e: e.activation(mag[:], mag[:], AF.Exp), [mag], [mag])
        ew('dve', lambda e: e.tensor_tensor(th[:], ai[:], dl[:], ALU.mult), [ai, dl], [th])
        sincos(th, co, si)
        ew('dve', lambda e: e.tensor_tensor(abr[:], mag[:], co[:], ALU.mult), [mag, co], [abr])
        ew('dve', lambda e: e.tensor_tensor(abi[:], mag[:], si[:], ALU.mult), [mag, si], [abi])
        ew('dve', lambda e: e.tensor_tensor(den[:], ar[:], ar[:], ALU.mult), [ar], [den])
        ew('dve', lambda e: e.tensor_tensor(t1[:], ai[:], ai[:], ALU.mult), [ai], [t1])
        ew('dve', lambda e: e.tensor_tensor(den[:], den[:], t1[:], ALU.add), [den, t1], [den])
        ew('dve', lambda e: e.reciprocal(den[:], den[:]), [den], [den])
        ew('dve', lambda e: e.tensor_scalar(t2[:], abr[:], -1.0, None, ALU.add), [abr], [t2])
        ew('dve', lambda e: e.tensor_tensor(zr[:], t2[:], ar[:], ALU.mult), [t2, ar], [zr])
        ew('dve', lambda e: e.tensor_tensor(t1[:], abi[:], ai[:], ALU.mult), [abi, ai], [t1])
        ew('dve', lambda e: e.tensor_tensor(zr[:], zr[:], t1[:], ALU.add), [zr, t1], [zr])
        ew('dve', lambda e: e.tensor_tensor(zr[:], zr[:], den[:], ALU.mult), [zr, den], [zr])
        ew('dve', lambda e: e.tensor_tensor(zi[:], abi[:], ar[:], ALU.mult), [abi, ar], [zi])
        ew('dve', lambda e: e.tensor_tensor(t1[:], t2[:], ai[:], ALU.mult), [t2, ai], [t1])
        ew('dve', lambda e: e.tensor_tensor(zi[:], zi[:], t1[:], ALU.subtract), [zi, t1], [zi])
        ew('dve', lambda e: e.tensor_tensor(zi[:], zi[:], den[:], ALU.mult), [zi, den], [zi])
        NLV = int(math.ceil(math.log2(LP)))
        pwr, pwi = sbt('pwr', [128, NLV, NST]), sbt('pwi', [128, NLV, NST])
        npwi = sbt('npwi', [128, NLV, NST])
        ew('dve', lambda e: e.tensor_copy(pwr[:, 0, :], abr[:]), [abr], [pwr])
        ew('dve', lambda e: e.tensor_copy(pwi[:, 0, :], abi[:]), [abi], [pwi])
        for k in range(1, NLV):
            ew('dve', lambda e, k=k: e.tensor_tensor(t1[:], pwr[:, k - 1, :], pwr[:, k - 1, :], ALU.mult), [pwr], [t1])
            ew('dve', lambda e, k=k: e.tensor_tensor(t2[:], pwi[:, k - 1, :], pwi[:, k - 1, :], ALU.mult), [pwi], [t2])
            ew('dve', lambda e, k=k: e.tensor_tensor(pwr[:, k, :], t1[:], t2[:], ALU.subtract), [t1, t2], [pwr])
            ew('dve', lambda e, k=k: e.tensor_tensor(t1[:], pwr[:, k - 1, :], pwi[:, k - 1, :], ALU.mult), [pwr, pwi], [t1])
            ew('dve', lambda e, k=k: e.tensor_scalar(pwi[:, k, :], t1[:], 2.0, None, ALU.mult), [t1], [pwi])
        ew('dve', lambda e: e.tensor_scalar(npwi[:], pwi[:], -1.0, None, ALU.mult), [pwi], [npwi])
        Bb_r, Bb_i = sbt('Bb_r', [128, NST, 32]), sbt('Bb_i', [128, NST, 32])
        Br, Bi = sbt('Br', [128, NST, 32]), sbt('Bi', [128, NST, 32])
        for t_ in (Br, Bi):
            ew('dve', lambda e, t_=t_: e.memset(t_[:], 0.0), [], [t_])
        with kb.nc.allow_non_contiguous_dma("small s5 parameter loads"):
            for two in range(2):
                for (src, dst) in ((W['s5_b_re'], Br), (W['s5_b_im'], Bi)):
                    v = src.rearrange("(s two) p c -> two p s c", two=2)[two]
                    for s0, sn in chunks(NST, 8):
                        S.dma('sp', dst[two * 64:(two + 1) * 64, s0:s0 + sn, two * 16:(two + 1) * 16],
                              v[:, s0:s0 + sn, :], writes=[dst.res])
        zrb = zr[:].unsqueeze(2).to_broadcast([128, NST, 32])
        zib = zi[:].unsqueeze(2).to_broadcast([128, NST, 32])
        tB = sbt('tB', [128, NST, 32])
        ew('dve', lambda e: e.tensor_tensor(Bb_r[:], Br[:], zrb, ALU.mult), [Br, zr], [Bb_r])
        ew('dve', lambda e: e.tensor_tensor(tB[:], Bi[:], zib, ALU.mult), [Bi, zi], [tB])
        ew('dve', lambda e: e.tensor_tensor(Bb_r[:], Bb_r[:], tB[:], ALU.subtract), [Bb_r, tB], [Bb_r])
        ew('dve', lambda e: e.tensor_tensor(Bb_i[:], Bi[:], zrb, ALU.mult), [Bi, zr], [Bb_i])
        ew('dve', lambda e: e.tensor_tensor(tB[:], Br[:], zib, ALU.mult), [Br, zi], [tB])
        ew('dve', lambda e: e.tensor_tensor(Bb_i[:], Bb_i[:], tB[:], ALU.add), [Bb_i, tB], [Bb_i])
        ident = sbt('ident', [128, 128])
        ew('dve', lambda e: e.memset(ident[:], 0.0), [], [ident])
        S.op('pool', lambda e: e.affine_select(ident[:], ident[:], [[-1, 128]], ALU.not_equal, 1.0, 0, 1),
             reads=[ident.res], writes=[ident.res])
        BT_r, BT_i = sbt('BT_r', [32, NST, 128]), sbt('BT_i', [32, NST, 128])
        ptr = [kb.ps(st, name + '_ptr%d' % i, [128, 512]) for i in range(2)]
        n = 0
        for (src, dst) in ((Bb_r, BT_r), (Bb_i, BT_i)):
            for s in range(NST):
                p_ = ptr[n % 2]
                n += 1
                S.op('pe', lambda e, p_=p_, src=src, s=s: e.transpose(p_[0:32, 0:128], src[:, s, :], ident[:]),
                     reads=[src.res, ident.res], writes=[p_.res])
                S.op('dve', lambda e, p_=p_, dst=dst, s=s: e.tensor_copy(dst[:, s, :], p_[0:32, 0:128]),
                     reads=[p_.res], writes=[dst.res])
        NT8 = NGr // 8
        Xc = sbt('Xc', [128, 2, 128])
        Cb_r, Cb_i = sbt('Cb_r', [128, NT8, 128]), sbt('Cb_i', [128, NT8, 128])
        for T8 in range(NT8):
            ew('dve', lambda e: e.memset(Xc[:], 0.0), [], [Xc])
            for ri, src in enumerate((W['s5_c_re'], W['s5_c_im'])):
                for gl in range(8):
                    S.dma('sp', Xc[16 * gl:16 * gl + 16, ri, (gl % 2) * 64:(gl % 2) * 64 + 64], src[8 * T8 + gl],
                          writes=[Xc.res])
            for ri, dst in enumerate((Cb_r, Cb_i)):
                p_ = ptr[n % 2]
                n += 1
                S.op('pe', lambda e, p_=p_, ri=ri: e.transpose(p_[:, 0:128], Xc[:, ri, :], ident[:]),
                     reads=[Xc.res, ident.res], writes=[p_.res])
                if ri == 0:
                    S.op('dve', lambda e, p_=p_, dst=dst, T8=T8: e.tensor_copy(dst[:, T8, :], p_[:, 0:128]),
                         reads=[p_.res], writes=[dst.res])
                else:
                    S.op('dve', lambda e, p_=p_, dst=dst, T8=T8: e.tensor_scalar(dst[:, T8, :], p_[:, 0:128], -1.0, None,
                                                                               ALU.mult),
                         reads=[p_.res], writes=[dst.res])
        dsk = sbt('dsk', [32, NST])
        with kb.nc.allow_non_contiguous_dma("small s5 parameter loads"):
            S.dma('sp', dsk[:], W['s5_d'].rearrange("(s two) c -> (two c) s", two=2), writes=[dsk.res])
        kb.dump('abr', abr, [128, NST]); kb.dump('abi', abi, [128, NST]); kb.dump('zr', zr, [128, NST]); kb.dump('zi', zi, [128, NST])
        kb.dump('Bb_r', Bb_r, [128, NST, 32]); kb.dump('BT_r', BT_r, [32, NST, 128]); kb.dump('Cb_r', Cb_r, [128, NT8, 128])
        kb.dump('pwr', pwr, [128, NLV, NST])
        u = [sbt('u%d' % i, [32, LP]) for i in range(2)]
        X = [[sbt('x%d%d' % (i, j), [128, LP]) for j in range(2)] for i in range(2)]
        tmp = sbt('ptmp', [128, LP])
        yo = [sbt('yo%d' % i, [32, CH]) for i in range(2)]
        y2 = sbt('y2', [32, CH])
        hb = [sbt('hb%d' % i, [32, CH], BF16) for i in range(2)]
        pb = [kb.ps(st, name + '_pb%d' % i, [128, 512]) for i in range(4)]
        pc = 0
        for s in range(NST):
            us = u[s % 2]
            S.dma('sp', us[:], pT[c['OFF_S5'] + 32 * s:c['OFF_S5'] + 32 * s + 32, :],
                  reads=[kb.dres['pT']], writes=[us.res])
            cur = 0
            for ri, BT in enumerate((BT_r, BT_i)):
                for c0, cw in chunks(LP, CH):
                    p_ = pb[pc % 4]
                    pc += 1
                    S.op('pe', lambda e, p_=p_, BT=BT, c0=c0, cw=cw, us=us, s=s:
                         e.matmul(p_[:, 0:cw], BT[:, s, :], us[:, c0:c0 + cw], start=True, stop=True),
                         reads=[BT.res, us.res], writes=[p_.res])
                    S.op('act', lambda e, p_=p_, ri=ri, c0=c0, cw=cw: e.copy(X[0][ri][:, c0:c0 + cw], p_[:, 0:cw]),
                         reads=[p_.res], writes=[X[0][ri].res])
            for k in range(NLV):
                d = 1 << k
                if d >= LP:
                    break
                a, b_ = X[cur], X[1 - cur]
                Pr, Pi, nPi = pwr[:, k, s:s + 1], pwi[:, k, s:s + 1], npwi[:, k, s:s + 1]
                n_ = LP - d
                ew('dve', lambda e: e.scalar_tensor_tensor(b_[0][:, d:LP], a[0][:, 0:n_], Pr, a[0][:, d:LP],
                                                           ALU.mult, ALU.add), [a[0], pwr], [b_[0]])
                ew('dve', lambda e: e.scalar_tensor_tensor(b_[0][:, d:LP], a[1][:, 0:n_], nPi, b_[0][:, d:LP],
                                                           ALU.mult, ALU.add), [a[1], npwi, b_[0]], [b_[0]])
                ew('act', lambda e: e.copy(b_[0][:, 0:d], a[0][:, 0:d]), [a[0]], [b_[0]])
                ew('dve', lambda e: e.scalar_tensor_tensor(b_[1][:, d:LP], a[1][:, 0:n_], Pr, a[1][:, d:LP],
                                                           ALU.mult, ALU.add), [a[1], pwr], [b_[1]])
                ew('dve', lambda e: e.scalar_tensor_tensor(b_[1][:, d:LP], a[0][:, 0:n_], Pi, b_[1][:, d:LP],
                                                           ALU.mult, ALU.add), [a[0], pwi, b_[1]], [b_[1]])
                ew('act', lambda e: e.copy(b_[1][:, 0:d], a[1][:, 0:d]), [a[1]], [b_[1]])
                cur = 1 - cur
            xr, xi = X[cur]
            if s == 0:
                kb.dump('xr0', xr, [128, LP]); kb.dump('bu0', X[0][0], [128, LP])
            T8, j4 = s // 4, s % 4
            for ci, (c0, cw) in enumerate(chunks(LP, CH)):
                p_ = pb[pc % 4]
                pc += 1
                yy, hh = yo[ci % 2], hb[ci % 2]

                def fy(pe, p_=p_, c0=c0, cw=cw, xr=xr, xi=xi, T8=T8, j4=j4):
                    pe.matmul(p_[0:32, 0:cw], Cb_r[:, T8, 32 * j4:32 * j4 + 32], xr[:, c0:c0 + cw], start=True, stop=False)
                    return pe.matmul(p_[0:32, 0:cw], Cb_i[:, T8, 32 * j4:32 * j4 + 32], xi[:, c0:c0 + cw],
                                     start=False, stop=True)
                S.op('pe', fy, reads=[Cb_r.res, Cb_i.res, xr.res, xi.res], writes=[p_.res])
                ew_r = [p_.res, us.res, dsk.res]
                S.op('dve', lambda e, p_=p_, yy=yy, c0=c0, cw=cw, us=us, s=s: e.scalar_tensor_tensor(
                    yy[:, 0:cw], us[:, c0:c0 + cw], dsk[:, s:s + 1], p_[0:32, 0:cw], ALU.mult, ALU.add),
                    reads=ew_r, writes=[yy.res])
                S.op('dve', lambda e, yy=yy, cw=cw: e.tensor_tensor(y2[:, 0:cw], yy[:, 0:cw], yy[:, 0:cw], ALU.mult),
                     reads=[yy.res], writes=[y2.res])
                S.op('dve', lambda e, cw=cw: e.tensor_scalar(y2[:, 0:cw], y2[:, 0:cw], 0.044715, 1.0, ALU.mult, ALU.add),
                     reads=[y2.res], writes=[y2.res])
                S.op('dve', lambda e, yy=yy, cw=cw: e.tensor_tensor(y2[:, 0:cw], y2[:, 0:cw], yy[:, 0:cw], ALU.mult),
                     reads=[y2.res, yy.res], writes=[y2.res])
                S.op('act', lambda e, cw=cw: e.activation(y2[:, 0:cw], y2[:, 0:cw], AF.Sigmoid, scale=1.5957691216057308),
                     reads=[y2.res], writes=[y2.res])
                S.op('dve', lambda e, yy=yy, hh=hh, cw=cw: e.tensor_tensor(hh[:, 0:cw], y2[:, 0:cw], yy[:, 0:cw], ALU.mult),
                     reads=[y2.res, yy.res], writes=[hh.res])
                S.dma('sp', s5hT[32 * s:32 * s + 32, c0:c0 + cw], hh[:, 0:cw], reads=[hh.res], writes=[kb.dres['s5hT']])
    S.barrier()


def make_ident(kb, st, name):
    ident = kb.sb(st, name, [128, 128], F32)
    kb.S.op('dve', lambda e: e.memset(ident[:], 0.0), writes=[ident.res])
    kb.S.op('pool', lambda e: e.affine_select(ident[:], ident[:], [[-1, 128]], ALU.not_equal, 1.0, 0, 1),
            reads=[ident.res], writes=[ident.res])
    return ident


def dn_phase(kb, name, l, W, pT, bgT, dnoT):
    c, S = kb.c, kb.S
    LP, NT = c['LP'], c['NT']
    NH, DK, DV = c['DNH'], c['DK'], c['DV']
    assert DK == 128 and DV == 128
    QKW = c['DNQK']
    CH = 512
    NLV = 7

    def ew(eng, fn, reads, writes):
        S.op(eng, fn, reads=[r.res for r in reads], writes=[w.res for w in writes])

    with ExitStack() as st:
        ab = kb.sb(st, name + '_ab', [NH, 2, LP], F32)
        gg = [kb.sb(st, name + '_gg%d' % i, [NH, LP], F32) for i in range(2)]
        prm = kb.sb(st, name + '_prm', [NH, 2], F32)
        for j, off in enumerate((c['OFF_DN_A'], c['OFF_DN_B'])):
            S.dma('sp', ab[:, j, :], pT[off:off + NH, :], reads=[kb.dres['pT']], writes=[ab.res])
        with kb.nc.allow_non_contiguous_dma("tiny"):
            S.dma('sp', prm[:, 0:1], W['dn_a_log'].rearrange("(h o) -> h o", o=1), writes=[prm.res])
            S.dma('sp', prm[:, 1:2], W['dn_dt_bias'].rearrange("(h o) -> h o", o=1), writes=[prm.res])
        ew('act', lambda e: e.activation(ab[:, 1, :], ab[:, 1, :], AF.Sigmoid), [ab], [ab])
        ew('act', lambda e: e.activation(prm[:, 0:1], prm[:, 0:1], AF.Exp), [prm], [prm])
        ew('dve', lambda e: e.tensor_scalar(prm[:, 0:1], prm[:, 0:1], -1.0, None, ALU.mult), [prm], [prm])
        ew('act', lambda e: e.activation(gg[0][:], ab[:, 0, :], AF.Exp, bias=prm[:, 1:2]), [ab, prm], [gg[0]])
        ew('act', lambda e: e.activation(gg[0][:], gg[0][:], AF.Ln, bias=1.0), [gg[0]], [gg[0]])
        ew('dve', lambda e: e.tensor_scalar(gg[0][:], gg[0][:], prm[:, 0:1], None, ALU.mult), [gg[0], prm], [gg[0]])
        cur = 0
        for k in range(NLV):
            d = 1 << k
            a3 = gg[cur][:].rearrange("h (t p) -> h t p", p=128)
            b3 = gg[1 - cur][:].rearrange("h (t p) -> h t p", p=128)
            ew('dve', lambda e: e.tensor_tensor(b3[:, :, d:128], a3[:, :, d:128], a3[:, :, 0:128 - d], ALU.add),
               [gg[cur]], [gg[1 - cur]])
            ew('dve', lambda e: e.tensor_copy(b3[:, :, 0:d], a3[:, :, 0:d]), [gg[cur]], [gg[1 - cur]])
            cur = 1 - cur
        S.dma('sp', bgT[0:NH, :], ab[:, 1, :], reads=[ab.res], writes=[kb.dres['bgT']])
        S.dma('sp', bgT[NH:2 * NH, :], gg[cur][:], reads=[gg[cur].res], writes=[kb.dres['bgT']])
    S.barrier()

    with ExitStack() as st:
        def sbt(n, shape, dt=F32):
            return kb.sb(st, name + '_' + n, shape, dt)
        ident = make_ident(kb, st, name + '_ident')
        ones = sbt('ones', [128, 128])
        ew('dve', lambda e: e.memset(ones[:], 1.0), [], [ones])
        m_incl, m_strict = sbt('m_incl', [128, 128]), sbt('m_strict', [128, 128])
        ew('dve', lambda e: e.memset(m_incl[:], 1.0), [], [m_incl])
        ew('dve', lambda e: e.memset(m_strict[:], 1.0), [], [m_strict])
        S.op('pool', lambda e: e.affine_select(m_incl[:], m_incl[:], [[-1, 128]], ALU.is_ge, 0.0, 0, 1),
             reads=[m_incl.res], writes=[m_incl.res])
        S.op('pool', lambda e: e.affine_select(m_strict[:], m_strict[:], [[-1, 128]], ALU.is_gt, 0.0, 0, 1),
             reads=[m_strict.res], writes=[m_strict.res])
        tm = sbt('tm', [128, NT, 2 * NH])
        with kb.nc.allow_non_contiguous_dma("token-major per-token scalars"):
            for t in range(NT):
                S.dma('sp', tm[:, t, :], bgT[:, t * 128:(t + 1) * 128].rearrange("r p -> p r"),
                      reads=[kb.dres['bgT']], writes=[tm.res])
        cwt = sbt('cwt', [128, 3, 4])
        gout = sbt('gout', [128, 1])
        with kb.nc.allow_non_contiguous_dma("tiny"):
            S.dma('sp', gout[:], W['dn_out_norm_g'].rearrange("(p o) -> p o", o=1), writes=[gout.res])
        qT, kT, vT = sbt('qT', [128, LP]), sbt('kT', [128, LP]), sbt('vT', [128, LP])
        tmp, gcrow, qe, oT = sbt('tmp', [128, LP]), sbt('gcrow', [128, LP]), sbt('qe', [128, LP]), sbt('oT', [128, LP])
        rn = sbt('rn', [128, CH])
        sc = sbt('sc', [128, 6, NT])
        Sst = [sbt('S%d' % i, [128, 128]) for i in range(2)]
        CS = []
        NIL = 2
        for q_ in range(NIL):
            CS.append(dict(
                Rt=[sbt('R%d_%d' % (i, q_), [128, 256]) for i in range(2)],
                kd=sbt('kd_%d' % q_, [128, 128]), vn=sbt('vn_%d' % q_, [128, 128]), wT=sbt('wT_%d' % q_, [128, 128]),
                E=sbt('E_%d' % q_, [128, 128]), Dm=sbt('Dm_%d' % q_, [128, 128]), Dms=sbt('Dms_%d' % q_, [128, 128]),
                QKm=sbt('QKm_%d' % q_, [128, 128]), QKT=sbt('QKT_%d' % q_, [128, 128]),
                Ak=[sbt('A%d_%d' % (i, q_), [128, 128]) for i in range(2)],
                Bk=[sbt('B%d_%d' % (i, q_), [128, 128]) for i in range(2)]))
        ob = [sbt('ob%d' % i, [128, CH], BF16) for i in range(2)]
        PSL = [kb.ps(st, name + '_ps%d' % i, [128, 512]) for i in range(8)]
        pcn = [0]

        def nps():
            p = PSL[pcn[0] % 8]
            pcn[0] += 1
            return p

        def colsumsq(src, D_, eps, dst_rn, c0, cw):
            p_ = nps()
            ew('act', lambda e: e.activation(tmp[:, c0:c0 + cw], src[:, c0:c0 + cw], AF.Square), [src], [tmp])
            ew('pe', lambda e: e.matmul(p_[:, 0:cw], ones[:], tmp[:, c0:c0 + cw], start=True, stop=True), [ones, tmp], [p_])
            rstd_from_sumsq(kb, dst_rn[:, 0:cw], dst_rn.res, p_[:, 0:cw], p_.res, D_, eps)

        for h in range(NH):
            for ti, (dst, base) in enumerate(((qT, 0), (kT, QKW), (vT, 2 * QKW))):
                ch0 = base + h * 128
                S.dma('sp', tmp[:], pT[c['OFF_DN_QKV'] + ch0:c['OFF_DN_QKV'] + ch0 + 128, :],
                      reads=[kb.dres['pT']], writes=[tmp.res])
                with kb.nc.allow_non_contiguous_dma("tiny"):
                    S.dma('sp', cwt[:, ti, :], W['dn_conv_w'][:, ch0:ch0 + 128].rearrange("j p -> p j"), writes=[cwt.res])
                ew('dve', lambda e: e.tensor_scalar(dst[:], tmp[:], cwt[:, ti, 3:4], None, ALU.mult), [tmp, cwt], [dst])
                for sft in (1, 2, 3):
                    ew('dve', lambda e, sft=sft: e.scalar_tensor_tensor(
                        dst[:, sft:LP], tmp[:, 0:LP - sft], cwt[:, ti, 3 - sft:4 - sft], dst[:, sft:LP], ALU.mult, ALU.add),
                        [tmp, cwt, dst], [dst])
                ew('act', lambda e: e.activation(dst[:], dst[:], AF.Silu), [dst], [dst])
            for dst, mul in ((qT, DK ** -0.5), (kT, 1.0)):
                for c0, cw in chunks(LP, CH):
                    colsumsq(dst, 1.0, EPS, rn, c0, cw)
                    ew('dve', lambda e, c0=c0, cw=cw: e.scalar_tensor_tensor(
                        dst[:, c0:c0 + cw], dst[:, c0:c0 + cw], float(mul), rn[:, 0:cw], ALU.mult, ALU.mult), [dst, rn], [dst])
            S.dma('sp', gcrow[:], bgT[NH + h:NH + h + 1, :].broadcast_to([128, LP]),
                  reads=[kb.dres['bgT']], writes=[gcrow.res])
            ew('act', lambda e: e.activation(qe[:], gcrow[:], AF.Exp), [gcrow], [qe])
            ew('dve', lambda e: e.tensor_copy(sc[:, 0, :], tm[:, :, h]), [tm], [sc])
            ew('dve', lambda e: e.tensor_copy(sc[:, 1, :], tm[:, :, NH + h]), [tm], [sc])
            ew('dve', lambda e: e.tensor_copy(sc[:, 4, :], qe[:].rearrange("p (t k) -> p t k", k=128)[:, :, 127]), [qe], [sc])
            ew('act', lambda e: e.activation(sc[:, 2, :], sc[:, 1, :], AF.Exp), [sc], [sc])
            ew('dve', lambda e: e.tensor_tensor(sc[:, 2, :], sc[:, 2, :], sc[:, 0, :], ALU.mult), [sc], [sc])
            ew('dve', lambda e: e.tensor_tensor(sc[:, 5, :], gcrow[:].rearrange("p (t k) -> p t k", k=128)[:, :, 127],
                                                sc[:, 1, :], ALU.subtract), [gcrow, sc], [sc])
            ew('act', lambda e: e.activation(sc[:, 3, :], sc[:, 5, :], AF.Exp), [sc], [sc])
            ew('dve', lambda e: e.tensor_tensor(qe[:], qe[:], qT[:], ALU.mult), [qe, qT], [qe])
            ew('dve', lambda e: e.memset(Sst[0][:], 0.0), [], [Sst[0]])
            def chunk_gen(ci):
                T_ = CS[ci % NIL]
                Rt, kd, vn, wT = T_['Rt'], T_['kd'], T_['vn'], T_['wT']
                E, Dm, Dms, QKm, QKT, Ak, Bk = T_['E'], T_['Dm'], T_['Dms'], T_['QKm'], T_['QKT'], T_['Ak'], T_['Bk']
                cs_ = slice(ci * 128, (ci + 1) * 128)
                beta_i, gc_i = sc[:, 0, ci:ci + 1], sc[:, 1, ci:ci + 1]
                beg_i, kds_i, egl = sc[:, 2, ci:ci + 1], sc[:, 3, ci:ci + 1], sc[:, 4, ci:ci + 1]
                pK, pV = nps(), nps()
                ew('pe', lambda e: e.transpose(pK[:, 0:128], kT[:, cs_], ident[:]), [kT, ident], [pK])
                ew('pe', lambda e: e.transpose(pV[:, 0:128], vT[:, cs_], ident[:]), [vT, ident], [pV])
                pG, pQ = nps(), nps()
                ew('pe', lambda e: e.matmul(pG[:, 0:128], kT[:, cs_], kT[:, cs_], start=True, stop=True), [kT], [pG])
                ew('pe', lambda e: e.matmul(pQ[:, 0:128], qT[:, cs_], kT[:, cs_], start=True, stop=True), [qT, kT], [pQ])
                ew('dve', lambda e: e.tensor_scalar(E[:], gcrow[:, cs_], gc_i, 0.0, ALU.subtract, ALU.max), [gcrow, sc], [E])
                ew('act', lambda e: e.activation(E[:], E[:], AF.Exp, scale=-1.0), [E], [E])
                yield
                X0 = Rt[0]
                ew('act', lambda e: e.activation(X0[:, 0:128], pV[:, 0:128], AF.Copy, scale=beta_i), [pV, sc], [X0])
                ew('dve', lambda e: e.tensor_scalar(X0[:, 128:256], pK[:, 0:128], beg_i, None, ALU.mult), [pK, sc], [X0])
                ew('dve', lambda e: e.tensor_scalar(kd[:], pK[:, 0:128], kds_i, None, ALU.mult), [pK, sc], [kd])
                ew('dve', lambda e: e.tensor_tensor(Dm[:], E[:], m_incl[:], ALU.mult), [E, m_incl], [Dm])
                ew('dve', lambda e: e.tensor_tensor(Dms[:], E[:], m_strict[:], ALU.mult), [E, m_strict], [Dms])
                A0, B0 = Ak[0], Bk[0]
                ew('dve', lambda e: e.scalar_tensor_tensor(A0[:], pG[:, 0:128], beta_i, Dms[:], ALU.mult, ALU.mult),
                   [pG, sc, Dms], [A0])
                ew('dve', lambda e: e.tensor_tensor(QKm[:], pQ[:, 0:128], Dm[:], ALU.mult), [pQ, Dm], [QKm])
                yield
                pB, pT_ = nps(), nps()
                ew('pe', lambda e: e.transpose(pB[:, 0:128], A0[:], ident[:]), [A0, ident], [pB])
                ew('pe', lambda e: e.transpose(pT_[:, 0:128], QKm[:], ident[:]), [QKm, ident], [pT_])
                yield
                ew('act', lambda e: e.copy(B0[:], pB[:, 0:128]), [pB], [B0])
                ew('act', lambda e: e.copy(QKT[:], pT_[:, 0:128]), [pT_], [QKT])
                yield
                xc = 0
                for k in range(NLV):
                    Ac, Bc = Ak[k % 2], Bk[k % 2]
                    An, Bn = Ak[(k + 1) % 2], Bk[(k + 1) % 2]
                    Xc, Xn = Rt[xc], Rt[1 - xc]
                    pY = nps()
                    ew('pe', lambda e: e.matmul(pY[:, 0:256], Bc[:], Xc[:], start=True, stop=True), [Bc, Xc], [pY])
                    if k < NLV - 1:
                        pA, pB2 = nps(), nps()
                        ew('pe', lambda e: e.matmul(pA[:, 0:128], Bc[:], Ac[:], start=True, stop=True), [Bc, Ac], [pA])
                        ew('pe', lambda e: e.matmul(pB2[:, 0:128], Ac[:], Bc[:], start=True, stop=True), [Ac, Bc], [pB2])
                    yield
                    ew('dve', lambda e: e.tensor_tensor(Xn[:], Xc[:], pY[:, 0:256], ALU.subtract if k == 0 else ALU.add),
                       [Xc, pY], [Xn])
                    xc = 1 - xc
                    if k < NLV - 1:
                        ew('act', lambda e: e.copy(An[:], pA[:, 0:128]), [pA], [An])
                        ew('dve', lambda e: e.tensor_copy(Bn[:], pB2[:, 0:128]), [pB2], [Bn])
                    yield
                Xf = Rt[xc]
                pW = nps()
                ew('pe', lambda e: e.transpose(pW[:, 0:128], Xf[:, 128:256], ident[:]), [Xf, ident], [pW])
                yield
                ew('act', lambda e: e.copy(wT[:], pW[:, 0:128]), [pW], [wT])
                yield
                Sc, Sn = Sst[ci % 2], Sst[(ci + 1) % 2]
                pv_, po_, pS_ = nps(), nps(), nps()
                ew('pe', lambda e: e.matmul(pv_[:, 0:128], wT[:], Sc[:], start=True, stop=True), [wT, Sc], [pv_])
                ew('dve', lambda e: e.tensor_tensor(vn[:], Xf[:, 0:128], pv_[:, 0:128], ALU.subtract), [Xf, pv_], [vn])

                def fo(e):
                    e.matmul(po_[:, 0:128], Sc[:], qe[:, cs_], start=True, stop=False)
                    return e.matmul(po_[:, 0:128], vn[:], QKT[:], start=False, stop=True)
                ew('pe', fo, [Sc, qe, vn, QKT], [po_])
                ew('pe', lambda e: e.matmul(pS_[:, 0:128], kd[:], vn[:], start=True, stop=True), [kd, vn], [pS_])
                ew('dve', lambda e: e.scalar_tensor_tensor(Sn[:], Sc[:], egl, pS_[:, 0:128], ALU.mult, ALU.add),
                   [Sc, sc, pS_], [Sn])
                ew('act', lambda e: e.copy(oT[:, cs_], po_[:, 0:128]), [po_], [oT])
                yield

            for c0_ in range(0, NT, NIL):
                gens = [chunk_gen(ci) for ci in range(c0_, min(c0_ + NIL, NT))]
                while gens:
                    for g_ in list(gens):
                        try:
                            next(g_)
                        except StopIteration:
                            gens.remove(g_)
            S.dma('sp', tmp[:], pT[c['OFF_DN_Z'] + h * 128:c['OFF_DN_Z'] + (h + 1) * 128, :],
                  reads=[kb.dres['pT']], writes=[tmp.res])
            ew('act', lambda e: e.activation(gcrow[:], tmp[:], AF.Silu), [tmp], [gcrow])
            for ci, (c0, cw) in enumerate(chunks(LP, CH)):
                colsumsq(oT, float(DV), EPS, rn, c0, cw)
                ew('dve', lambda e: e.scalar_tensor_tensor(oT[:, c0:c0 + cw], oT[:, c0:c0 + cw], gout[:, 0:1], rn[:, 0:cw],
                                                           ALU.mult, ALU.mult), [oT, gout, rn], [oT])
                o_ = ob[ci % 2]
                ew('dve', lambda e: e.tensor_tensor(o_[:, 0:cw], oT[:, c0:c0 + cw], gcrow[:, c0:c0 + cw], ALU.mult),
                   [oT, gcrow], [o_])
                S.dma('sp', dnoT[h * 128:(h + 1) * 128, c0:c0 + cw], o_[:, 0:cw], reads=[o_.res], writes=[kb.dres['dnoT']])
    S.barrier()


def merge_phase(kb, name, W, gT, oT, s5hT, dnoT, mergedT):
    c, S = kb.c, kb.S
    D, TG = c['D'], c['TG']
    k1, k2, k3 = c['MLAW'] // 128, c['S5W'] // 128, c['DNW'] // 128
    jobs = []
    for j in range(D // 128):
        jobs.append(dict(j=j, groups=[[(W['mla_w_o'], j * 128, 128, 0, k1)],
                                      [(W['s5_w_glu'], j * 128, 128, k1, k2)],
                                      [(W['s5_w_glu'], D + j * 128, 128, k1, k2)],
                                      [(W['dn_w_o'], j * 128, 128, k1 + k2, k3)]]))

    def alloc(kb, st):
        return dict(g=[kb.sb(st, name + '_g%d' % i, [128, 3, TG], F32) for i in range(2)],
                    t=[kb.sb(st, name + '_t%d' % i, [128, TG], F32) for i in range(3)],
                    o=[kb.sb(st, name + '_o%d' % i, [128, TG], BF16) for i in range(2)], n=[0])

    def ep(kb, job, ps, g, t0, ctx):
        i = ctx['n'][0]
        ctx['n'][0] += 1
        gt_ = ctx['g'][i % 2]
        t1, t2, t3 = ctx['t']
        o = ctx['o'][i % 2]
        j = job['j']
        for b in range(3):
            r0 = b * D + j * 128
            S.dma('sp', gt_[:, b, :], gT[r0:r0 + 128, t0:t0 + TG], reads=[kb.dres['gT']], writes=[gt_.res])
        S.op('act', lambda e: e.activation(t1[:], ps[2][:, 0:TG], AF.Sigmoid), reads=[ps[2].res], writes=[t1.res])
        S.op('dve', lambda e: e.tensor_tensor(t1[:], t1[:], ps[1][:, 0:TG], ALU.mult), reads=[t1.res, ps[1].res], writes=[t1.res])
        S.op('dve', lambda e: e.tensor_tensor(t1[:], t1[:], gt_[:, 1, :], ALU.mult), reads=[t1.res, gt_.res], writes=[t1.res])
        S.op('dve', lambda e: e.tensor_tensor(t2[:], ps[0][:, 0:TG], gt_[:, 0, :], ALU.mult), reads=[ps[0].res, gt_.res], writes=[t2.res])
        S.op('dve', lambda e: e.tensor_tensor(t3[:], ps[3][:, 0:TG], gt_[:, 2, :], ALU.mult), reads=[ps[3].res, gt_.res], writes=[t3.res])
        S.op('dve', lambda e: e.tensor_tensor(t1[:], t1[:], t2[:], ALU.add), reads=[t1.res, t2.res], writes=[t1.res])
        S.op('dve', lambda e: e.tensor_tensor(o[:], t1[:], t3[:], ALU.add), reads=[t1.res, t3.res], writes=[o.res])
        S.dma('sp', mergedT[j * 128:(j + 1) * 128, t0:t0 + TG], o[:], reads=[o.res], writes=[kb.dres['mergedT']])
    gemm_phase(kb, name, [(oT, 'oT'), (s5hT, 's5hT'), (dnoT, 'dnoT')], jobs, ep, SG=c.get('SG1', 4), NB=2,
               nbanks_per_job=4, ep_alloc=alloc)


def plain_gemm(kb, name, X, xname, Wd, K_, N_, out, oname, SG, NB):
    c, S = kb.c, kb.S
    TG = c['TG']
    nk = K_ // 128
    jobs = [dict(j=j, groups=[[(Wd, j * 128, 128, 0, nk)]]) for j in range(N_ // 128)]

    def alloc(kb, st):
        return dict(o=[kb.sb(st, name + '_o%d' % i, [128, TG], F32) for i in range(3)], n=[0])

    def ep(kb, job, ps, g, t0, ctx):
        i = ctx['n'][0]
        ctx['n'][0] += 1
        o = ctx['o'][i % 3]
        if i % 2 == 0:
            S.op('act', lambda e: e.copy(o[:], ps[0][:, 0:TG]), reads=[ps[0].res], writes=[o.res])
        else:
            S.op('dve', lambda e: e.tensor_copy(o[:], ps[0][:, 0:TG]), reads=[ps[0].res], writes=[o.res])
        j = job['j']
        if oname in kb.blocked:
            store_blocked(kb, out, oname, j, t0, o)
        else:
            S.dma('sp', out[j * 128:(j + 1) * 128, t0:t0 + TG], o[:], reads=[o.res], writes=[kb.dres[oname]])
    gemm_phase(kb, name, [(X, xname)], jobs, ep, SG=SG, NB=NB, nbanks_per_job=1, ep_alloc=alloc)


PER_LAYER = ['ffn1_w13', 'ffn1_w2', 'w_in', 'mla_q_norm_g', 'mla_kv_norm_g', 'mla_w_uq', 'mla_w_ukv', 'mla_w_o',
             's5_a_re', 's5_a_im', 's5_log_dt', 's5_b_re', 's5_b_im', 's5_c_re', 's5_c_im', 's5_d', 's5_w_glu',
             'dn_conv_w', 'dn_a_log', 'dn_dt_bias', 'dn_out_norm_g', 'dn_w_o', 'w_out', 'ffn2_w13', 'ffn2_w2']


RELAID = ('ffn1_w13', 'ffn1_w2', 'w_in', 'mla_w_o', 's5_w_glu', 'dn_w_o', 'w_out', 'ffn2_w13', 'ffn2_w2')


def relay_weight(W, tiles):
    K_, N_ = W.shape
    nkt = K_ // 128
    if all(m == 128 for _, m in tiles):
        return np.ascontiguousarray(W.reshape(nkt, 128, N_ // 128, 128).transpose(1, 2, 0, 3).reshape(128, nkt * N_))
    out = np.empty((128, nkt * N_), np.float32)
    W3 = W.reshape(nkt, 128, N_)
    for c0, m in tiles:
        out[:, nkt * c0:nkt * (c0 + m)] = W3[:, :, c0:c0 + m].transpose(1, 0, 2).reshape(128, nkt * m)
    return out


def weight_shapes(c):
    D, F = c['D'], c['F']
    return dict(ffn1_w13=[D, 2 * F], ffn1_w2=[F, D], w_in=[D, c['NIN']], mla_q_norm_g=[c['QL']],
                mla_kv_norm_g=[c['KVL']], mla_w_uq=[c['QL'], c['H'] * c['QK']],
                mla_w_ukv=[c['KVL'], c['H'] * (c['NOPE'] + c['VD'])], mla_w_o=[c['MLAW'], D],
                s5_a_re=[c['S5NG'], c['S5P']], s5_a_im=[c['S5NG'], c['S5P']], s5_log_dt=[c['S5NG']],
                s5_b_re=[c['S5NG'], c['S5P'], c['S5G']], s5_b_im=[c['S5NG'], c['S5P'], c['S5G']],
                s5_c_re=[c['S5NG'], c['S5G'], c['S5P']], s5_c_im=[c['S5NG'], c['S5G'], c['S5P']],
                s5_d=[c['S5NG'], c['S5G']], s5_w_glu=[c['S5W'], 2 * D],
                dn_conv_w=[c['DNCONV'], 2 * c['DNQK'] + c['DNW']], dn_a_log=[c['DNH']], dn_dt_bias=[c['DNH']],
                dn_out_norm_g=[c['DV']], dn_w_o=[c['DNW'], D], w_out=[D, D], ffn2_w13=[D, 2 * F], ffn2_w2=[F, D])


def build_program(cfg, debug=()):
    c = derive(cfg)
    kb = K(c)
    kb.dbg = debug
    S = kb.S
    D, F, LP, DEPTH = c['D'], c['F'], c['LP'], c['DEPTH']
    NW = c['TG'] // 2
    BSH = [LP // NW, 128, (D // 128) * NW]
    kb.blocked |= {'xT', 'hT', 'fT', 'outT'}
    xT = kb.dram('xT', BSH, F32, kind="ExternalInput")
    sg = kb.dram('sandwich_g', [DEPTH * 6, D], F32, kind="ExternalInput")
    Wl = []
    shp = weight_shapes(c)
    for l in range(DEPTH):
        wd = {}
        for n in PER_LAYER:
            if n in RELAID:
                K_, N_ = shp[n]
                wd[n] = kb.dram('%s_%d' % (n, l), [128, (K_ // 128) * N_], F32, kind="ExternalInput")
                kb.relaid.add(id(wd[n]))
            else:
                wd[n] = kb.dram('%s_%d' % (n, l), shp[n], F32, kind="ExternalInput")
        Wl.append(wd)
    outT = kb.dram('outT', BSH, F32, kind="ExternalOutput")

    def scratch(name, shape, dt):
        return kb.dram(name, shape, dt, kind=("ExternalOutput" if name in debug else "Internal"))
    hT = scratch('hT', BSH, F32)
    nT = scratch('nT', [D, LP], BF16)
    hidT = scratch('hidT', [F, LP], BF16)
    fT = scratch('fT', BSH, F32)
    pT = scratch('pT', [c['OFF_GATE'], LP], F32)
    gT = scratch('gT', [3 * D, LP], F32)
    scr = dict(cqnT=scratch('cqnT', [c['QL'], LP], BF16), ckvnT=scratch('ckvnT', [c['KVL'], LP], BF16),
               qnT=scratch('qnT', [c['H'] * c['NOPE'], LP], BF16), qrT=scratch('qrT', [c['H'] * c['ROPE'], LP], BF16),
               knT=scratch('knT', [c['H'] * c['NOPE'], LP], BF16), krT=scratch('krT', [c['ROPE'], LP], BF16),
               Vtok=scratch('Vtok', [LP, c['MLAW']], BF16), oT=scratch('oT', [c['MLAW'], LP], BF16),
               csT=scratch('csT', [c['ROPE'] // 2, 2, LP], F32))
    s5hT = scratch('s5hT', [c['S5W'], LP], BF16)
    bgT = scratch('bgT', [2 * c['DNH'], LP], F32)
    dnoT = scratch('dnoT', [c['DNW'], LP], BF16)
    mergedT = scratch('mergedT', [D, LP], BF16)

    rope_tables(kb, scr['csT'])
    with ExitStack() as st:
        G = [load_gain(kb, st, 'sg%d' % i, sg[i, :], D) for i in range(DEPTH * 6)]
        hsrc = (xT, 'xT', 0)
        norm_phase(kb, 'n_in', D, h=hsrc, g_pre=G[0], n_out=(nT, 'nT', 0))
        for l in range(DEPTH):
            W = Wl[l]
            g = G[6 * l:6 * l + 6]
            last = (l == DEPTH - 1)
            ffn_phase(kb, 'f1_%d' % l, nT, W['ffn1_w13'], W['ffn1_w2'], hidT, fT)
            norm_phase(kb, 'n1_%d' % l, D, h=hsrc, y=(fT, 'fT', 0), coef=0.5, g_post=g[1], g_pre=g[2],
                       n_out=(nT, 'nT', 0), h_out=(hT, 'hT', 0))
            hsrc = (hT, 'hT', 0)
            inproj_phase(kb, 'ip_%d' % l, nT, W['w_in'], pT, gT)
            mla_phase(kb, 'mla_%d' % l, l, W, pT, scr)
            s5_phase(kb, 's5_%d' % l, l, W, pT, s5hT)
            dn_phase(kb, 'dn_%d' % l, l, W, pT, bgT, dnoT)
            merge_phase(kb, 'mg_%d' % l, W, gT, scr['oT'], s5hT, dnoT, mergedT)
            plain_gemm(kb, 'wo_%d' % l, mergedT, 'mergedT', W['w_out'], D, D, fT, 'fT', SG=c.get('SG1', 4), NB=4)
            norm_phase(kb, 'n3_%d' % l, D, h=hsrc, y=(fT, 'fT', 0), coef=1.0, g_post=g[3], g_pre=g[4],
                       n_out=(nT, 'nT', 0), h_out=(hT, 'hT', 0))
            ffn_phase(kb, 'f2_%d' % l, nT, W['ffn2_w13'], W['ffn2_w2'], hidT, fT)
            if last:
                norm_phase(kb, 'n5_%d' % l, D, h=hsrc, y=(fT, 'fT', 0), coef=0.5, g_post=g[5],
                           h_out=(outT, 'outT', 0))
            else:
                norm_phase(kb, 'n5_%d' % l, D, h=hsrc, y=(fT, 'fT', 0), coef=0.5, g_post=g[5], g_pre=G[6 * (l + 1)],
                           n_out=(nT, 'nT', 0), h_out=(hT, 'hT', 0))
    S.barrier()
    S.finish()
    return kb


def make_in_maps(c, inputs, ncores):
    c = derive(c)
    D, L, LP = c['D'], c['L'], c['LP']
    x = np.asarray(inputs['x'], dtype=np.float32)
    meta = np.asarray(inputs['meta_tokens'], dtype=np.float32)
    shared = {'sandwich_g': np.ascontiguousarray(np.asarray(inputs['sandwich_g'], np.float32).reshape(-1, D))}
    for l in range(c['DEPTH']):
        for n in PER_LAYER:
            w = np.asarray(inputs[n], np.float32)[l]
            if n in RELAID:
                tiles = inproj_tiles(c) if n == 'w_in' else [(i, 128) for i in range(0, w.shape[1], 128)]
                shared['%s_%d' % (n, l)] = relay_weight(w, tiles)
            else:
                shared['%s_%d' % (n, l)] = np.ascontiguousarray(w)
    maps = []
    for b in range(ncores):
        xT = np.zeros((D, LP), np.float32)
        xT[:, :c['NMETA']] = meta.T
        xT[:, c['NMETA']:L] = x[b].T
        m = dict(shared)
        m['xT'] = to_blocked(c, xT)
        maps.append(m)
    return maps


def to_blocked(c, a):
    D, LP = a.shape
    NW = c['TG'] // 2
    return np.ascontiguousarray(a.reshape(D // 128, 128, LP // NW, NW).transpose(2, 1, 0, 3).reshape(LP // NW, 128, -1))


def from_blocked(c, b, D):
    G, P, R = b.shape
    NW = c['TG'] // 2
    return np.ascontiguousarray(b.reshape(G, 128, D // 128, NW).transpose(2, 1, 0, 3).reshape(D, G * NW))


_CACHE = {}


def kernel(**inputs):
    cfg = full_cfg()
    c = derive(cfg)
    B = np.asarray(inputs['x']).shape[0]
    if 'kb' not in _CACHE:
        _CACHE['kb'] = build_program(cfg)
    kb = _CACHE['kb']
    maps = make_in_maps(cfg, inputs, B)
    res = run_bass_kernel_spmd(kb.nc, maps, core_ids=list(range(B)))
    out = np.stack([np.ascontiguousarray(from_blocked(c, res.results[b]['outT'], c['D'])[:, c['NMETA']:c['L']].T)
                    for b in range(B)], axis=0)
    return out.astype(np.float32)
```

```python
import math
from contextlib import ExitStack
import numpy as np
import concourse.bass as bass
import concourse.mybir as mybir
from concourse.bass_utils import run_bass_kernel_spmd

F32 = mybir.dt.float32
BF16 = mybir.dt.bfloat16
AF = mybir.ActivationFunctionType
ALU = mybir.AluOpType
EPS = 1e-6


def full_cfg():
    return dict(D=4096, SEQ=4096, DEPTH=2, NMETA=16, CHUNK=64, F=11008,
                H=16, QL=1024, KVL=512, NOPE=128, ROPE=64, VD=128,
                S5W=1024, S5G=16, S5P=64, DNH=8, DK=128, DV=128, DNCONV=4,
                TG=384, THETA=10000.0)


def derive(c):
    c = dict(c)
    c['L'] = c['SEQ'] + c['NMETA']
    c['LP'] = -(-c['L'] // c['TG']) * c['TG']
    c['NG'] = c['LP'] // c['TG']
    c['NT'] = c['LP'] // 128
    c['QK'] = c['NOPE'] + c['ROPE']
    c['MLAW'] = c['H'] * c['VD']
    c['S5NG'] = c['S5W'] // c['S5G']
    c['DNQK'] = c['DNH'] * c['DK']
    c['DNW'] = c['DNH'] * c['DV']
    o = 0
    c['OFF_Q'] = o; o += c['QL']
    c['OFF_KV'] = o; o += c['KVL']
    c['OFF_KR'] = o; o += c['ROPE']
    c['OFF_S5'] = o; o += c['S5W']
    c['OFF_DN_QKV'] = o; o += 2 * c['DNQK'] + c['DNW']
    c['OFF_DN_Z'] = o; o += c['DNW']
    c['OFF_DN_A'] = o; o += c['DNH']
    c['OFF_DN_B'] = o; o += c['DNH']
    c['OFF_GATE'] = o; o += 3 * c['D']
    c['NIN'] = o
    return c


class Res:
    __slots__ = ('w', 'r', 'name')

    def __init__(self, name=''):
        self.w = {}
        self.r = {}
        self.name = name


class Sched:
    NDS = 24

    def __init__(self, nc, stack):
        self.nc = nc
        self.eng = {'pe': nc.tensor, 'act': nc.scalar, 'dve': nc.vector,
                    'pool': nc.gpsimd, 'sp': nc.sync}
        self.sem = {}
        for k in ('pe', 'act', 'dve', 'pool'):
            self.sem[k] = stack.enter_context(nc.semaphore('s_' + k))
        for i in range(self.NDS):
            self.sem[('d', i)] = stack.enter_context(nc.semaphore('s_d%d' % i))
        self.cnt = {k: 0 for k in self.sem}
        self.waited = {e: {} for e in self.eng}
        self.nd = 0
        self.nins = 0

    def _hazards(self, eng, reads, writes, waits, is_dma=False):
        def need(tok):
            if tok is None:
                return
            k, v = tok
            if eng == 'pe' and k == 'pe':
                return
            if self.waited[eng].get(k, 0) >= v:
                return
            if waits.get(k, 0) < v:
                waits[k] = v
        for r in reads:
            for k, v in r.w.items():
                need((k, v))
        for w in writes:
            if w.r:
                for k, v in w.r.items():
                    need((k, v))
                for k, v in w.w.items():
                    need((k, v))
            else:
                for k, v in w.w.items():
                    if k == eng or (is_dma and isinstance(k, tuple)):
                        continue
                    need((k, v))

    def _emit_waits(self, eng, waits):
        E = self.eng[eng]
        for k, v in waits.items():
            E.wait_ge(self.sem[k], v)
            self.waited[eng][k] = v
            self.nins += 1

    def _commit(self, tok, reads, writes):
        k, v = tok
        for r in reads:
            if r.r.get(k, 0) < v:
                r.r[k] = v
        for w in writes:
            if w.r:
                w.w = {}
                w.r = {}
            if w.w.get(k, 0) < v:
                w.w[k] = v

    def op(self, eng, fn, reads=(), writes=()):
        waits = {}
        self._hazards(eng, reads, writes, waits)
        self._emit_waits(eng, waits)
        ins = fn(self.eng[eng])
        self.cnt[eng] += 1
        ins.then_inc(self.sem[eng], 1)
        self.nins += 1
        self._commit((eng, self.cnt[eng]), reads, writes)

    def dma(self, q, out, in_, reads=(), writes=()):
        i = self.nd % self.NDS
        self.nd += 1
        key = ('d', i)
        waits = {}
        self._hazards(q, reads, writes, waits, is_dma=True)
        prev = self.cnt[key]
        if prev > 0 and self.waited[q].get(key, 0) < prev and waits.get(key, 0) < prev:
            waits[key] = prev
        self._emit_waits(q, waits)
        ins = self.eng[q].dma_start(out=out, in_=in_)
        self.cnt[key] += 16
        ins.then_inc(self.sem[key], 16)
        self.nins += 1
        self._commit((key, self.cnt[key]), reads, writes)

    def barrier(self):
        for e in self.eng:
            for k, v in self.cnt.items():
                if v > 0 and self.waited[e].get(k, 0) < v and not (e == k):
                    self.eng[e].wait_ge(self.sem[k], v)
                    self.waited[e][k] = v
                    self.nins += 1

    def finish(self):
        E = self.eng['sp']
        for i in range(self.NDS):
            key = ('d', i)
            if self.cnt[key] > 0:
                E.wait_ge(self.sem[key], self.cnt[key])
        for k in ('pe', 'act', 'dve', 'pool'):
            if self.cnt[k] > 0:
                E.wait_ge(self.sem[k], self.cnt[k])


class T:
    def __init__(self, ap_handle, name=''):
        self.t = ap_handle
        self.res = Res(name)

    def __getitem__(self, k):
        return self.t[k]


class K:
    def __init__(self, cfg):
        self.c = cfg
        self.nc = bass.Bass("TRN2", target_bir_lowering=False)
        self.root = ExitStack()
        self.S = Sched(self.nc, self.root)
        self.dres = {}
        self.relaid = set()
        self.blocked = set()

    def dram(self, name, shape, dt, kind="Internal"):
        t = self.nc.dram_tensor(name, list(shape), dt, kind=kind).ap()
        self.dres[name] = Res(name)
        return t

    def dump(self, name, t, shape, dt=F32):
        if name not in getattr(self, 'dbg', ()):
            return
        d = self.nc.dram_tensor('dbg_' + name, list(shape), dt, kind="ExternalOutput").ap()
        self.dres['dbg_' + name] = Res(name)
        self.S.dma('sp', d, t[:], reads=[t.res], writes=[self.dres['dbg_' + name]])

    def sb(self, stack, name, shape, dt):
        return T(stack.enter_context(self.nc.sbuf_tensor(name, list(shape), dt)), name)

    def ps(self, stack, name, shape, dt=F32):
        return T(stack.enter_context(self.nc.psum_tensor(name, list(shape), dt)), name)


def rstd_from_sumsq(kb, out_ap, out_res, ss_ap, ss_res, D, eps=EPS):
    S = kb.S
    S.op('dve', lambda e: e.tensor_scalar(out_ap, ss_ap, 1.0 / D, eps, ALU.mult, ALU.add),
         reads=[ss_res], writes=[out_res])
    S.op('act', lambda e: e.activation(out_ap, out_ap, AF.Sqrt), reads=[out_res], writes=[out_res])
    S.op('dve', lambda e: e.reciprocal(out_ap, out_ap), reads=[out_res], writes=[out_res])


def chunks(n, m):
    return [(i, min(m, n - i)) for i in range(0, n, m)]


def gemm_phase(kb, name, xsrcs, jobs, epilogue, SG, NB, nbanks_per_job, ep_alloc=None):
    c, S, nc = kb.c, kb.S, kb.nc
    TG, NG, LP = c['TG'], c['NG'], c['LP']
    xk = []
    for ap, rn in xsrcs:
        Ki = ap.shape[0]
        for r0, rows in chunks(Ki, 128):
            xk.append((ap, rn, r0, rows))
    nxk = len(xk)
    for j in jobs:
        segs = []
        for gi, grp in enumerate(j['groups']):
            for (W, c0, M, xkt0, nkt) in grp:
                segs.append((gi, W, c0, M, xkt0, nkt))
        j['segs'] = segs
        j['wcols'] = sum(s[3] * s[5] for s in segs)
    maxw = max(sum(j['wcols'] for j in jobs[b:b + NB]) for b in range(0, len(jobs), NB))
    with ExitStack() as st:
        XS = kb.sb(st, name + '_xs', [128, nxk, SG * TG], BF16)
        WB = [kb.sb(st, name + '_wb%d' % i, [128, maxw], BF16) for i in range(2)]
        nps = 8 // nbanks_per_job
        nps = min(nps, 4)
        PS = [[kb.ps(st, name + '_ps%d_%d' % (i, b), [128, 512]) for b in range(nbanks_per_job)]
              for i in range(nps)]
        ctx = ep_alloc(kb, st) if ep_alloc else None
        pscnt = 0
        wbcnt = 0
        for sg0 in range(0, NG, SG):
            ngs = min(SG, NG - sg0)
            for i, (ap, rn, r0, rows) in enumerate(xk):
                S.dma('sp', XS[0:rows, i, 0:ngs * TG], ap[r0:r0 + rows, sg0 * TG:(sg0 + ngs) * TG],
                      reads=[kb.dres[rn]], writes=[XS.res])
            for b0 in range(0, len(jobs), NB):
                blk = jobs[b0:b0 + NB]
                wb = WB[wbcnt % 2]
                wbcnt += 1
                off = 0
                for j in blk:
                    j['woff'] = []
                    for (gi, W, c0, M, xkt0, nkt) in j['segs']:
                        j['woff'].append(off)
                        if id(W) in kb.relaid:
                            S.dma('pool', wb[:, off:off + nkt * M], W[:, nkt * c0:nkt * c0 + nkt * M],
                                  reads=[], writes=[wb.res])
                        else:
                            dst = wb[:, off:off + nkt * M].rearrange("p (k m) -> p k m", m=M)
                            for k0, kn in chunks(nkt, 8):
                                src = W[k0 * 128:(k0 + kn) * 128, c0:c0 + M].rearrange("(k p) m -> p k m", p=128)
                                S.dma('pool', dst[:, k0:k0 + kn, :], src, reads=[], writes=[wb.res])
                        off += nkt * M
                for j in blk:
                    for g in range(ngs):
                        ps = PS[pscnt % nps]
                        pscnt += 1
                        ng = len(j['groups'])
                        for gi in range(ng):
                            segs = [(s, o) for s, o in zip(j['segs'], j['woff']) if s[0] == gi]
                            mm = []
                            for (s, o) in segs:
                                (_, W, c0, M, xkt0, nkt) = s
                                for kt in range(nkt):
                                    rows = xk[xkt0 + kt][3]
                                    mm.append((o + kt * M, M, xkt0 + kt, rows))

                            def fn(pe, mm=mm, ps=ps, gi=gi, g=g, wb=wb):
                                ins = None
                                for idx, (wo, M, xi, rows) in enumerate(mm):
                                    ins = pe.matmul(ps[gi][0:M, 0:TG], wb[0:rows, wo:wo + M],
                                                    XS[0:rows, xi, g * TG:(g + 1) * TG],
                                                    start=(idx == 0), stop=(idx == len(mm) - 1))
                                return ins
                            S.op('pe', fn, reads=[wb.res, XS.res], writes=[ps[gi].res])
                        epilogue(kb, j, ps, sg0 + g, (sg0 + g) * TG, ctx)
    S.barrier()


def norm_phase(kb, name, D, h, y=None, coef=1.0, g_post=None, g_pre=None,
               n_out=None, h_out=None):
    c, S = kb.c, kb.S
    LP = c['LP']
    NW = c['TG'] // 2
    assert D % 128 == 0 and LP % NW == 0
    nkt = D // 128
    with ExitStack() as st:
        Hh = [kb.sb(st, name + '_h%d' % i, [128, nkt, NW], F32) for i in range(2)]
        Yy = [kb.sb(st, name + '_y%d' % i, [128, nkt, NW], F32) for i in range(2)] if y is not None else None
        SQs = [kb.sb(st, name + '_sq%d' % i, [128, nkt, NW], F32) for i in range(2)]
        Nns = [kb.sb(st, name + '_n%d' % i, [128, nkt, NW], BF16) for i in range(2)] if n_out is not None else None
        ones = kb.sb(st, name + '_ones', [128, 128], F32)
        rss = [kb.sb(st, name + '_rs%d' % i, [128, NW], F32) for i in range(2)]
        pss = [kb.ps(st, name + '_ps%d' % i, [128, 512]) for i in range(2)]
        S.op('dve', lambda e: e.memset(ones[:], 1.0), writes=[ones.res])

        def stats(src, SQ, ps, rs):
            for kt in range(nkt):
                S.op('act', lambda e, kt=kt: e.activation(SQ[:, kt, :], src[:, kt, :], AF.Square),
                     reads=[src.res], writes=[SQ.res])

            def fn(pe):
                ins = None
                for kt in range(nkt):
                    ins = pe.matmul(ps[:, 0:NW], ones[:], SQ[:, kt, :], start=(kt == 0), stop=(kt == nkt - 1))
                return ins
            S.op('pe', fn, reads=[ones.res, SQ.res], writes=[ps.res])
            rstd_from_sumsq(kb, rs[:], rs.res, ps[:, 0:NW], ps.res, D)

        def view(t, t0):
            ap, rn, r0 = t
            if rn in kb.blocked:
                return ap[t0 // NW].rearrange("p (k t) -> p k t", t=NW), kb.dres[rn]
            return ap[r0:r0 + D, t0:t0 + NW].rearrange("(k p) t -> p k t", p=128), kb.dres[rn]

        def stage_a(g):
            t0 = g * NW
            bi = g % 2
            H_, SQ, ps, rs = Hh[bi], SQs[bi], pss[bi], rss[bi]
            v, r = view(h, t0)
            S.dma('sp', H_[:], v, reads=[r], writes=[H_.res])
            if y is not None:
                Y = Yy[bi]
                v, r = view(y, t0)
                S.dma('sp', Y[:], v, reads=[r], writes=[Y.res])
                stats(Y, SQ, ps, rs)
                for kt in range(nkt):
                    S.op('dve', lambda e, kt=kt: e.scalar_tensor_tensor(
                        SQ[:, kt, :], Y[:, kt, :], g_post[:, kt:kt + 1], rs[:], ALU.mult, ALU.mult),
                        reads=[Y.res, rs.res], writes=[SQ.res])
                    S.op('dve', lambda e, kt=kt: e.scalar_tensor_tensor(
                        H_[:, kt, :], SQ[:, kt, :], float(coef), H_[:, kt, :], ALU.mult, ALU.add),
                        reads=[SQ.res, H_.res], writes=[H_.res])
                if h_out is not None:
                    v, r = view(h_out, t0)
                    S.dma('sp', v, H_[:], reads=[H_.res], writes=[r])

        def stage_b(g):
            t0 = g * NW
            bi = g % 2
            H_, SQ, ps, rs = Hh[bi], SQs[bi], pss[bi], rss[bi]
            if n_out is not None:
                Nn = Nns[bi]
                stats(H_, SQ, ps, rs)
                for kt in range(nkt):
                    S.op('dve', lambda e, kt=kt: e.scalar_tensor_tensor(
                        Nn[:, kt, :], H_[:, kt, :], g_pre[:, kt:kt + 1], rs[:], ALU.mult, ALU.mult),
                        reads=[H_.res, rs.res], writes=[Nn.res])
                v, r = view(n_out, t0)
                S.dma('sp', v, Nn[:], reads=[Nn.res], writes=[r])

        ngr = LP // NW
        for g in range(ngr):
            stage_a(g)
            if g > 0:
                stage_b(g - 1)
        stage_b(ngr - 1)
    S.barrier()


def ffn_phase(kb, name, nT, w13, w2, hidT, fT):
    c, S = kb.c, kb.S
    D, F, TG = c['D'], c['F'], c['TG']
    nkt = D // 128
    jobs = []
    for j in range(F // 128):
        jobs.append(dict(groups=[[(w13, j * 128, 128, 0, nkt)], [(w13, F + j * 128, 128, 0, nkt)]], j=j))

    def alloc1(kb, st):
        return dict(sa=[kb.sb(st, name + '_sa%d' % i, [128, TG], F32) for i in range(2)],
                    ho=[kb.sb(st, name + '_ho%d' % i, [128, TG], BF16) for i in range(3)], n=[0])

    def ep1(kb, job, ps, g, t0, ctx):
        i = ctx['n'][0]
        ctx['n'][0] += 1
        sa = ctx['sa'][i % 2]
        ho = ctx['ho'][i % 3]
        S.op('act', lambda e: e.activation(sa[:], ps[0][:, 0:TG], AF.Silu), reads=[ps[0].res], writes=[sa.res])
        S.op('dve', lambda e: e.tensor_tensor(ho[:], sa[:], ps[1][:, 0:TG], ALU.mult),
             reads=[sa.res, ps[1].res], writes=[ho.res])
        j = job['j']
        S.dma('sp', hidT[j * 128:(j + 1) * 128, t0:t0 + TG], ho[:], reads=[ho.res], writes=[kb.dres['hidT']])
    gemm_phase(kb, name + 'a', [(nT, 'nT')], jobs, ep1, SG=c.get('SG1', 4), NB=c.get('NB1', 2),
               nbanks_per_job=2, ep_alloc=alloc1)

    fkt = F // 128
    jobs2 = [dict(groups=[[(w2, j * 128, 128, 0, fkt)]], j=j) for j in range(D // 128)]

    def alloc2(kb, st):
        return dict(o=[kb.sb(st, name + '_fo%d' % i, [128, TG], F32) for i in range(3)], n=[0])

    def ep2(kb, job, ps, g, t0, ctx):
        i = ctx['n'][0]
        ctx['n'][0] += 1
        o = ctx['o'][i % 3]
        eng = 'act' if i % 2 == 0 else 'dve'
        if eng == 'act':
            S.op('act', lambda e: e.copy(o[:], ps[0][:, 0:TG]), reads=[ps[0].res], writes=[o.res])
        else:
            S.op('dve', lambda e: e.tensor_copy(o[:], ps[0][:, 0:TG]), reads=[ps[0].res], writes=[o.res])
        j = job['j']
        store_blocked(kb, fT, 'fT', j, t0, o)
    gemm_phase(kb, name + 'b', [(hidT, 'hidT')], jobs2, ep2, SG=c.get('SG2', 2), NB=1,
               nbanks_per_job=1, ep_alloc=alloc2)


def store_blocked(kb, dst, dname, j, t0, o):
    TG = kb.c['TG']
    NW = TG // 2
    for hh in range(2):
        g = (t0 + hh * NW) // NW
        kb.S.dma('sp', dst[g][:, j * NW:(j + 1) * NW], o[:, hh * NW:(hh + 1) * NW],
                 reads=[o.res], writes=[kb.dres[dname]])


def load_gain(kb, st, name, src_row_ap, D):
    nkt = D // 128
    t = kb.sb(st, name, [128, nkt], F32)
    v = src_row_ap.rearrange("(k p) -> p k", p=128)
    for k0, kn in chunks(nkt, 8):
        with kb.nc.allow_non_contiguous_dma("gain vector transpose load"):
            kb.S.dma('sp', t[:, k0:k0 + kn], v[:, k0:k0 + kn], writes=[t.res])
    return t


def rope_tables(kb, csT):
    c, S = kb.c, kb.S
    LP = c['LP']
    R2 = c['ROPE'] // 2
    I32 = mybir.dt.int32
    with ExitStack() as st:
        pi_ = kb.sb(st, 'rt_pi', [R2, LP], I32)
        pf = kb.sb(st, 'rt_pf', [R2, LP], F32)
        ii = kb.sb(st, 'rt_ii', [R2, 1], I32)
        inv = kb.sb(st, 'rt_inv', [R2, 1], F32)
        ang = kb.sb(st, 'rt_ang', [R2, LP], F32)
        cs = kb.sb(st, 'rt_cs', [R2, 2, LP], F32)
        ki = kb.sb(st, 'rt_ki', [R2, LP], I32)
        kf = kb.sb(st, 'rt_kf', [R2, LP], F32)
        S.op('pool', lambda e: e.iota(pi_[:], [[1, LP]], 0, 0), writes=[pi_.res])
        S.op('pool', lambda e: e.iota(ii[:], [[1, 1]], 0, 1), writes=[ii.res])
        S.op('dve', lambda e: e.tensor_copy(pf[:], pi_[:]), reads=[pi_.res], writes=[pf.res])
        S.op('dve', lambda e: e.tensor_copy(inv[:], ii[:]), reads=[ii.res], writes=[inv.res])
        S.op('act', lambda e: e.activation(inv[:], inv[:], AF.Exp, scale=-math.log(c['THETA']) / R2),
             reads=[inv.res], writes=[inv.res])
        S.op('dve', lambda e: e.tensor_scalar(ang[:], pf[:], inv[:, 0:1], None, ALU.mult),
             reads=[pf.res, inv.res], writes=[ang.res])
        for j, sh in enumerate((0.5 * math.pi, 0.0)):
            S.op('dve', lambda e, j=j, sh=sh: e.tensor_scalar(cs[:, j, :], ang[:], sh, None, ALU.add),
                 reads=[ang.res], writes=[cs.res])
            S.op('dve', lambda e, j=j: e.tensor_scalar(kf[:], cs[:, j, :], 1.0 / (2 * math.pi), None, ALU.mult),
                 reads=[cs.res], writes=[kf.res])
            S.op('dve', lambda e: e.tensor_copy(ki[:], kf[:]), reads=[kf.res], writes=[ki.res])
            S.op('dve', lambda e: e.tensor_copy(kf[:], ki[:]), reads=[ki.res], writes=[kf.res])
            S.op('dve', lambda e, j=j: e.scalar_tensor_tensor(cs[:, j, :], kf[:], -2 * math.pi, cs[:, j, :],
                                                              ALU.mult, ALU.add),
                 reads=[kf.res, cs.res], writes=[cs.res])
            S.op('dve', lambda e, j=j: e.tensor_scalar(cs[:, j, :], cs[:, j, :], -3.14159, 3.14159, ALU.max, ALU.min),
                 reads=[cs.res], writes=[cs.res])
            S.op('act', lambda e, j=j: e.activation(cs[:, j, :], cs[:, j, :], AF.Sin), reads=[cs.res], writes=[cs.res])
        S.dma('sp', csT, cs[:], reads=[cs.res], writes=[kb.dres['csT']])
    S.barrier()


def inproj_tiles(c):
    bounds = [c['OFF_Q'], c['OFF_KV'], c['OFF_KR'], c['OFF_S5'], c['OFF_DN_QKV'], c['OFF_DN_Z'],
              c['OFF_DN_A'], c['OFF_GATE'], c['NIN']]
    tiles = []
    for a, b in zip(bounds[:-1], bounds[1:]):
        for c0, m in chunks(b - a, 128):
            tiles.append((a + c0, m))
    return tiles


def inproj_phase(kb, name, nT, w_in, pT, gT):
    c, S = kb.c, kb.S
    D, TG = c['D'], c['TG']
    nkt = D // 128
    jobs = []
    for c0, m in inproj_tiles(c):
        jobs.append(dict(groups=[[(w_in, c0, m, 0, nkt)]], c0=c0, m=m, gate=(c0 >= c['OFF_GATE'])))

    def alloc(kb, st):
        return dict(o=[kb.sb(st, name + '_o%d' % i, [128, TG], F32) for i in range(4)], n=[0])

    def ep(kb, job, ps, g, t0, ctx):
        i = ctx['n'][0]
        ctx['n'][0] += 1
        o = ctx['o'][i % 4]
        m, c0 = job['m'], job['c0']
        if job['gate']:
            S.op('act', lambda e: e.activation(o[0:m, :], ps[0][0:m, 0:TG], AF.Sigmoid),
                 reads=[ps[0].res], writes=[o.res])
        else:
            S.op('dve', lambda e: e.tensor_copy(o[0:m, :], ps[0][0:m, 0:TG]), reads=[ps[0].res], writes=[o.res])
        if job['gate']:
            r0 = c0 - c['OFF_GATE']
            S.dma('sp', gT[r0:r0 + m, t0:t0 + TG], o[0:m, :], reads=[o.res], writes=[kb.dres['gT']])
        else:
            S.dma('sp', pT[c0:c0 + m, t0:t0 + TG], o[0:m, :], reads=[o.res], writes=[kb.dres['pT']])
    gemm_phase(kb, name, [(nT, 'nT')], jobs, ep, SG=c.get('SG1', 4), NB=4, nbanks_per_job=1, ep_alloc=alloc)


def mla_phase(kb, name, l, W, pT, scr):
    c, S = kb.c, kb.S
    TG, NG, LP, NT = c['TG'], c['NG'], c['LP'], c['NT']
    H, QL, KVL, NOPE, ROPE, VD = c['H'], c['QL'], c['KVL'], c['NOPE'], c['ROPE'], c['VD']
    R2 = ROPE // 2
    QKD = NOPE + ROPE
    scale = QKD ** -0.5
    cqnT, ckvnT, qnT, qrT, knT, krT, Vtok, oT, csT = (scr[k] for k in
                                                       ('cqnT', 'ckvnT', 'qnT', 'qrT', 'knT', 'krT', 'Vtok', 'oT', 'csT'))
    with ExitStack() as st:
        gq = load_gain(kb, st, name + '_gq', W['mla_q_norm_g'], QL)
        gkv = load_gain(kb, st, name + '_gkv', W['mla_kv_norm_g'], KVL)
        norm_phase(kb, name + '_nq', QL, h=(pT, 'pT', c['OFF_Q']), g_pre=gq, n_out=(cqnT, 'cqnT', 0))
        norm_phase(kb, name + '_nkv', KVL, h=(pT, 'pT', c['OFF_KV']), g_pre=gkv, n_out=(ckvnT, 'ckvnT', 0))
    S.barrier()
    with ExitStack() as st:
        cs = kb.sb(st, name + '_cs', [R2, 2, LP], F32)
        x12 = kb.sb(st, name + '_x12', [R2, 2, LP], F32)
        t1 = kb.sb(st, name + '_t1', [R2, LP], F32)
        t2 = kb.sb(st, name + '_t2', [R2, LP], F32)
        kr = kb.sb(st, name + '_kr', [R2, 2, LP], BF16)
        S.dma('sp', cs[:], csT, reads=[kb.dres['csT']], writes=[cs.res])
        for j in range(2):
            S.dma('sp', x12[:, j, :], pT[c['OFF_KR'] + j * R2:c['OFF_KR'] + (j + 1) * R2, :],
                  reads=[kb.dres['pT']], writes=[x12.res])
        rd = [x12.res, cs.res]
        S.op('dve', lambda e: e.tensor_tensor(t1[:], x12[:, 0, :], cs[:, 0, :], ALU.mult), reads=rd, writes=[t1.res])
        S.op('dve', lambda e: e.tensor_tensor(t2[:], x12[:, 1, :], cs[:, 1, :], ALU.mult), reads=rd, writes=[t2.res])
        S.op('dve', lambda e: e.tensor_tensor(kr[:, 0, :], t1[:], t2[:], ALU.subtract),
             reads=[t1.res, t2.res], writes=[kr.res])
        S.op('dve', lambda e: e.tensor_tensor(t1[:], x12[:, 1, :], cs[:, 0, :], ALU.mult),
             reads=rd + [kr.res], writes=[t1.res])
        S.op('dve', lambda e: e.tensor_tensor(t2[:], x12[:, 0, :], cs[:, 1, :], ALU.mult),
             reads=rd + [kr.res], writes=[t2.res])
        S.op('dve', lambda e: e.tensor_tensor(kr[:, 1, :], t1[:], t2[:], ALU.add),
             reads=[t1.res, t2.res], writes=[kr.res])
        for j in range(2):
            S.dma('sp', krT[j * R2:(j + 1) * R2, :], kr[:, j, :], reads=[kr.res], writes=[kb.dres['krT']])
    S.barrier()
    nq = QL // 128
    jobs = []
    for h in range(H):
        jobs.append(dict(groups=[[(W['mla_w_uq'], h * QKD, NOPE, 0, nq)]], h=h, kind='n'))
        jobs.append(dict(groups=[[(W['mla_w_uq'], h * QKD + NOPE, R2, 0, nq)],
                                 [(W['mla_w_uq'], h * QKD + NOPE + R2, R2, 0, nq)]], h=h, kind='r'))

    def alloc_q(kb, st):
        cs = kb.sb(st, name + '_qcs', [R2, 2, LP], F32)
        S.dma('sp', cs[:], csT, reads=[kb.dres['csT']], writes=[cs.res])
        S.op('dve', lambda e: e.tensor_scalar(cs[:], cs[:], scale, None, ALU.mult), reads=[cs.res], writes=[cs.res])
        return dict(cs=cs, o=[kb.sb(st, name + '_qo%d' % i, [128, TG], BF16) for i in range(3)],
                    r=[kb.sb(st, name + '_qr%d' % i, [R2, 2, TG], BF16) for i in range(2)],
                    t=[kb.sb(st, name + '_qt%d' % i, [R2, TG], F32) for i in range(2)], n=[0])

    def ep_q(kb, job, ps, g, t0, ctx):
        i = ctx['n'][0]
        ctx['n'][0] += 1
        h = job['h']
        if job['kind'] == 'n':
            o = ctx['o'][i % 3]
            S.op('act', lambda e: e.activation(o[:], ps[0][:, 0:TG], AF.Copy, scale=scale),
                 reads=[ps[0].res], writes=[o.res])
            S.dma('sp', qnT[h * NOPE:(h + 1) * NOPE, t0:t0 + TG], o[:], reads=[o.res], writes=[kb.dres['qnT']])
        else:
            cs = ctx['cs']
            r = ctx['r'][i % 2]
            t1, t2 = ctx['t']
            x1, x2 = ps[0][0:R2, 0:TG], ps[1][0:R2, 0:TG]
            cc, ss = cs[:, 0, t0:t0 + TG], cs[:, 1, t0:t0 + TG]
            rd = [ps[0].res, ps[1].res, cs.res]
            S.op('dve', lambda e: e.tensor_tensor(t1[:], x1, cc, ALU.mult), reads=rd, writes=[t1.res])
            S.op('dve', lambda e: e.tensor_tensor(t2[:], x2, ss, ALU.mult), reads=rd, writes=[t2.res])
            S.op('dve', lambda e: e.tensor_tensor(r[:, 0, :], t1[:], t2[:], ALU.subtract),
                 reads=[t1.res, t2.res], writes=[r.res])
            S.op('dve', lambda e: e.tensor_tensor(t1[:], x2, cc, ALU.mult), reads=rd + [r.res], writes=[t1.res])
            S.op('dve', lambda e: e.tensor_tensor(t2[:], x1, ss, ALU.mult), reads=rd + [r.res], writes=[t2.res])
            S.op('dve', lambda e: e.tensor_tensor(r[:, 1, :], t1[:], t2[:], ALU.add),
                 reads=[t1.res, t2.res], writes=[r.res])
            for j in range(2):
                S.dma('sp', qrT[h * ROPE + j * R2:h * ROPE + (j + 1) * R2, t0:t0 + TG], r[:, j, :],
                      reads=[r.res], writes=[kb.dres['qrT']])
    gemm_phase(kb, name + '_q', [(cqnT, 'cqnT')], jobs, ep_q, SG=NG, NB=2, nbanks_per_job=2, ep_alloc=alloc_q)
    nkv = KVL // 128
    jobs = [dict(groups=[[(W['mla_w_ukv'], h * (NOPE + VD), NOPE, 0, nkv)]], h=h) for h in range(H)]

    def alloc_k(kb, st):
        return dict(o=[kb.sb(st, name + '_ko%d' % i, [128, TG], BF16) for i in range(3)], n=[0])

    def ep_k(kb, job, ps, g, t0, ctx):
        i = ctx['n'][0]
        ctx['n'][0] += 1
        o = ctx['o'][i % 3]
        h = job['h']
        S.op('act', lambda e: e.copy(o[:], ps[0][:, 0:TG]), reads=[ps[0].res], writes=[o.res])
        S.dma('sp', knT[h * NOPE:(h + 1) * NOPE, t0:t0 + TG], o[:], reads=[o.res], writes=[kb.dres['knT']])
    gemm_phase(kb, name + '_k', [(ckvnT, 'ckvnT')], jobs, ep_k, SG=NG, NB=4, nbanks_per_job=1, ep_alloc=alloc_k)
    with ExitStack() as st:
        X = kb.sb(st, name + '_vx', [128, nkv, LP], BF16)
        Wv = kb.sb(st, name + '_vw', [128, nkv, H * VD], BF16)
        vo = [kb.sb(st, name + '_vo%d' % i, [128, 512], BF16) for i in range(3)]
        pv = [kb.ps(st, name + '_vps%d' % i, [128, 512]) for i in range(4)]
        for k in range(nkv):
            S.dma('sp', X[:, k, :], ckvnT[k * 128:(k + 1) * 128, :], reads=[kb.dres['ckvnT']], writes=[X.res])
        for h in range(H):
            c0 = h * (NOPE + VD) + NOPE
            S.dma('pool', Wv[:, :, h * VD:(h + 1) * VD],
                  W['mla_w_ukv'][:, c0:c0 + VD].rearrange("(k p) m -> p k m", p=128), writes=[Wv.res])
        cnt = 0
        HV = H * VD
        for t in range(NT):
            for c0, cw in chunks(HV, 512):
                ps = pv[cnt % 4]
                o = vo[cnt % 3]
                cnt += 1

                def fn(pe, ps=ps, t=t, c0=c0, cw=cw):
                    ins = None
                    for k in range(nkv):
                        ins = pe.matmul(ps[:, 0:cw], X[:, k, t * 128:(t + 1) * 128], Wv[:, k, c0:c0 + cw],
                                        start=(k == 0), stop=(k == nkv - 1))
                    return ins
                S.op('pe', fn, reads=[X.res, Wv.res], writes=[ps.res])
                S.op('act', lambda e, o=o, ps=ps, cw=cw: e.copy(o[:, 0:cw], ps[:, 0:cw]), reads=[ps.res], writes=[o.res])
                S.dma('sp', Vtok[t * 128:(t + 1) * 128, c0:c0 + cw], o[:, 0:cw], reads=[o.res], writes=[kb.dres['Vtok']])
    S.barrier()
    gt = TG // 128
    npart = gt + 1
    with ExitStack() as st:
        masks = kb.sb(st, name + '_mask', [128, npart, TG], BF16)
        onesb = kb.sb(st, name + '_ones', [128, 128], BF16)
        Kr = kb.sb(st, name + '_Kr', [ROPE, LP], BF16)
        Qn = [kb.sb(st, name + '_Qn%d' % i, [128, LP], BF16) for i in range(2)]
        Qr = [kb.sb(st, name + '_Qr%d' % i, [ROPE, LP], BF16) for i in range(2)]
        Kn = [kb.sb(st, name + '_Kn%d' % i, [128, LP], BF16) for i in range(2)]
        Vh = [kb.sb(st, name + '_Vh%d' % i, [128, NT, VD], BF16) for i in range(2)]
        PT = [kb.sb(st, name + '_PT%d' % i, [128, TG], BF16) for i in range(3)]
        rden = kb.sb(st, name + '_rden', [128, TG], F32)
        oo = [kb.sb(st, name + '_oo%d' % i, [128, TG], BF16) for i in range(2)]
        pS = [kb.ps(st, name + '_pS%d' % i, [128, 512]) for i in range(3)]
        pO = [kb.ps(st, name + '_pO%d' % i, [128, 512]) for i in range(2)]
        pD = [kb.ps(st, name + '_pD%d' % i, [128, 512]) for i in range(2)]
        S.op('dve', lambda e: e.memset(onesb[:], 1.0), writes=[onesb.res])
        S.op('dve', lambda e: e.memset(masks[:], 0.0), writes=[masks.res])
        for j in range(npart):
            for m in range(-1, (TG - 16) // 64 + 1):
                q0 = max(0, 16 + 64 * m)
                q1 = min(TG, 80 + 64 * m)
                thr = min(128, 80 + 64 * m - 128 * j)
                if thr > 0 and q1 > q0:
                    S.op('dve', lambda e, j=j, thr=thr, q0=q0, q1=q1: e.memset(masks[0:thr, j, q0:q1], 1.0),
                         writes=[masks.res])
        S.dma('sp', Kr[:], krT, reads=[kb.dres['krT']], writes=[Kr.res])
        cnt = 0
        for h in range(H):
            b = h % 2
            S.dma('sp', Qn[b][:], qnT[h * NOPE:(h + 1) * NOPE, :], reads=[kb.dres['qnT']], writes=[Qn[b].res])
            S.dma('sp', Qr[b][:], qrT[h * ROPE:(h + 1) * ROPE, :], reads=[kb.dres['qrT']], writes=[Qr[b].res])
            S.dma('sp', Kn[b][:], knT[h * NOPE:(h + 1) * NOPE, :], reads=[kb.dres['knT']], writes=[Kn[b].res])
            S.dma('sp', Vh[b][:], Vtok[:, h * VD:(h + 1) * VD].rearrange("(t p) d -> p t d", p=128),
                  reads=[kb.dres['Vtok']], writes=[Vh[b].res])
            steps = []
            for g in range(NG):
                kts = list(range(0, min(gt * g + npart, NT)))
                for ki, kt in enumerate(kts):
                    steps.append((g, ki, kt, len(kts)))

            def emit_score(step, idx):
                g, ki, kt, nk = step
                q0 = g * TG
                ps = pS[idx % 3]
                pt = PT[idx % 3]

                def fs(pe):
                    pe.matmul(ps[:, 0:TG], Kn[b][:, kt * 128:(kt + 1) * 128], Qn[b][:, q0:q0 + TG],
                              start=True, stop=False)
                    return pe.matmul(ps[:, 0:TG], Kr[:, kt * 128:(kt + 1) * 128], Qr[b][:, q0:q0 + TG],
                                     start=False, stop=True)
                S.op('pe', fs, reads=[Kn[b].res, Qn[b].res, Kr.res, Qr[b].res], writes=[ps.res])
                S.op('act', lambda e: e.activation(pt[:], ps[:, 0:TG], AF.Exp), reads=[ps.res], writes=[pt.res])
                j = kt - gt * g
                if j >= 0:
                    S.op('dve', lambda e: e.tensor_tensor(pt[:], pt[:], masks[:, j, :], ALU.mult),
                         reads=[pt.res, masks.res], writes=[pt.res])

            def emit_pv(step, idx):
                g, ki, kt, nk = step
                q0 = g * TG
                pt = PT[idx % 3]
                po, pd = pO[g % 2], pD[g % 2]
                first, last = (ki == 0), (ki == nk - 1)

                def fo(pe):
                    pe.matmul(po[:, 0:TG], Vh[b][:, kt, :], pt[:], start=first, stop=last)
                    return pe.matmul(pd[:, 0:TG], onesb[:], pt[:], start=first, stop=last)
                S.op('pe', fo, reads=[Vh[b].res, pt.res, onesb.res], writes=[po.res, pd.res])
                if last:
                    o = oo[g % 2]
                    S.op('dve', lambda e: e.reciprocal(rden[:], pd[:, 0:TG]), reads=[pd.res], writes=[rden.res])
                    S.op('dve', lambda e: e.tensor_tensor(o[:], po[:, 0:TG], rden[:], ALU.mult),
                         reads=[po.res, rden.res], writes=[o.res])
                    S.dma('sp', oT[h * VD:(h + 1) * VD, q0:q0 + TG], o[:], reads=[o.res], writes=[kb.dres['oT']])

            for i, stp in enumerate(steps):
                emit_score(stp, cnt + i)
                if i > 0:
                    emit_pv(steps[i - 1], cnt + i - 1)
            emit_pv(steps[-1], cnt + len(steps) - 1)
            cnt += len(steps)
    S.barrier()


def s5_phase(kb, name, l, W, pT, s5hT):
    c, S = kb.c, kb.S
    LP = c['LP']
    G, P, NGr, SW = c['S5G'], c['S5P'], c['S5NG'], c['S5W']
    assert P == 64 and G == 16
    NST = NGr // 2
    CH = 512
    with ExitStack() as st:
        def sbt(n, shape, dt=F32):
            return kb.sb(st, name + '_' + n, shape, dt)
        ar, ai, dl = sbt('ar', [128, NST]), sbt('ai', [128, NST]), sbt('dl', [128, NST])
        mag, th, co, si = sbt('mag', [128, NST]), sbt('th', [128, NST]), sbt('co', [128, NST]), sbt('si', [128, NST])
        abr, abi, den = sbt('abr', [128, NST]), sbt('abi', [128, NST]), sbt('den', [128, NST])
        zr, zi, t1, t2 = sbt('zr', [128, NST]), sbt('zi', [128, NST]), sbt('t1', [128, NST]), sbt('t2', [128, NST])
        ki = sbt('ki', [128, NST], mybir.dt.int32)
        nl = W['s5_log_dt']
        with kb.nc.allow_non_contiguous_dma("small s5 parameter loads"):
            for two in range(2):
                S.dma('sp', ar[two * 64:(two + 1) * 64, :],
                      W['s5_a_re'].rearrange("(s two) p -> two p s", two=2)[two], writes=[ar.res])
                S.dma('sp', ai[two * 64:(two + 1) * 64, :],
                      W['s5_a_im'].rearrange("(s two) p -> two p s", two=2)[two], writes=[ai.res])
                S.dma('sp', dl[two * 64:(two + 1) * 64, :],
                      nl.rearrange("(s two) -> two s", two=2)[two:two + 1, :].broadcast_to([64, NST]),
                      writes=[dl.res])

        def ew(eng, fn, reads, writes):
            S.op(eng, fn, reads=[r.res for r in reads], writes=[w.res for w in writes])

        def sincos(th_, co_, si_):
            y, y2, p_ = t1, t2, zr
            ew('dve', lambda e: e.tensor_scalar(y[:], th_[:], 1.0 / (2 * math.pi), None, ALU.mult), [th_], [y])
            ew('dve', lambda e: e.tensor_copy(ki[:], y[:]), [y], [ki])
            ew('dve', lambda e: e.tensor_copy(y[:], ki[:]), [ki], [y])
            ew('dve', lambda e: e.scalar_tensor_tensor(y[:], y[:], -2 * math.pi, th_[:], ALU.mult, ALU.add), [y, th_], [y])
            ew('dve', lambda e: e.tensor_scalar(y[:], y[:], 0.125, None, ALU.mult), [y], [y])
            ew('dve', lambda e: e.tensor_tensor(y2[:], y[:], y[:], ALU.mult), [y], [y2])
            ew('dve', lambda e: e.tensor_scalar(p_[:], y2[:], 1.0 / 362880, None, ALU.mult), [y2], [p_])
            for a_ in (-1.0 / 5040, 1.0 / 120, -1.0 / 6):
                ew('dve', lambda e, a_=a_: e.scalar_tensor_tensor(p_[:], p_[:], a_, y2[:], ALU.add, ALU.mult), [p_, y2], [p_])
            ew('dve', lambda e: e.scalar_tensor_tensor(si_[:], p_[:], 1.0, y[:], ALU.add, ALU.mult), [p_, y], [si_])
            ew('dve', lambda e: e.tensor_scalar(p_[:], y2[:], -1.0 / 3628800, None, ALU.mult), [y2], [p_])
            for a_ in (1.0 / 40320, -1.0 / 720, 1.0 / 24, -0.5):
                ew('dve', lambda e, a_=a_: e.scalar_tensor_tensor(p_[:], p_[:], a_, y2[:], ALU.add, ALU.mult), [p_, y2], [p_])
            ew('dve', lambda e: e.tensor_scalar(co_[:], p_[:], 1.0, None, ALU.add), [p_], [co_])
            for _ in range(3):
                ew('dve', lambda e: e.tensor_tensor(y2[:], si_[:], si_[:], ALU.mult), [si_], [y2])
                ew('dve', lambda e: e.scalar_tensor_tensor(si_[:], si_[:], 2.0, co_[:], ALU.mult, ALU.mult), [si_, co_], [si_])
                ew('dve', lambda e: e.tensor_scalar(co_[:], y2[:], -2.0, 1.0, ALU.mult, ALU.add), [y2], [co_])

        ew('act', lambda e: e.activation(dl[:], dl[:], AF.Exp), [dl], [dl])
        ew('dve', lambda e: e.tensor_tensor(mag[:], ar[:], dl[:], ALU.mult), [ar, dl], [mag])
        ew('act', lambda e: e.activation(mag[:], mag[:], AF.Exp), [mag], [mag])
        ew('dve', lambda e: e.tensor_tensor(th[:], ai[:], dl[:], ALU.mult), [ai, dl], [th])
        sincos(th, co, si)
        ew('dve', lambda e: e.tensor_tensor(abr[:], mag[:], co[:], ALU.mult), [mag, co], [abr])
        ew('dve', lambda e: e.tensor_tensor(abi[:], mag[:], si[:], ALU.mult), [mag, si], [abi])
        ew('dve', lambda e: e.tensor_tensor(den[:], ar[:], ar[:], ALU.mult), [ar], [den])
        ew('dve', lambda e: e.tensor_tensor(t1[:], ai[:], ai[:], ALU.mult), [ai], [t1])
        ew('dve', lambda e: e.tensor_tensor(den[:], den[:], t1[:], ALU.add), [den, t1], [den])
        ew('dve', lambda e: e.reciprocal(den[:], den[:]), [den], [den])
        ew('dve', lambda e: e.tensor_scalar(t2[:], abr[:], -1.0, None, ALU.add), [abr], [t2])
        ew('dve', lambda e: e.tensor_tensor(zr[:], t2[:], ar[:], ALU.mult), [t2, ar], [zr])
        ew('dve', lambda e: e.tensor_tensor(t1[:], abi[:], ai[:], ALU.mult), [abi, ai], [t1])
        ew('dve', lambda e: e.tensor_tensor(zr[:], zr[:], t1[:], ALU.add), [zr, t1], [zr])
        ew('dve', lambda e: e.tensor_tensor(zr[:], zr[:], den[:], ALU.mult), [zr, den], [zr])
        ew('dve', lambda e: e.tensor_tensor(zi[:], abi[:], ar[:], ALU.mult), [abi, ar], [zi])
        ew('dve', lambda e: e.tensor_tensor(t1[:], t2[:], ai[:], ALU.mult), [t2, ai], [t1])
        ew('dve', lambda e: e.tensor_tensor(zi[:], zi[:], t1[:], ALU.subtract), [zi, t1], [zi])
        ew('dve', lambda e: e.tensor_tensor(zi[:], zi[:], den[:], ALU.mult), [zi, den], [zi])
        NLV = int(math.ceil(math.log2(LP)))
        pwr, pwi = sbt('pwr', [128, NLV, NST]), sbt('pwi', [128, NLV, NST])
        npwi = sbt('npwi', [128, NLV, NST])
        ew('dve', lambda e: e.tensor_copy(pwr[:, 0, :], abr[:]), [abr], [pwr])
        ew('dve', lambda e: e.tensor_copy(pwi[:, 0, :], abi[:]), [abi], [pwi])
        for k in range(1, NLV):
            ew('dve', lambda e, k=k: e.tensor_tensor(t1[:], pwr[:, k - 1, :], pwr[:, k - 1, :], ALU.mult), [pwr], [t1])
            ew('dve', lambda e, k=k: e.tensor_tensor(t2[:], pwi[:, k - 1, :], pwi[:, k - 1, :], ALU.mult), [pwi], [t2])
            ew('dve', lambda e, k=k: e.tensor_tensor(pwr[:, k, :], t1[:], t2[:], ALU.subtract), [t1, t2], [pwr])
            ew('dve', lambda e, k=k: e.tensor_tensor(t1[:], pwr[:, k - 1, :], pwi[:, k - 1, :], ALU.mult), [pwr, pwi], [t1])
            ew('dve', lambda e, k=k: e.tensor_scalar(pwi[:, k, :], t1[:], 2.0, None, ALU.mult), [t1], [pwi])
        ew('dve', lambda e: e.tensor_scalar(npwi[:], pwi[:], -1.0, None, ALU.mult), [pwi], [npwi])
        Bb_r, Bb_i = sbt('Bb_r', [128, NST, 32]), sbt('Bb_i', [128, NST, 32])
        Br, Bi = sbt('Br', [128, NST, 32]), sbt('Bi', [128, NST, 32])
        for t_ in (Br, Bi):
            ew('dve', lambda e, t_=t_: e.memset(t_[:], 0.0), [], [t_])
        with kb.nc.allow_non_contiguous_dma("small s5 parameter loads"):
            for two in range(2):
                for (src, dst) in ((W['s5_b_re'], Br), (W['s5_b_im'], Bi)):
                    v = src.rearrange("(s two) p c -> two p s c", two=2)[two]
                    for s0, sn in chunks(NST, 8):
                        S.dma('sp', dst[two * 64:(two + 1) * 64, s0:s0 + sn, two * 16:(two + 1) * 16],
                              v[:, s0:s0 + sn, :], writes=[dst.res])
        zrb = zr[:].unsqueeze(2).to_broadcast([128, NST, 32])
        zib = zi[:].unsqueeze(2).to_broadcast([128, NST, 32])
        tB = sbt('tB', [128, NST, 32])
        ew('dve', lambda e: e.tensor_tensor(Bb_r[:], Br[:], zrb, ALU.mult), [Br, zr], [Bb_r])
        ew('dve', lambda e: e.tensor_tensor(tB[:], Bi[:], zib, ALU.mult), [Bi, zi], [tB])
        ew('dve', lambda e: e.tensor_tensor(Bb_r[:], Bb_r[:], tB[:], ALU.subtract), [Bb_r, tB], [Bb_r])
        ew('dve', lambda e: e.tensor_tensor(Bb_i[:], Bi[:], zrb, ALU.mult), [Bi, zr], [Bb_i])
        ew('dve', lambda e: e.tensor_tensor(tB[:], Br[:], zib, ALU.mult), [Br, zi], [tB])
        ew('dve', lambda e: e.tensor_tensor(Bb_i[:], Bb_i[:], tB[:], ALU.add), [Bb_i, tB], [Bb_i])
        ident = sbt('ident', [128, 128])
        ew('dve', lambda e: e.memset(ident[:], 0.0), [], [ident])
        S.op('pool', lambda e: e.affine_select(ident[:], ident[:], [[-1, 128]], ALU.not_equal, 1.0, 0, 1),
             reads=[ident.res], writes=[ident.res])
        BT_r, BT_i = sbt('BT_r', [32, NST, 128]), sbt('BT_i', [32, NST, 128])
        ptr = [kb.ps(st, name + '_ptr%d' % i, [128, 512]) for i in range(2)]
        n = 0
        for (src, dst) in ((Bb_r, BT_r), (Bb_i, BT_i)):
            for s in range(NST):
                p_ = ptr[n % 2]
                n += 1
                S.op('pe', lambda e, p_=p_, src=src, s=s: e.transpose(p_[0:32, 0:128], src[:, s, :], ident[:]),
                     reads=[src.res, ident.res], writes=[p_.res])
                S.op('dve', lambda e, p_=p_, dst=dst, s=s: e.tensor_copy(dst[:, s, :], p_[0:32, 0:128]),
                     reads=[p_.res], writes=[dst.res])
        NT8 = NGr // 8
        Xc = sbt('Xc', [128, 2, 128])
        Cb_r, Cb_i = sbt('Cb_r', [128, NT8, 128]), sbt('Cb_i', [128, NT8, 128])
        for T8 in range(NT8):
            ew('dve', lambda e: e.memset(Xc[:], 0.0), [], [Xc])
            for ri, src in enumerate((W['s5_c_re'], W['s5_c_im'])):
                for gl in range(8):
                    S.dma('sp', Xc[16 * gl:16 * gl + 16, ri, (gl % 2) * 64:(gl % 2) * 64 + 64], src[8 * T8 + gl],
                          writes=[Xc.res])
            for ri, dst in enumerate((Cb_r, Cb_i)):
                p_ = ptr[n % 2]
                n += 1
                S.op('pe', lambda e, p_=p_, ri=ri: e.transpose(p_[:, 0:128], Xc[:, ri, :], ident[:]),
                     reads=[Xc.res, ident.res], writes=[p_.res])
                if ri == 0:
                    S.op('dve', lambda e, p_=p_, dst=dst, T8=T8: e.tensor_copy(dst[:, T8, :], p_[:, 0:128]),
                         reads=[p_.res], writes=[dst.res])
                else:
                    S.op('dve', lambda e, p_=p_, dst=dst, T8=T8: e.tensor_scalar(dst[:, T8, :], p_[:, 0:128], -1.0, None,
                                                                               ALU.mult),
                         reads=[p_.res], writes=[dst.res])
        dsk = sbt('dsk', [32, NST])
        with kb.nc.allow_non_contiguous_dma("small s5 parameter loads"):
            S.dma('sp', dsk[:], W['s5_d'].rearrange("(s two) c -> (two c) s", two=2), writes=[dsk.res])
        kb.dump('abr', abr, [128, NST]); kb.dump('abi', abi, [128, NST]); kb.dump('zr', zr, [128, NST]); kb.dump('zi', zi, [128, NST])
        kb.dump('Bb_r', Bb_r, [128, NST, 32]); kb.dump('BT_r', BT_r, [32, NST, 128]); kb.dump('Cb_r', Cb_r, [128, NT8, 128])
        kb.dump('pwr', pwr, [128, NLV, NST])
        u = [sbt('u%d' % i, [32, LP]) for i in range(2)]
        X = [[sbt('x%d%d' % (i, j), [128, LP]) for j in range(2)] for i in range(2)]
        tmp = sbt('ptmp', [128, LP])
        yo = [sbt('yo%d' % i, [32, CH]) for i in range(2)]
        y2 = sbt('y2', [32, CH])
        hb = [sbt('hb%d' % i, [32, CH], BF16) for i in range(2)]
        pb = [kb.ps(st, name + '_pb%d' % i, [128, 512]) for i in range(4)]
        pc = 0
        for s in range(NST):
            us = u[s % 2]
            S.dma('sp', us[:], pT[c['OFF_S5'] + 32 * s:c['OFF_S5'] + 32 * s + 32, :],
                  reads=[kb.dres['pT']], writes=[us.res])
            cur = 0
            for ri, BT in enumerate((BT_r, BT_i)):
                for c0, cw in chunks(LP, CH):
                    p_ = pb[pc % 4]
                    pc += 1
                    S.op('pe', lambda e, p_=p_, BT=BT, c0=c0, cw=cw, us=us, s=s:
                         e.matmul(p_[:, 0:cw], BT[:, s, :], us[:, c0:c0 + cw], start=True, stop=True),
                         reads=[BT.res, us.res], writes=[p_.res])
                    S.op('act', lambda e, p_=p_, ri=ri, c0=c0, cw=cw: e.copy(X[0][ri][:, c0:c0 + cw], p_[:, 0:cw]),
                         reads=[p_.res], writes=[X[0][ri].res])
            for k in range(NLV):
                d = 1 << k
                if d >= LP:
                    break
                a, b_ = X[cur], X[1 - cur]
                Pr, Pi, nPi = pwr[:, k, s:s + 1], pwi[:, k, s:s + 1], npwi[:, k, s:s + 1]
                n_ = LP - d
                ew('dve', lambda e: e.scalar_tensor_tensor(b_[0][:, d:LP], a[0][:, 0:n_], Pr, a[0][:, d:LP],
                                                           ALU.mult, ALU.add), [a[0], pwr], [b_[0]])
                ew('dve', lambda e: e.scalar_tensor_tensor(b_[0][:, d:LP], a[1][:, 0:n_], nPi, b_[0][:, d:LP],
                                                           ALU.mult, ALU.add), [a[1], npwi, b_[0]], [b_[0]])
                ew('act', lambda e: e.copy(b_[0][:, 0:d], a[0][:, 0:d]), [a[0]], [b_[0]])
                ew('dve', lambda e: e.scalar_tensor_tensor(b_[1][:, d:LP], a[1][:, 0:n_], Pr, a[1][:, d:LP],
                                                           ALU.mult, ALU.add), [a[1], pwr], [b_[1]])
                ew('dve', lambda e: e.scalar_tensor_tensor(b_[1][:, d:LP], a[0][:, 0:n_], Pi, b_[1][:, d:LP],
                                                           ALU.mult, ALU.add), [a[0], pwi, b_[1]], [b_[1]])
                ew('act', lambda e: e.copy(b_[1][:, 0:d], a[1][:, 0:d]), [a[1]], [b_[1]])
                cur = 1 - cur
            xr, xi = X[cur]
            if s == 0:
                kb.dump('xr0', xr, [128, LP]); kb.dump('bu0', X[0][0], [128, LP])
            T8, j4 = s // 4, s % 4
            for ci, (c0, cw) in enumerate(chunks(LP, CH)):
                p_ = pb[pc % 4]
                pc += 1
                yy, hh = yo[ci % 2], hb[ci % 2]

                def fy(pe, p_=p_, c0=c0, cw=cw, xr=xr, xi=xi, T8=T8, j4=j4):
                    pe.matmul(p_[0:32, 0:cw], Cb_r[:, T8, 32 * j4:32 * j4 + 32], xr[:, c0:c0 + cw], start=True, stop=False)
                    return pe.matmul(p_[0:32, 0:cw], Cb_i[:, T8, 32 * j4:32 * j4 + 32], xi[:, c0:c0 + cw],
                                     start=False, stop=True)
                S.op('pe', fy, reads=[Cb_r.res, Cb_i.res, xr.res, xi.res], writes=[p_.res])
                ew_r = [p_.res, us.res, dsk.res]
                S.op('dve', lambda e, p_=p_, yy=yy, c0=c0, cw=cw, us=us, s=s: e.scalar_tensor_tensor(
                    yy[:, 0:cw], us[:, c0:c0 + cw], dsk[:, s:s + 1], p_[0:32, 0:cw], ALU.mult, ALU.add),
                    reads=ew_r, writes=[yy.res])
                S.op('dve', lambda e, yy=yy, cw=cw: e.tensor_tensor(y2[:, 0:cw], yy[:, 0:cw], yy[:, 0:cw], ALU.mult),
                     reads=[yy.res], writes=[y2.res])
                S.op('dve', lambda e, cw=cw: e.tensor_scalar(y2[:, 0:cw], y2[:, 0:cw], 0.044715, 1.0, ALU.mult, ALU.add),
                     reads=[y2.res], writes=[y2.res])
                S.op('dve', lambda e, yy=yy, cw=cw: e.tensor_tensor(y2[:, 0:cw], y2[:, 0:cw], yy[:, 0:cw], ALU.mult),
                     reads=[y2.res, yy.res], writes=[y2.res])
                S.op('act', lambda e, cw=cw: e.activation(y2[:, 0:cw], y2[:, 0:cw], AF.Sigmoid, scale=1.5957691216057308),
                     reads=[y2.res], writes=[y2.res])
                S.op('dve', lambda e, yy=yy, hh=hh, cw=cw: e.tensor_tensor(hh[:, 0:cw], y2[:, 0:cw], yy[:, 0:cw], ALU.mult),
                     reads=[y2.res, yy.res], writes=[hh.res])
                S.dma('sp', s5hT[32 * s:32 * s + 32, c0:c0 + cw], hh[:, 0:cw], reads=[hh.res], writes=[kb.dres['s5hT']])
    S.barrier()


def make_ident(kb, st, name):
    ident = kb.sb(st, name, [128, 128], F32)
    kb.S.op('dve', lambda e: e.memset(ident[:], 0.0), writes=[ident.res])
    kb.S.op('pool', lambda e: e.affine_select(ident[:], ident[:], [[-1, 128]], ALU.not_equal, 1.0, 0, 1),
            reads=[ident.res], writes=[ident.res])
    return ident


def dn_phase(kb, name, l, W, pT, bgT, dnoT):
    c, S = kb.c, kb.S
    LP, NT = c['LP'], c['NT']
    NH, DK, DV = c['DNH'], c['DK'], c['DV']
    assert DK == 128 and DV == 128
    QKW = c['DNQK']
    CH = 512
    NLV = 7

    def ew(eng, fn, reads, writes):
        S.op(eng, fn, reads=[r.res for r in reads], writes=[w.res for w in writes])

    with ExitStack() as st:
        ab = kb.sb(st, name + '_ab', [NH, 2, LP], F32)
        gg = [kb.sb(st, name + '_gg%d' % i, [NH, LP], F32) for i in range(2)]
        prm = kb.sb(st, name + '_prm', [NH, 2], F32)
        for j, off in enumerate((c['OFF_DN_A'], c['OFF_DN_B'])):
            S.dma('sp', ab[:, j, :], pT[off:off + NH, :], reads=[kb.dres['pT']], writes=[ab.res])
        with kb.nc.allow_non_contiguous_dma("tiny"):
            S.dma('sp', prm[:, 0:1], W['dn_a_log'].rearrange("(h o) -> h o", o=1), writes=[prm.res])
            S.dma('sp', prm[:, 1:2], W['dn_dt_bias'].rearrange("(h o) -> h o", o=1), writes=[prm.res])
        ew('act', lambda e: e.activation(ab[:, 1, :], ab[:, 1, :], AF.Sigmoid), [ab], [ab])
        ew('act', lambda e: e.activation(prm[:, 0:1], prm[:, 0:1], AF.Exp), [prm], [prm])
        ew('dve', lambda e: e.tensor_scalar(prm[:, 0:1], prm[:, 0:1], -1.0, None, ALU.mult), [prm], [prm])
        ew('act', lambda e: e.activation(gg[0][:], ab[:, 0, :], AF.Exp, bias=prm[:, 1:2]), [ab, prm], [gg[0]])
        ew('act', lambda e: e.activation(gg[0][:], gg[0][:], AF.Ln, bias=1.0), [gg[0]], [gg[0]])
        ew('dve', lambda e: e.tensor_scalar(gg[0][:], gg[0][:], prm[:, 0:1], None, ALU.mult), [gg[0], prm], [gg[0]])
        cur = 0
        for k in range(NLV):
            d = 1 << k
            a3 = gg[cur][:].rearrange("h (t p) -> h t p", p=128)
            b3 = gg[1 - cur][:].rearrange("h (t p) -> h t p", p=128)
            ew('dve', lambda e: e.tensor_tensor(b3[:, :, d:128], a3[:, :, d:128], a3[:, :, 0:128 - d], ALU.add),
               [gg[cur]], [gg[1 - cur]])
            ew('dve', lambda e: e.tensor_copy(b3[:, :, 0:d], a3[:, :, 0:d]), [gg[cur]], [gg[1 - cur]])
            cur = 1 - cur
        S.dma('sp', bgT[0:NH, :], ab[:, 1, :], reads=[ab.res], writes=[kb.dres['bgT']])
        S.dma('sp', bgT[NH:2 * NH, :], gg[cur][:], reads=[gg[cur].res], writes=[kb.dres['bgT']])
    S.barrier()

    with ExitStack() as st:
        def sbt(n, shape, dt=F32):
            return kb.sb(st, name + '_' + n, shape, dt)
        ident = make_ident(kb, st, name + '_ident')
        ones = sbt('ones', [128, 128])
        ew('dve', lambda e: e.memset(ones[:], 1.0), [], [ones])
        m_incl, m_strict = sbt('m_incl', [128, 128]), sbt('m_strict', [128, 128])
        ew('dve', lambda e: e.memset(m_incl[:], 1.0), [], [m_incl])
        ew('dve', lambda e: e.memset(m_strict[:], 1.0), [], [m_strict])
        S.op('pool', lambda e: e.affine_select(m_incl[:], m_incl[:], [[-1, 128]], ALU.is_ge, 0.0, 0, 1),
             reads=[m_incl.res], writes=[m_incl.res])
        S.op('pool', lambda e: e.affine_select(m_strict[:], m_strict[:], [[-1, 128]], ALU.is_gt, 0.0, 0, 1),
             reads=[m_strict.res], writes=[m_strict.res])
        tm = sbt('tm', [128, NT, 2 * NH])
        with kb.nc.allow_non_contiguous_dma("token-major per-token scalars"):
            for t in range(NT):
                S.dma('sp', tm[:, t, :], bgT[:, t * 128:(t + 1) * 128].rearrange("r p -> p r"),
                      reads=[kb.dres['bgT']], writes=[tm.res])
        cwt = sbt('cwt', [128, 3, 4])
        gout = sbt('gout', [128, 1])
        with kb.nc.allow_non_contiguous_dma("tiny"):
            S.dma('sp', gout[:], W['dn_out_norm_g'].rearrange("(p o) -> p o", o=1), writes=[gout.res])
        qT, kT, vT = sbt('qT', [128, LP]), sbt('kT', [128, LP]), sbt('vT', [128, LP])
        tmp, gcrow, qe, oT = sbt('tmp', [128, LP]), sbt('gcrow', [128, LP]), sbt('qe', [128, LP]), sbt('oT', [128, LP])
        rn = sbt('rn', [128, CH])
        sc = sbt('sc', [128, 6, NT])
        Sst = [sbt('S%d' % i, [128, 128]) for i in range(2)]
        CS = []
        NIL = 2
        for q_ in range(NIL):
            CS.append(dict(
                Rt=[sbt('R%d_%d' % (i, q_), [128, 256]) for i in range(2)],
                kd=sbt('kd_%d' % q_, [128, 128]), vn=sbt('vn_%d' % q_, [128, 128]), wT=sbt('wT_%d' % q_, [128, 128]),
                E=sbt('E_%d' % q_, [128, 128]), Dm=sbt('Dm_%d' % q_, [128, 128]), Dms=sbt('Dms_%d' % q_, [128, 128]),
                QKm=sbt('QKm_%d' % q_, [128, 128]), QKT=sbt('QKT_%d' % q_, [128, 128]),
                Ak=[sbt('A%d_%d' % (i, q_), [128, 128]) for i in range(2)],
                Bk=[sbt('B%d_%d' % (i, q_), [128, 128]) for i in range(2)]))
        ob = [sbt('ob%d' % i, [128, CH], BF16) for i in range(2)]
        PSL = [kb.ps(st, name + '_ps%d' % i, [128, 512]) for i in range(8)]
        pcn = [0]

        def nps():
            p = PSL[pcn[0] % 8]
            pcn[0] += 1
            return p

        def colsumsq(src, D_, eps, dst_rn, c0, cw):
            p_ = nps()
            ew('act', lambda e: e.activation(tmp[:, c0:c0 + cw], src[:, c0:c0 + cw], AF.Square), [src], [tmp])
            ew('pe', lambda e: e.matmul(p_[:, 0:cw], ones[:], tmp[:, c0:c0 + cw], start=True, stop=True), [ones, tmp], [p_])
            rstd_from_sumsq(kb, dst_rn[:, 0:cw], dst_rn.res, p_[:, 0:cw], p_.res, D_, eps)

        for h in range(NH):
            for ti, (dst, base) in enumerate(((qT, 0), (kT, QKW), (vT, 2 * QKW))):
                ch0 = base + h * 128
                S.dma('sp', tmp[:], pT[c['OFF_DN_QKV'] + ch0:c['OFF_DN_QKV'] + ch0 + 128, :],
                      reads=[kb.dres['pT']], writes=[tmp.res])
                with kb.nc.allow_non_contiguous_dma("tiny"):
                    S.dma('sp', cwt[:, ti, :], W['dn_conv_w'][:, ch0:ch0 + 128].rearrange("j p -> p j"), writes=[cwt.res])
                ew('dve', lambda e: e.tensor_scalar(dst[:], tmp[:], cwt[:, ti, 3:4], None, ALU.mult), [tmp, cwt], [dst])
                for sft in (1, 2, 3):
                    ew('dve', lambda e, sft=sft: e.scalar_tensor_tensor(
                        dst[:, sft:LP], tmp[:, 0:LP - sft], cwt[:, ti, 3 - sft:4 - sft], dst[:, sft:LP], ALU.mult, ALU.add),
                        [tmp, cwt, dst], [dst])
                ew('act', lambda e: e.activation(dst[:], dst[:], AF.Silu), [dst], [dst])
            for dst, mul in ((qT, DK ** -0.5), (kT, 1.0)):
                for c0, cw in chunks(LP, CH):
                    colsumsq(dst, 1.0, EPS, rn, c0, cw)
                    ew('dve', lambda e, c0=c0, cw=cw: e.scalar_tensor_tensor(
                        dst[:, c0:c0 + cw], dst[:, c0:c0 + cw], float(mul), rn[:, 0:cw], ALU.mult, ALU.mult), [dst, rn], [dst])
            S.dma('sp', gcrow[:], bgT[NH + h:NH + h + 1, :].broadcast_to([128, LP]),
                  reads=[kb.dres['bgT']], writes=[gcrow.res])
            ew('act', lambda e: e.activation(qe[:], gcrow[:], AF.Exp), [gcrow], [qe])
            ew('dve', lambda e: e.tensor_copy(sc[:, 0, :], tm[:, :, h]), [tm], [sc])
            ew('dve', lambda e: e.tensor_copy(sc[:, 1, :], tm[:, :, NH + h]), [tm], [sc])
            ew('dve', lambda e: e.tensor_copy(sc[:, 4, :], qe[:].rearrange("p (t k) -> p t k", k=128)[:, :, 127]), [qe], [sc])
            ew('act', lambda e: e.activation(sc[:, 2, :], sc[:, 1, :], AF.Exp), [sc], [sc])
            ew('dve', lambda e: e.tensor_tensor(sc[:, 2, :], sc[:, 2, :], sc[:, 0, :], ALU.mult), [sc], [sc])
            ew('dve', lambda e: e.tensor_tensor(sc[:, 5, :], gcrow[:].rearrange("p (t k) -> p t k", k=128)[:, :, 127],
                                                sc[:, 1, :], ALU.subtract), [gcrow, sc], [sc])
            ew('act', lambda e: e.activation(sc[:, 3, :], sc[:, 5, :], AF.Exp), [sc], [sc])
            ew('dve', lambda e: e.tensor_tensor(qe[:], qe[:], qT[:], ALU.mult), [qe, qT], [qe])
            ew('dve', lambda e: e.memset(Sst[0][:], 0.0), [], [Sst[0]])
            def chunk_gen(ci):
                T_ = CS[ci % NIL]
                Rt, kd, vn, wT = T_['Rt'], T_['kd'], T_['vn'], T_['wT']
                E, Dm, Dms, QKm, QKT, Ak, Bk = T_['E'], T_['Dm'], T_['Dms'], T_['QKm'], T_['QKT'], T_['Ak'], T_['Bk']
                cs_ = slice(ci * 128, (ci + 1) * 128)
                beta_i, gc_i = sc[:, 0, ci:ci + 1], sc[:, 1, ci:ci + 1]
                beg_i, kds_i, egl = sc[:, 2, ci:ci + 1], sc[:, 3, ci:ci + 1], sc[:, 4, ci:ci + 1]
                pK, pV = nps(), nps()
                ew('pe', lambda e: e.transpose(pK[:, 0:128], kT[:, cs_], ident[:]), [kT, ident], [pK])
                ew('pe', lambda e: e.transpose(pV[:, 0:128], vT[:, cs_], ident[:]), [vT, ident], [pV])
                pG, pQ = nps(), nps()
                ew('pe', lambda e: e.matmul(pG[:, 0:128], kT[:, cs_], kT[:, cs_], start=True, stop=True), [kT], [pG])
                ew('pe', lambda e: e.matmul(pQ[:, 0:128], qT[:, cs_], kT[:, cs_], start=True, stop=True), [qT, kT], [pQ])
                ew('dve', lambda e: e.tensor_scalar(E[:], gcrow[:, cs_], gc_i, 0.0, ALU.subtract, ALU.max), [gcrow, sc], [E])
                ew('act', lambda e: e.activation(E[:], E[:], AF.Exp, scale=-1.0), [E], [E])
                yield
                X0 = Rt[0]
                ew('act', lambda e: e.activation(X0[:, 0:128], pV[:, 0:128], AF.Copy, scale=beta_i), [pV, sc], [X0])
                ew('dve', lambda e: e.tensor_scalar(X0[:, 128:256], pK[:, 0:128], beg_i, None, ALU.mult), [pK, sc], [X0])
                ew('dve', lambda e: e.tensor_scalar(kd[:], pK[:, 0:128], kds_i, None, ALU.mult), [pK, sc], [kd])
                ew('dve', lambda e: e.tensor_tensor(Dm[:], E[:], m_incl[:], ALU.mult), [E, m_incl], [Dm])
                ew('dve', lambda e: e.tensor_tensor(Dms[:], E[:], m_strict[:], ALU.mult), [E, m_strict], [Dms])
                A0, B0 = Ak[0], Bk[0]
                ew('dve', lambda e: e.scalar_tensor_tensor(A0[:], pG[:, 0:128], beta_i, Dms[:], ALU.mult, ALU.mult),
                   [pG, sc, Dms], [A0])
                ew('dve', lambda e: e.tensor_tensor(QKm[:], pQ[:, 0:128], Dm[:], ALU.mult), [pQ, Dm], [QKm])
                yield
                pB, pT_ = nps(), nps()
                ew('pe', lambda e: e.transpose(pB[:, 0:128], A0[:], ident[:]), [A0, ident], [pB])
                ew('pe', lambda e: e.transpose(pT_[:, 0:128], QKm[:], ident[:]), [QKm, ident], [pT_])
                yield
                ew('act', lambda e: e.copy(B0[:], pB[:, 0:128]), [pB], [B0])
                ew('act', lambda e: e.copy(QKT[:], pT_[:, 0:128]), [pT_], [QKT])
                yield
                xc = 0
                for k in range(NLV):
                    Ac, Bc = Ak[k % 2], Bk[k % 2]
                    An, Bn = Ak[(k + 1) % 2], Bk[(k + 1) % 2]
                    Xc, Xn = Rt[xc], Rt[1 - xc]
                    pY = nps()
                    ew('pe', lambda e: e.matmul(pY[:, 0:256], Bc[:], Xc[:], start=True, stop=True), [Bc, Xc], [pY])
                    if k < NLV - 1:
                        pA, pB2 = nps(), nps()
                        ew('pe', lambda e: e.matmul(pA[:, 0:128], Bc[:], Ac[:], start=True, stop=True), [Bc, Ac], [pA])
                        ew('pe', lambda e: e.matmul(pB2[:, 0:128], Ac[:], Bc[:], start=True, stop=True), [Ac, Bc], [pB2])
                    yield
                    ew('dve', lambda e: e.tensor_tensor(Xn[:], Xc[:], pY[:, 0:256], ALU.subtract if k == 0 else ALU.add),
                       [Xc, pY], [Xn])
                    xc = 1 - xc
                    if k < NLV - 1:
                        ew('act', lambda e: e.copy(An[:], pA[:, 0:128]), [pA], [An])
                        ew('dve', lambda e: e.tensor_copy(Bn[:], pB2[:, 0:128]), [pB2], [Bn])
                    yield
                Xf = Rt[xc]
                pW = nps()
                ew('pe', lambda e: e.transpose(pW[:, 0:128], Xf[:, 128:256], ident[:]), [Xf, ident], [pW])
                yield
                ew('act', lambda e: e.copy(wT[:], pW[:, 0:128]), [pW], [wT])
                yield
                Sc, Sn = Sst[ci % 2], Sst[(ci + 1) % 2]
                pv_, po_, pS_ = nps(), nps(), nps()
                ew('pe', lambda e: e.matmul(pv_[:, 0:128], wT[:], Sc[:], start=True, stop=True), [wT, Sc], [pv_])
                ew('dve', lambda e: e.tensor_tensor(vn[:], Xf[:, 0:128], pv_[:, 0:128], ALU.subtract), [Xf, pv_], [vn])

                def fo(e):
                    e.matmul(po_[:, 0:128], Sc[:], qe[:, cs_], start=True, stop=False)
                    return e.matmul(po_[:, 0:128], vn[:], QKT[:], start=False, stop=True)
                ew('pe', fo, [Sc, qe, vn, QKT], [po_])
                ew('pe', lambda e: e.matmul(pS_[:, 0:128], kd[:], vn[:], start=True, stop=True), [kd, vn], [pS_])
                ew('dve', lambda e: e.scalar_tensor_tensor(Sn[:], Sc[:], egl, pS_[:, 0:128], ALU.mult, ALU.add),
                   [Sc, sc, pS_], [Sn])
                ew('act', lambda e: e.copy(oT[:, cs_], po_[:, 0:128]), [po_], [oT])
                yield

            for c0_ in range(0, NT, NIL):
                gens = [chunk_gen(ci) for ci in range(c0_, min(c0_ + NIL, NT))]
                while gens:
                    for g_ in list(gens):
                        try:
                            next(g_)
                        except StopIteration:
                            gens.remove(g_)
            S.dma('sp', tmp[:], pT[c['OFF_DN_Z'] + h * 128:c['OFF_DN_Z'] + (h + 1) * 128, :],
                  reads=[kb.dres['pT']], writes=[tmp.res])
            ew('act', lambda e: e.activation(gcrow[:], tmp[:], AF.Silu), [tmp], [gcrow])
            for ci, (c0, cw) in enumerate(chunks(LP, CH)):
                colsumsq(oT, float(DV), EPS, rn, c0, cw)
                ew('dve', lambda e: e.scalar_tensor_tensor(oT[:, c0:c0 + cw], oT[:, c0:c0 + cw], gout[:, 0:1], rn[:, 0:cw],
                                                           ALU.mult, ALU.mult), [oT, gout, rn], [oT])
                o_ = ob[ci % 2]
                ew('dve', lambda e: e.tensor_tensor(o_[:, 0:cw], oT[:, c0:c0 + cw], gcrow[:, c0:c0 + cw], ALU.mult),
                   [oT, gcrow], [o_])
                S.dma('sp', dnoT[h * 128:(h + 1) * 128, c0:c0 + cw], o_[:, 0:cw], reads=[o_.res], writes=[kb.dres['dnoT']])
    S.barrier()


def merge_phase(kb, name, W, gT, oT, s5hT, dnoT, mergedT):
    c, S = kb.c, kb.S
    D, TG = c['D'], c['TG']
    k1, k2, k3 = c['MLAW'] // 128, c['S5W'] // 128, c['DNW'] // 128
    jobs = []
    for j in range(D // 128):
        jobs.append(dict(j=j, groups=[[(W['mla_w_o'], j * 128, 128, 0, k1)],
                                      [(W['s5_w_glu'], j * 128, 128, k1, k2)],
                                      [(W['s5_w_glu'], D + j * 128, 128, k1, k2)],
                                      [(W['dn_w_o'], j * 128, 128, k1 + k2, k3)]]))

    def alloc(kb, st):
        return dict(g=[kb.sb(st, name + '_g%d' % i, [128, 3, TG], F32) for i in range(2)],
                    t=[kb.sb(st, name + '_t%d' % i, [128, TG], F32) for i in range(3)],
                    o=[kb.sb(st, name + '_o%d' % i, [128, TG], BF16) for i in range(2)], n=[0])

    def ep(kb, job, ps, g, t0, ctx):
        i = ctx['n'][0]
        ctx['n'][0] += 1
        gt_ = ctx['g'][i % 2]
        t1, t2, t3 = ctx['t']
        o = ctx['o'][i % 2]
        j = job['j']
        for b in range(3):
            r0 = b * D + j * 128
            S.dma('sp', gt_[:, b, :], gT[r0:r0 + 128, t0:t0 + TG], reads=[kb.dres['gT']], writes=[gt_.res])
        S.op('act', lambda e: e.activation(t1[:], ps[2][:, 0:TG], AF.Sigmoid), reads=[ps[2].res], writes=[t1.res])
        S.op('dve', lambda e: e.tensor_tensor(t1[:], t1[:], ps[1][:, 0:TG], ALU.mult), reads=[t1.res, ps[1].res], writes=[t1.res])
        S.op('dve', lambda e: e.tensor_tensor(t1[:], t1[:], gt_[:, 1, :], ALU.mult), reads=[t1.res, gt_.res], writes=[t1.res])
        S.op('dve', lambda e: e.tensor_tensor(t2[:], ps[0][:, 0:TG], gt_[:, 0, :], ALU.mult), reads=[ps[0].res, gt_.res], writes=[t2.res])
        S.op('dve', lambda e: e.tensor_tensor(t3[:], ps[3][:, 0:TG], gt_[:, 2, :], ALU.mult), reads=[ps[3].res, gt_.res], writes=[t3.res])
        S.op('dve', lambda e: e.tensor_tensor(t1[:], t1[:], t2[:], ALU.add), reads=[t1.res, t2.res], writes=[t1.res])
        S.op('dve', lambda e: e.tensor_tensor(o[:], t1[:], t3[:], ALU.add), reads=[t1.res, t3.res], writes=[o.res])
        S.dma('sp', mergedT[j * 128:(j + 1) * 128, t0:t0 + TG], o[:], reads=[o.res], writes=[kb.dres['mergedT']])
    gemm_phase(kb, name, [(oT, 'oT'), (s5hT, 's5hT'), (dnoT, 'dnoT')], jobs, ep, SG=c.get('SG1', 4), NB=2,
               nbanks_per_job=4, ep_alloc=alloc)


def plain_gemm(kb, name, X, xname, Wd, K_, N_, out, oname, SG, NB):
    c, S = kb.c, kb.S
    TG = c['TG']
    nk = K_ // 128
    jobs = [dict(j=j, groups=[[(Wd, j * 128, 128, 0, nk)]]) for j in range(N_ // 128)]

    def alloc(kb, st):
        return dict(o=[kb.sb(st, name + '_o%d' % i, [128, TG], F32) for i in range(3)], n=[0])

    def ep(kb, job, ps, g, t0, ctx):
        i = ctx['n'][0]
        ctx['n'][0] += 1
        o = ctx['o'][i % 3]
        if i % 2 == 0:
            S.op('act', lambda e: e.copy(o[:], ps[0][:, 0:TG]), reads=[ps[0].res], writes=[o.res])
        else:
            S.op('dve', lambda e: e.tensor_copy(o[:], ps[0][:, 0:TG]), reads=[ps[0].res], writes=[o.res])
        j = job['j']
        if oname in kb.blocked:
            store_blocked(kb, out, oname, j, t0, o)
        else:
            S.dma('sp', out[j * 128:(j + 1) * 128, t0:t0 + TG], o[:], reads=[o.res], writes=[kb.dres[oname]])
    gemm_phase(kb, name, [(X, xname)], jobs, ep, SG=SG, NB=NB, nbanks_per_job=1, ep_alloc=alloc)


PER_LAYER = ['ffn1_w13', 'ffn1_w2', 'w_in', 'mla_q_norm_g', 'mla_kv_norm_g', 'mla_w_uq', 'mla_w_ukv', 'mla_w_o',
             's5_a_re', 's5_a_im', 's5_log_dt', 's5_b_re', 's5_b_im', 's5_c_re', 's5_c_im', 's5_d', 's5_w_glu',
             'dn_conv_w', 'dn_a_log', 'dn_dt_bias', 'dn_out_norm_g', 'dn_w_o', 'w_out', 'ffn2_w13', 'ffn2_w2']


RELAID = ('ffn1_w13', 'ffn1_w2', 'w_in', 'mla_w_o', 's5_w_glu', 'dn_w_o', 'w_out', 'ffn2_w13', 'ffn2_w2')


def relay_weight(W, tiles):
    K_, N_ = W.shape
    nkt = K_ // 128
    if all(m == 128 for _, m in tiles):
        return np.ascontiguousarray(W.reshape(nkt, 128, N_ // 128, 128).transpose(1, 2, 0, 3).reshape(128, nkt * N_))
    out = np.empty((128, nkt * N_), np.float32)
    W3 = W.reshape(nkt, 128, N_)
    for c0, m in tiles:
        out[:, nkt * c0:nkt * (c0 + m)] = W3[:, :, c0:c0 + m].transpose(1, 0, 2).reshape(128, nkt * m)
    return out


def weight_shapes(c):
    D, F = c['D'], c['F']
    return dict(ffn1_w13=[D, 2 * F], ffn1_w2=[F, D], w_in=[D, c['NIN']], mla_q_norm_g=[c['QL']],
                mla_kv_norm_g=[c['KVL']], mla_w_uq=[c['QL'], c['H'] * c['QK']],
                mla_w_ukv=[c['KVL'], c['H'] * (c['NOPE'] + c['VD'])], mla_w_o=[c['MLAW'], D],
                s5_a_re=[c['S5NG'], c['S5P']], s5_a_im=[c['S5NG'], c['S5P']], s5_log_dt=[c['S5NG']],
                s5_b_re=[c['S5NG'], c['S5P'], c['S5G']], s5_b_im=[c['S5NG'], c['S5P'], c['S5G']],
                s5_c_re=[c['S5NG'], c['S5G'], c['S5P']], s5_c_im=[c['S5NG'], c['S5G'], c['S5P']],
                s5_d=[c['S5NG'], c['S5G']], s5_w_glu=[c['S5W'], 2 * D],
                dn_conv_w=[c['DNCONV'], 2 * c['DNQK'] + c['DNW']], dn_a_log=[c['DNH']], dn_dt_bias=[c['DNH']],
                dn_out_norm_g=[c['DV']], dn_w_o=[c['DNW'], D], w_out=[D, D], ffn2_w13=[D, 2 * F], ffn2_w2=[F, D])


def build_program(cfg, debug=()):
    c = derive(cfg)
    kb = K(c)
    kb.dbg = debug
    S = kb.S
    D, F, LP, DEPTH = c['D'], c['F'], c['LP'], c['DEPTH']
    NW = c['TG'] // 2
    BSH = [LP // NW, 128, (D // 128) * NW]
    kb.blocked |= {'xT', 'hT', 'fT', 'outT'}
    xT = kb.dram('xT', BSH, F32, kind="ExternalInput")
    sg = kb.dram('sandwich_g', [DEPTH * 6, D], F32, kind="ExternalInput")
    Wl = []
    shp = weight_shapes(c)
    for l in range(DEPTH):
        wd = {}
        for n in PER_LAYER:
            if n in RELAID:
                K_, N_ = shp[n]
                wd[n] = kb.dram('%s_%d' % (n, l), [128, (K_ // 128) * N_], F32, kind="ExternalInput")
                kb.relaid.add(id(wd[n]))
            else:
                wd[n] = kb.dram('%s_%d' % (n, l), shp[n], F32, kind="ExternalInput")
        Wl.append(wd)
    outT = kb.dram('outT', BSH, F32, kind="ExternalOutput")

    def scratch(name, shape, dt):
        return kb.dram(name, shape, dt, kind=("ExternalOutput" if name in debug else "Internal"))
    hT = scratch('hT', BSH, F32)
    nT = scratch('nT', [D, LP], BF16)
    hidT = scratch('hidT', [F, LP], BF16)
    fT = scratch('fT', BSH, F32)
    pT = scratch('pT', [c['OFF_GATE'], LP], F32)
    gT = scratch('gT', [3 * D, LP], F32)
    scr = dict(cqnT=scratch('cqnT', [c['QL'], LP], BF16), ckvnT=scratch('ckvnT', [c['KVL'], LP], BF16),
               qnT=scratch('qnT', [c['H'] * c['NOPE'], LP], BF16), qrT=scratch('qrT', [c['H'] * c['ROPE'], LP], BF16),
               knT=scratch('knT', [c['H'] * c['NOPE'], LP], BF16), krT=scratch('krT', [c['ROPE'], LP], BF16),
               Vtok=scratch('Vtok', [LP, c['MLAW']], BF16), oT=scratch('oT', [c['MLAW'], LP], BF16),
               csT=scratch('csT', [c['ROPE'] // 2, 2, LP], F32))
    s5hT = scratch('s5hT', [c['S5W'], LP], BF16)
    bgT = scratch('bgT', [2 * c['DNH'], LP], F32)
    dnoT = scratch('dnoT', [c['DNW'], LP], BF16)
    mergedT = scratch('mergedT', [D, LP], BF16)

    rope_tables(kb, scr['csT'])
    with ExitStack() as st:
        G = [load_gain(kb, st, 'sg%d' % i, sg[i, :], D) for i in range(DEPTH * 6)]
        hsrc = (xT, 'xT', 0)
        norm_phase(kb, 'n_in', D, h=hsrc, g_pre=G[0], n_out=(nT, 'nT', 0))
        for l in range(DEPTH):
            W = Wl[l]
            g = G[6 * l:6 * l + 6]
            last = (l == DEPTH - 1)
            ffn_phase(kb, 'f1_%d' % l, nT, W['ffn1_w13'], W['ffn1_w2'], hidT, fT)
            norm_phase(kb, 'n1_%d' % l, D, h=hsrc, y=(fT, 'fT', 0), coef=0.5, g_post=g[1], g_pre=g[2],
                       n_out=(nT, 'nT', 0), h_out=(hT, 'hT', 0))
            hsrc = (hT, 'hT', 0)
            inproj_phase(kb, 'ip_%d' % l, nT, W['w_in'], pT, gT)
            mla_phase(kb, 'mla_%d' % l, l, W, pT, scr)
            s5_phase(kb, 's5_%d' % l, l, W, pT, s5hT)
            dn_phase(kb, 'dn_%d' % l, l, W, pT, bgT, dnoT)
            merge_phase(kb, 'mg_%d' % l, W, gT, scr['oT'], s5hT, dnoT, mergedT)
            plain_gemm(kb, 'wo_%d' % l, mergedT, 'mergedT', W['w_out'], D, D, fT, 'fT', SG=c.get('SG1', 4), NB=4)
            norm_phase(kb, 'n3_%d' % l, D, h=hsrc, y=(fT, 'fT', 0), coef=1.0, g_post=g[3], g_pre=g[4],
                       n_out=(nT, 'nT', 0), h_out=(hT, 'hT', 0))
            ffn_phase(kb, 'f2_%d' % l, nT, W['ffn2_w13'], W['ffn2_w2'], hidT, fT)
            if last:
                norm_phase(kb, 'n5_%d' % l, D, h=hsrc, y=(fT, 'fT', 0), coef=0.5, g_post=g[5],
                           h_out=(outT, 'outT', 0))
            else:
                norm_phase(kb, 'n5_%d' % l, D, h=hsrc, y=(fT, 'fT', 0), coef=0.5, g_post=g[5], g_pre=G[6 * (l + 1)],
                           n_out=(nT, 'nT', 0), h_out=(hT, 'hT', 0))
    S.barrier()
    S.finish()
    return kb


def make_in_maps(c, inputs, ncores):
    c = derive(c)
    D, L, LP = c['D'], c['L'], c['LP']
    x = np.asarray(inputs['x'], dtype=np.float32)
    meta = np.asarray(inputs['meta_tokens'], dtype=np.float32)
    shared = {'sandwich_g': np.ascontiguousarray(np.asarray(inputs['sandwich_g'], np.float32).reshape(-1, D))}
    for l in range(c['DEPTH']):
        for n in PER_LAYER:
            w = np.asarray(inputs[n], np.float32)[l]
            if n in RELAID:
                tiles = inproj_tiles(c) if n == 'w_in' else [(i, 128) for i in range(0, w.shape[1], 128)]
                shared['%s_%d' % (n, l)] = relay_weight(w, tiles)
            else:
                shared['%s_%d' % (n, l)] = np.ascontiguousarray(w)
    maps = []
    for b in range(ncores):
        xT = np.zeros((D, LP), np.float32)
        xT[:, :c['NMETA']] = meta.T
        xT[:, c['NMETA']:L] = x[b].T
        m = dict(shared)
        m['xT'] = to_blocked(c, xT)
        maps.append(m)
    return maps


def to_blocked(c, a):
    D, LP = a.shape
    NW = c['TG'] // 2
    return np.ascontiguousarray(a.reshape(D // 128, 128, LP // NW, NW).transpose(2, 1, 0, 3).reshape(LP // NW, 128, -1))


def from_blocked(c, b, D):
    G, P, R = b.shape
    NW = c['TG'] // 2
    return np.ascontiguousarray(b.reshape(G, 128, D // 128, NW).transpose(2, 1, 0, 3).reshape(D, G * NW))


_CACHE = {}


def kernel(**inputs):
    cfg = full_cfg()
    c = derive(cfg)
    B = np.asarray(inputs['x']).shape[0]
    if 'kb' not in _CACHE:
        _CACHE['kb'] = build_program(cfg)
    kb = _CACHE['kb']
    maps = make_in_maps(cfg, inputs, B)
    res = run_bass_kernel_spmd(kb.nc, maps, core_ids=list(range(B)))
    out = np.stack([np.ascontiguousarray(from_blocked(c, res.results[b]['outT'], c['D'])[:, c['NMETA']:c['L']].T)
                    for b in range(B)], axis=0)
    return out.astype(np.float32)
```
